# Optimizing a Trainium2 kernel written in Bass

```python
import math
import jax
import jax.numpy as jnp
from jax import lax
import numpy as np

D_MODEL = 1024
BATCH = 8
SEQ = 8192
DEPTH = 2

GRID_W = 64
CTX_LEN = 256
N_BRANCH = 3
MLA_HEADS = 8
MLA_Q_RANK = 384
MLA_KV_RANK = 256
MLA_NOPE = 64
MLA_ROPE = 32
MLA_V = 64
DIFF_HEADS = 4
DIFF_QK = 64
DIFF_V = 128
RET_HEADS = 4
RET_K = 64
RET_V = 128
RET_CHUNK = 128
BRANCH_W = 512
D_FF = 2816
N_MOD = 9
ROPE_BASE = 10000.0
Q_BLOCK = 128
EPS = 1e-6
PROJ_SIZES = (MLA_Q_RANK, MLA_KV_RANK, MLA_ROPE,
              DIFF_HEADS * 2 * DIFF_QK, DIFF_HEADS * 2 * DIFF_QK, DIFF_HEADS * DIFF_V,
              RET_HEADS * RET_K, RET_HEADS * RET_K, RET_HEADS * RET_V, RET_HEADS * RET_V,
              N_BRANCH * D_MODEL)
PROJ_DIM = sum(PROJ_SIZES)

kernel_name = 'hybrid_mla_diff_retention_trunk'


def _rmsnorm(x, gain):
    xf = x.astype(jnp.float32)
    xf = xf * lax.rsqrt(jnp.mean(xf * xf, axis=-1, keepdims=True) + EPS)
    return (xf * gain.astype(jnp.float32)).astype(x.dtype)


def _modulate(x, gain, shift, scale):
    return _rmsnorm(x, gain) * (1.0 + scale) + shift


def _swiglu(h, w1, w3, w2):
    return (jax.nn.silu(h @ w1) * (h @ w3)) @ w2


def _rope_1d(x, pos):
    d = x.shape[-1]
    inv = ROPE_BASE ** (-jnp.arange(0, d, 2, dtype=jnp.float32) / d)
    ang = pos.astype(jnp.float32)[:, None] * inv[None, :]
    cos, sin = jnp.cos(ang).astype(x.dtype), jnp.sin(ang).astype(x.dtype)
    x1, x2 = x[..., : d // 2], x[..., d // 2:]
    return jnp.concatenate([x1 * cos - x2 * sin, x1 * sin + x2 * cos], axis=-1)


def _rope_2d(x, row, col):
    h = x.shape[-1] // 2
    return jnp.concatenate([_rope_1d(x[..., :h], row), _rope_1d(x[..., h:], col)], axis=-1)


def _heads(z, n_heads):
    b, s, _ = z.shape
    return z.reshape(b, s, n_heads, -1).transpose(0, 2, 1, 3)


def _merge_heads(z):
    b, h, s, d = z.shape
    return z.transpose(0, 2, 1, 3).reshape(b, s, h * d)


def _sweep_queries(fn, *qs):
    b, h, n, _ = qs[0].shape
    nb = n // Q_BLOCK
    blocks = tuple(jnp.moveaxis(q.reshape(b, h, nb, Q_BLOCK, q.shape[-1]), 2, 0) for q in qs)
    out = lax.map(lambda a: fn(*a), blocks)
    return jnp.moveaxis(out, 0, 2).reshape(b, h, n, out.shape[-1])


def _mla_attend(q_nope, q_rope, k_nope, k_rope, v):
    s = jnp.einsum('bhqd,bhkd->bhqk', q_nope, k_nope) + jnp.einsum('bhqr,bkr->bhqk', q_rope, k_rope)
    p = jax.nn.softmax(s.astype(jnp.float32), axis=-1).astype(v.dtype)
    return jnp.einsum('bhqk,bhkv->bhqv', p, v)


def _diff_attend(q1, q2, k1, k2, v, lam):
    s1 = jnp.einsum('bhqd,bhkd->bhqk', q1, k1).astype(jnp.float32)
    s2 = jnp.einsum('bhqd,bhkd->bhqk', q2, k2).astype(jnp.float32)
    a = jax.nn.softmax(s1, axis=-1) - lam * jax.nn.softmax(s2, axis=-1)
    return jnp.einsum('bhqk,bhkv->bhqv', a.astype(v.dtype), v)


def _retention(q, k, v, log_gamma, state0):
    b, h, s, _ = q.shape
    dv = v.shape[-1]
    nc = s // RET_CHUNK
    idx = jnp.arange(RET_CHUNK, dtype=jnp.float32)
    lg = log_gamma.astype(jnp.float32)[:, None]
    rel = idx[:, None] - idx[None, :]
    intra = jnp.where(rel >= 0, jnp.exp(lg[:, :, None] * jnp.maximum(rel, 0.0)), 0.0)
    q_dec = jnp.exp(lg * (idx + 1.0))[:, :, None]
    k_dec = jnp.exp(lg * (RET_CHUNK - 1.0 - idx))[:, :, None]
    c_dec = jnp.exp(lg * RET_CHUNK)[:, :, None]

    def chunks(z):
        return jnp.moveaxis(z.reshape(b, h, nc, RET_CHUNK, z.shape[-1]), 2, 0)

    def step(state, inp):
        qi, ki, vi = inp
        a = jnp.einsum('bhqd,bhkd->bhqk', qi, ki) * intra
        o = jnp.einsum('bhqk,bhkv->bhqv', a, vi) + jnp.einsum('bhqd,bhdv->bhqv', qi * q_dec, state)
        state = state * c_dec + jnp.einsum('bhkd,bhkv->bhdv', ki * k_dec, vi)
        return state, o

    state, out = lax.scan(step, state0, (chunks(q), chunks(k), chunks(v)))
    return jnp.moveaxis(out, 0, 2).reshape(b, h, s, dv), state


def _token_mixer(hx, hc, row, col, t, w_in, mla_q_norm, mla_w_qb, mla_kv_norm, mla_w_kvb,
                 diff_lambda, diff_norm, lambda_init, ret_decay, ret_norm, w_branch, w_out, ctx_out):
    f32 = jnp.float32
    splits = np.cumsum(PROJ_SIZES)[:-1].tolist()
    px = jnp.split(hx @ w_in, splits, axis=-1)
    pc = jnp.split(hc @ w_in, splits, axis=-1)
    rope_grid = lambda z: _rope_2d(z, row, col)
    rope_seq = lambda z: _rope_1d(z, t)

    def mla_proj(p, rope):
        q = _heads(_rmsnorm(p[0], mla_q_norm) @ mla_w_qb, MLA_HEADS) * (MLA_NOPE + MLA_ROPE) ** -0.5
        kv = _heads(_rmsnorm(p[1], mla_kv_norm) @ mla_w_kvb, MLA_HEADS)
        q_nope, q_rope = q[..., :MLA_NOPE], q[..., MLA_NOPE:]
        k_nope, v = kv[..., :MLA_NOPE], kv[..., MLA_NOPE:]
        k_rope = p[2]
        if rope is not None:
            q_rope, k_rope = rope(q_rope), rope(k_rope)
        return q_nope, q_rope, k_nope, k_rope, v

    xqn, xqr, xkn, xkr, xv = mla_proj(px, rope_grid)
    cqn, cqr, ckn, ckr, cv = mla_proj(pc, None)
    kn_all = jnp.concatenate([xkn, ckn], axis=2)
    kr_all = jnp.concatenate([xkr, ckr], axis=1)
    mv_all = jnp.concatenate([xv, cv], axis=2)
    mla_x = _merge_heads(_sweep_queries(
        lambda qn, qr: _mla_attend(qn, qr, kn_all, kr_all, mv_all), xqn, xqr))

    dl = diff_lambda.astype(f32)
    lam = jnp.exp(jnp.sum(dl[0] * dl[1])) - jnp.exp(jnp.sum(dl[2] * dl[3])) + lambda_init

    def diff_proj(p, rope):
        b, s, _ = p[3].shape
        q = p[3].reshape(b, s, DIFF_HEADS, 2, DIFF_QK).transpose(3, 0, 2, 1, 4) * DIFF_QK ** -0.5
        k = p[4].reshape(b, s, DIFF_HEADS, 2, DIFF_QK).transpose(3, 0, 2, 1, 4)
        v = _heads(p[5], DIFF_HEADS)
        if rope is not None:
            q, k = rope(q), rope(k)
        return q[0], q[1], k[0], k[1], v

    def diff_post(o):
        return _merge_heads(_rmsnorm(o, diff_norm) * (1.0 - lambda_init))

    xq1, xq2, xk1, xk2, xdv = diff_proj(px, rope_grid)
    cq1, cq2, ck1, ck2, cdv = diff_proj(pc, None)
    k1_all = jnp.concatenate([xk1, ck1], axis=2)
    k2_all = jnp.concatenate([xk2, ck2], axis=2)
    dv_all = jnp.concatenate([xdv, cdv], axis=2)
    diff_x = diff_post(_sweep_queries(
        lambda a1, a2: _diff_attend(a1, a2, k1_all, k2_all, dv_all, lam), xq1, xq2))

    log_gamma = -jnp.exp(ret_decay.astype(f32))

    def ret_proj(p, rope):
        q = _heads(p[6], RET_HEADS)
        k = _heads(p[7], RET_HEADS) * RET_K ** -0.5
        v = _heads(p[8], RET_HEADS)
        if rope is not None:
            q, k = rope(q), rope(k)
        return q, k, v

    def ret_post(o, g):
        return _merge_heads(_rmsnorm(o.astype(g.dtype), ret_norm)) * jax.nn.silu(g)

    flip = lambda z: jnp.flip(z, axis=2)
    rq_c, rk_c, rv_c = ret_proj(pc, None)
    rq_x, rk_x, rv_x = ret_proj(px, rope_seq)
    zero = jnp.zeros((hx.shape[0], RET_HEADS, RET_K, RET_V), f32)
    oc_f, st_f = _retention(rq_c, rk_c, rv_c, log_gamma[0], zero)
    oc_b, st_b = _retention(flip(rq_c), flip(rk_c), flip(rv_c), log_gamma[1], zero)
    ox_f, _ = _retention(rq_x, rk_x, rv_x, log_gamma[0], st_f)
    ox_b, _ = _retention(flip(rq_x), flip(rk_x), flip(rv_x), log_gamma[1], st_b)
    ret_x = ret_post(ox_f + flip(ox_b), px[9])

    def merge(y_mla, y_diff, y_ret, gate_logits):
        b, s, _ = gate_logits.shape
        g = jax.nn.sigmoid(gate_logits).reshape(b, s, N_BRANCH, D_MODEL)
        y = (g[:, :, 0] * (y_mla @ w_branch[0]) + g[:, :, 1] * (y_diff @ w_branch[1])
             + g[:, :, 2] * (y_ret @ w_branch[2]))
        return y @ w_out

    y_x = merge(mla_x, diff_x, ret_x, px[10])
    y_c = None
    if ctx_out:
        mla_c = _merge_heads(_mla_attend(cqn, cqr, ckn, ckr, cv))
        diff_c = diff_post(_diff_attend(cq1, cq2, ck1, ck2, cdv, lam))
        ret_c = ret_post(oc_f + flip(oc_b), pc[9])
        y_c = merge(mla_c, diff_c, ret_c, pc[10])
    return y_x, y_c


def setup_inputs(seed: int = 0) -> dict:
    key = jax.random.key(seed)
    ks = jax.random.split(key, 26)
    f32 = jnp.float32
    d, nl = D_MODEL, DEPTH

    def nrm(i, shape, scale):
        return jax.random.normal(ks[i], shape, f32) * scale

    def gain(i, shape):
        return 1.0 + 0.02 * jax.random.normal(ks[i], shape, f32)

    base_decay = jnp.log(-jnp.log1p(-jnp.exp2(-5.0 - jnp.arange(RET_HEADS, dtype=f32))))
    return {
        'x': nrm(0, (BATCH, SEQ, d), 1.0),
        'c': nrm(1, (BATCH, d), 1.0),
        'ctx': nrm(2, (BATCH, CTX_LEN, d), 1.0),
        'c_ctx': nrm(3, (d,), 1.0),
        'ada_w': nrm(4, (nl, d, N_MOD * d), 0.5 * d ** -0.5),
        'ada_b': nrm(5, (nl, N_MOD * d), 0.02),
        'norm_gain': gain(6, (nl, 3, d)),
        'ffn1_w1': nrm(7, (nl, d, D_FF), d ** -0.5),
        'ffn1_w3': nrm(8, (nl, d, D_FF), d ** -0.5),
        'ffn1_w2': nrm(9, (nl, D_FF, d), D_FF ** -0.5),
        'ffn2_w1': nrm(10, (nl, d, D_FF), d ** -0.5),
        'ffn2_w3': nrm(11, (nl, d, D_FF), d ** -0.5),
        'ffn2_w2': nrm(12, (nl, D_FF, d), D_FF ** -0.5),
        'w_in': nrm(13, (nl, d, PROJ_DIM), d ** -0.5),
        'mla_q_norm': gain(14, (nl, MLA_Q_RANK)),
        'mla_w_qb': nrm(15, (nl, MLA_Q_RANK, MLA_HEADS * (MLA_NOPE + MLA_ROPE)), MLA_Q_RANK ** -0.5),
        'mla_kv_norm': gain(16, (nl, MLA_KV_RANK)),
        'mla_w_kvb': nrm(17, (nl, MLA_KV_RANK, MLA_HEADS * (MLA_NOPE + MLA_V)), MLA_KV_RANK ** -0.5),
        'diff_lambda': nrm(18, (nl, 4, DIFF_QK), 0.1),
        'diff_norm': gain(19, (nl, DIFF_V)),
        'ret_decay': base_decay + 0.05 * jax.random.normal(ks[20], (nl, 2, RET_HEADS), f32),
        'ret_norm': gain(21, (nl, RET_V)),
        'w_branch': nrm(22, (nl, N_BRANCH, BRANCH_W, d), BRANCH_W ** -0.5),
        'w_out': nrm(23, (nl, d, d), d ** -0.5),
        'final_norm': gain(24, (d,)),
    }


def reference(x, c, ctx, c_ctx, ada_w, ada_b, norm_gain, ffn1_w1, ffn1_w3, ffn1_w2,
              ffn2_w1, ffn2_w3, ffn2_w2, w_in, mla_q_norm, mla_w_qb, mla_kv_norm, mla_w_kvb,
              diff_lambda, diff_norm, ret_decay, ret_norm, w_branch, w_out, final_norm):
    n = x.shape[1]
    rows = n // GRID_W
    t = jnp.arange(n)
    row = jnp.repeat(jnp.arange(rows), GRID_W)
    col = t - row * GRID_W
    sc_x = jax.nn.silu(c)
    sc_c = jax.nn.silu(c_ctx)
    xc = ctx
    for l in range(DEPTH):
        lambda_init = 0.8 - 0.6 * math.exp(-0.3 * l)
        ctx_out = l < DEPTH - 1
        mx = jnp.split((sc_x @ ada_w[l] + ada_b[l])[:, None, :], N_MOD, axis=-1)
        mc = jnp.split(sc_c @ ada_w[l] + ada_b[l], N_MOD, axis=-1)
        x = x + mx[2] * 0.5 * _swiglu(_modulate(x, norm_gain[l, 0], mx[0], mx[1]),
                                      ffn1_w1[l], ffn1_w3[l], ffn1_w2[l])
        xc = xc + mc[2] * 0.5 * _swiglu(_modulate(xc, norm_gain[l, 0], mc[0], mc[1]),
                                        ffn1_w1[l], ffn1_w3[l], ffn1_w2[l])
        y_x, y_c = _token_mixer(
            _modulate(x, norm_gain[l, 1], mx[3], mx[4]), _modulate(xc, norm_gain[l, 1], mc[3], mc[4]),
            row, col, t, w_in[l], mla_q_norm[l], mla_w_qb[l], mla_kv_norm[l], mla_w_kvb[l],
            diff_lambda[l], diff_norm[l], lambda_init, ret_decay[l], ret_norm[l], w_branch[l], w_out[l],
            ctx_out)
        x = x + mx[5] * y_x
        x = x + mx[8] * 0.5 * _swiglu(_modulate(x, norm_gain[l, 2], mx[6], mx[7]),
                                      ffn2_w1[l], ffn2_w3[l], ffn2_w2[l])
        if ctx_out:
            xc = xc + mc[5] * y_c
            xc = xc + mc[8] * 0.5 * _swiglu(_modulate(xc, norm_gain[l, 2], mc[6], mc[7]),
                                            ffn2_w1[l], ffn2_w3[l], ffn2_w2[l])
    return _rmsnorm(x, final_norm)
```

```python
import math
import contextlib
import numpy as np
import ml_dtypes
import concourse.bass as bass
import concourse.mybir as mybir
from concourse.bass_utils import run_bass_kernel_spmd

F32 = mybir.dt.float32
BF16 = mybir.dt.bfloat16
AF = mybir.ActivationFunctionType
ALU = mybir.AluOpType

D = 1024
CTX = 256
DFF = 2816
NF = DFF // 128
PROJ = 6816
EPS = 1e-6
O_QLAT, O_KVLAT, O_KROPE, O_DQ, O_DK, O_DV = 0, 384, 640, 672, 1184, 1696
O_RQ, O_RK, O_RV, O_RG, O_GATE = 2208, 2464, 2720, 3232, 3744
DEPTH = 2


class Buf:
    __slots__ = ("writers", "readers")

    def __init__(self):
        self.writers = {}
        self.readers = {}


class Op:
    __slots__ = ("eng", "fn", "deps", "key", "is_dma", "signal", "val")


class Prog:
    def __init__(self, nc):
        self.nc = nc
        self.ops = []
        self.last = {}

    def add(self, eng, fn, reads=(), writes=(), pwrites=(), dma=None, extra=None, serial=True, track=True):
        op = Op()
        op.eng = eng
        op.fn = fn
        op.is_dma = dma is not None
        op.key = ("d", dma) if dma is not None else ("e", eng)
        op.signal = False
        op.val = 0
        idx = len(self.ops)
        deps = {}

        def need(d):
            for k, i in d.items():
                if deps.get(k, -1) < i:
                    deps[k] = i

        for b in reads:
            need(b.writers)
        for b in writes:
            need(b.writers)
            need(b.readers)
        for b in pwrites:
            need(b.writers)
            need(b.readers)
        if extra:
            need(extra)
        if op.is_dma and serial and op.key in self.last:
            need({op.key: self.last[op.key]})
        if eng == "pe" and not op.is_dma:
            deps.pop(("e", "pe"), None)
        op.deps = deps
        for b in reads:
            if b.readers.get(op.key, -1) < idx:
                b.readers[op.key] = idx
        for b in writes:
            b.writers = {op.key: idx}
            b.readers = {}
        for b in pwrites:
            if b.readers:
                b.writers = {op.key: idx}
                b.readers = {}
            else:
                b.writers[op.key] = idx
        self.ops.append(op)
        if track:
            self.last[op.key] = idx
        return idx

    def barrier(self):
        snap = dict(self.last)
        for eng in ("pe", "act", "dve", "pool", "sp"):
            self.add(eng, lambda e: None, extra=snap, track=False)

    def emit(self):
        nc = self.nc
        ops = self.ops
        for op in ops:
            for k, i in op.deps.items():
                ops[i].signal = True
        cnt = {}
        for op in ops:
            if op.signal:
                cnt[op.key] = cnt.get(op.key, 0) + 1
                op.val = cnt[op.key] * (16 if op.is_dma else 1)
        keys = sorted(cnt.keys(), key=str)
        with contextlib.ExitStack() as st:
            sems = {}
            for k in keys:
                sems[k] = st.enter_context(nc.semaphore("s_" + str(k[1])))
            block = st.enter_context(nc.Block())

            def run(engname, engobj):
                known = {}
                for op in ops:
                    if op.eng != engname:
                        continue
                    for k, i in op.deps.items():
                        v = ops[i].val
                        if known.get(k, 0) < v:
                            engobj.wait_ge(sems[k], v)
                            known[k] = v
                    ins = op.fn(engobj)
                    if op.signal:
                        assert ins is not None
                        ins.then_inc(sems[op.key], 16 if op.is_dma else 1)

            @block.tensor
            def _(e):
                run("pe", e)

            @block.scalar
            def _(e):
                run("act", e)

            @block.vector
            def _(e):
                run("dve", e)

            @block.gpsimd
            def _(e):
                run("pool", e)

            @block.sync
            def _(e):
                run("sp", e)
        return len(ops), len(keys)


class TL:
    __slots__ = ("t", "b")

    def __init__(self, t, b=None):
        self.t = t
        self.b = b if b is not None else Buf()


def _dtsize(dt):
    return 2 if dt == BF16 else 4


class SBAlloc:
    def __init__(self, nc):
        self.nc = nc
        self.base = (nc.sbuf_base + 63) // 64 * 64
        self.top = nc.sbuf_top
        self.cur = self.base
        self.n = 0

    def tile(self, shape, dt, at=None, buf=None):
        size = int(np.prod(shape[1:])) * _dtsize(dt)
        size = (size + 63) // 64 * 64
        off = self.cur if at is None else at
        self.n += 1
        t = self.nc.alloc_sbuf_tensor_at("sb%d" % self.n, list(shape), dt, offset=off)
        if at is None:
            self.cur += size
            assert self.cur <= self.top, ("SBUF overflow", self.cur - self.base, self.top - self.base)
        tl = TL(t, buf)
        return tl, off


def host_consts(S):
    t = np.arange(S)
    row = (t // 64).astype(np.float32)
    col = (t % 64).astype(np.float32)
    tt = t.astype(np.float32)

    def tab(bs, posf):
        C = np.zeros((128, S), np.float32)
        Sn = np.zeros((128, S), np.float32)
        h = bs // 2
        for r in range(128):
            i = r % bs
            f = i % h
            inv = np.float32(10000.0) ** (-(np.float32(2 * f) / np.float32(bs)))
            ang = (posf(r) * np.float32(inv)).astype(np.float32)
            C[r] = np.cos(ang.astype(np.float64)).astype(np.float32)
            Sn[r] = np.sin(ang.astype(np.float64)).astype(np.float32)
        return C, Sn

    cm, sm = tab(16, lambda r: row if (r % 32) < 16 else col)
    cd, sd = tab(32, lambda r: row if (r % 64) < 32 else col)
    cr, sr = tab(64, lambda r: tt)

    def perm(bs):
        h = bs // 2
        Pm = np.zeros((128, 128), np.float32)
        for i in range(128):
            if i % bs < h:
                Pm[i, i + h] = -1.0
            else:
                Pm[i, i - h] = 1.0
        return np.ascontiguousarray(Pm.T)

    k = np.arange(128)[:, None].astype(np.float32)
    q = np.arange(128)[None, :].astype(np.float32)
    relF = np.maximum(q - k, 0.0)
    mskF = (q >= k).astype(np.float32)
    relB = np.maximum(k - q, 0.0)
    mskB = (k >= q).astype(np.float32)
    ret4 = np.stack([relF, mskF, relB, mskB]).astype(np.float32)
    qd = np.stack([np.broadcast_to(q + 1.0, (128, 128)), np.broadcast_to(128.0 - q, (128, 128))]).astype(np.float32)
    kd = np.stack([127.0 - k[:, 0], k[:, 0]], axis=1).astype(np.float32)
    return {
        "k_cm": cm, "k_sm": sm, "k_cd": cd, "k_sd": sd, "k_cr": cr, "k_sr": sr,
        "k_p16": perm(16), "k_p32": perm(32), "k_p64": perm(64),
        "k_ident": np.eye(128, dtype=np.float32),
        "k_ret4": ret4, "k_qd": np.ascontiguousarray(qd), "k_kd": np.ascontiguousarray(kd),
    }


def build(S=8192, DEBUG=()):
    nc = bass.Bass("TRN2", target_bir_lowering=False)
    NXT = S // 128
    T = S + CTX
    NTT = NXT + 2
    NKT = NTT
    P = Prog(nc)
    sba = SBAlloc(nc)

    def dram_in(name, shape, dt=F32):
        return nc.dram_tensor(name, list(shape), dt, kind="ExternalInput").ap()

    def dram_scr(name, shape, dt):
        kind = "ExternalOutput" if name in DEBUG else "Internal"
        return TL(nc.dram_tensor(name, list(shape), dt, kind=kind).ap())

    x_in = dram_in("x", [S, D])
    c_in = dram_in("c", [D])
    ctx_in = dram_in("ctx", [CTX, D])
    cctx_in = dram_in("c_ctx", [D])
    ada_w = dram_in("ada_w", [DEPTH, D, 9 * D])
    ada_b = dram_in("ada_b", [DEPTH, 9 * D])
    norm_gain = dram_in("norm_gain", [DEPTH, 3, D])
    ffn_w = {}
    for nm in ("ffn1_w1", "ffn1_w3", "ffn2_w1", "ffn2_w3"):
        ffn_w[nm] = dram_in(nm, [DEPTH, D, DFF])
    for nm in ("ffn1_w2", "ffn2_w2"):
        ffn_w[nm] = dram_in(nm, [DEPTH, DFF, D])
    w_in = dram_in("w_in", [DEPTH, D, PROJ])
    mla_q_norm = dram_in("mla_q_norm", [DEPTH, 384])
    mla_w_qb = dram_in("mla_w_qb", [DEPTH, 384, 768])
    mla_kv_norm = dram_in("mla_kv_norm", [DEPTH, 256])
    mla_w_kvb = dram_in("mla_w_kvb", [DEPTH, 256, 1024])
    diff_lambda = dram_in("diff_lambda", [DEPTH, 4, 64])
    diff_norm = dram_in("diff_norm", [DEPTH, 128])
    ret_decay = dram_in("ret_decay", [DEPTH, 2, 4])
    ret_norm = dram_in("ret_norm", [DEPTH, 128])
    w_branch = dram_in("w_branch", [DEPTH, 3, 512, D])
    w_out = dram_in("w_out", [DEPTH, D, D])
    final_norm = dram_in("final_norm", [D])
    k_tab = {n: dram_in(n, [128, S]) for n in ("k_cm", "k_sm", "k_cd", "k_sd", "k_cr", "k_sr")}
    k_perm = {n: dram_in(n, [128, 128]) for n in ("k_p16", "k_p32", "k_p64")}
    k_ident = dram_in("k_ident", [128, 128])
    k_ret4 = dram_in("k_ret4", [4, 128, 128])
    k_qd = dram_in("k_qd", [2, 128, 128])
    k_kd = dram_in("k_kd", [128, 2])
    out = TL(nc.dram_tensor("out", [S, D], F32, kind="ExternalOutput").ap())

    modv = dram_scr("modv", [DEPTH, 2, 9 * D], F32)
    XS = {}
    for l in range(DEPTH):
        for st in (1, 2, 3):
            if l == DEPTH - 1 and st == 3:
                continue
            XS[(l, st)] = dram_scr("xs%d_%d" % (l, st), [T, D], F32)
    SC = {}
    for l in range(DEPTH):
        SC[l] = dict(
            QM=dram_scr("QM%d" % l, [8, 96, T], BF16),
            KMn=dram_scr("KMn%d" % l, [8, 64, T], BF16),
            KMr=dram_scr("KMr%d" % l, [32, T], BF16),
            VM=dram_scr("VM%d" % l, [T, 512], BF16),
            QD=dram_scr("QD%d" % l, [512, T], BF16),
            KD=dram_scr("KD%d" % l, [512, T], BF16),
            VD=dram_scr("VD%d" % l, [T, 512], BF16),
            RQ=dram_scr("RQ%d" % l, [256, T], BF16),
            RK=dram_scr("RK%d" % l, [256, T], BF16),
            RKt=dram_scr("RKt%d" % l, [T, 256], BF16),
            RV=dram_scr("RV%d" % l, [T, 512], BF16),
            RG=dram_scr("RG%d" % l, [512, T], BF16),
            GATE=dram_scr("GATE%d" % l, [3072, T], BF16),
            YM=dram_scr("YM%d" % l, [512, T], BF16),
            YD=dram_scr("YD%d" % l, [512, T], BF16),
            YR=dram_scr("YR%d" % l, [512, T], BF16),
            OF=dram_scr("OF%d" % l, [128, NTT, 512], F32),
        )

    PS = nc.alloc_psum_tensor("ps", [128, 4096], F32)
    bankbufs = [Buf() for _ in range(8)]

    class Bank:
        def __init__(self, j):
            self.j = j
            self.b = bankbufs[j]
            self.t = PS
            self.o = j * 512

        def ap(self, p0=0, p1=128, c0=0, c1=512):
            return self.t[p0:p1, self.o + c0:self.o + c1]

    BK = [Bank(j) for j in range(8)]

    def pair_ap(i, p0=0, p1=128, c0=0, c1=1024):
        return PS[p0:p1, i * 1024 + c0:i * 1024 + c1]

    def triple_ap(g, c0=0, c1=1536):
        return PS[:, g * 1536 + c0:g * 1536 + c1]

    GROUP_KEYS = ("wl", "tab", "m0")

    def dma(q, out_ap, in_ap, key, reads=(), writes=(), pwrites=(), **kw):
        grp_ = key in GROUP_KEYS or key[:2] in ("ak", "av", "my", "mg")
        P.add(q, lambda e: e.dma_start(out=out_ap, in_=in_ap, **kw), reads, writes, pwrites, dma=key, serial=not grp_)

    def mm(out_ap, lhsT, rhs, start, stop, reads, bank):
        if start:
            P.add("pe", lambda e: e.matmul(out_ap, lhsT=lhsT, rhs=rhs, start=start, stop=stop), reads, writes=[bank])
        else:
            P.add("pe", lambda e: e.matmul(out_ap, lhsT=lhsT, rhs=rhs, start=start, stop=stop), reads, pwrites=[bank])

    def mmp(out_ap, lhsT, rhs, start, stop, reads, bank):
        P.add("pe", lambda e: e.matmul(out_ap, lhsT=lhsT, rhs=rhs, start=start, stop=stop), reads, pwrites=[bank])

    def act(out_ap, in_ap, func, reads, writes=(), pwrites=(), **kw):
        P.add("act", lambda e: e.activation(out=out_ap, in_=in_ap, func=func, **kw), reads, writes, pwrites)

    def tt(eng, out_ap, a, b, op, reads, writes=(), pwrites=()):
        P.add(eng, lambda e: e.tensor_tensor(out=out_ap, in0=a, in1=b, op=op), reads, writes, pwrites)

    def ts(eng, out_ap, a, s1, s2, op0, op1, reads, writes=(), pwrites=()):
        if s2 is None:
            P.add(eng, lambda e: e.tensor_scalar(out=out_ap, in0=a, scalar1=s1, scalar2=None, op0=op0), reads, writes, pwrites)
        else:
            P.add(eng, lambda e: e.tensor_scalar(out=out_ap, in0=a, scalar1=s1, scalar2=s2, op0=op0, op1=op1), reads, writes, pwrites)

    def stt(eng, out_ap, a, s, b, op0, op1, reads, writes=(), pwrites=()):
        P.add(eng, lambda e: e.scalar_tensor_tensor(out=out_ap, in0=a, scalar=s, in1=b, op0=op0, op1=op1), reads, writes, pwrites)

    def cp(eng, out_ap, in_ap, reads, writes=(), pwrites=()):
        if eng == "act":
            P.add("act", lambda e: e.activation(out=out_ap, in_=in_ap, func=AF.Copy), reads, writes, pwrites)
        else:
            P.add(eng, lambda e: e.tensor_copy(out=out_ap, in_=in_ap), reads, writes, pwrites)

    def recip(out_ap, in_ap, reads, writes=(), pwrites=()):
        P.add("dve", lambda e: e.reciprocal(out=out_ap, in_=in_ap), reads, writes, pwrites)

    def colvec(v_ap):
        return v_ap.rearrange("(k p) -> p k", p=128)

    ident, _ = sba.tile([128, 128], F32)
    identb, _ = sba.tile([128, 128], BF16)
    onesb, _ = sba.tile([128, 128], BF16)
    onesf, _ = sba.tile([128, 128], F32)
    epsc, _ = sba.tile([128, 1], F32)
    dma("sp", ident.t[:], k_ident, "c0", writes=[ident.b])
    cp("dve", identb.t[:], ident.t[:], [ident.b], [identb.b])
    P.add("dve", lambda e: e.memset(onesb.t[:], 1.0), writes=[onesb.b])
    P.add("dve", lambda e: e.memset(onesf.t[:], 1.0), writes=[onesf.b])
    P.add("dve", lambda e: e.memset(epsc.t[:], EPS), writes=[epsc.b])
    pass_base = sba.cur

    def new_pass():
        P.barrier()
        sba.cur = pass_base

    def tok_src(stage, t):
        if stage is None:
            if t < NXT:
                return x_in[t * 128:(t + 1) * 128, :], None
            return ctx_in[(t - NXT) * 128:(t - NXT + 1) * 128, :], None
        tl = XS[stage]
        return tl.t[t * 128:(t + 1) * 128, :], tl

    x_blocks = [("x", list(range(4 * b, 4 * b + 4))) for b in range(S // 512)]
    c_block = ("c", [NXT, NXT + 1])

    def col0(t):
        return t * 128

    def pass_mod():
        cT, _ = sba.tile([128, 8, 2], F32)
        dma("sp", cT.t[:, :, 0], colvec(c_in), "m0", pwrites=[cT.b], allow_slow_non_contiguous=True)
        dma("sp", cT.t[:, :, 1], colvec(cctx_in), "m0", pwrites=[cT.b], allow_slow_non_contiguous=True)
        sT, _ = sba.tile([128, 8, 2], F32)
        act(sT.t[:], cT.t[:], AF.Silu, [cT.b], [sT.b])
        wch = [sba.tile([128, 8, 512], F32)[0] for _ in range(2)]
        bia = [sba.tile([2, 512], F32)[0] for _ in range(2)]
        res = [sba.tile([2, 512], F32)[0] for _ in range(2)]
        i = 0
        for l in range(DEPTH):
            for cb in range(18):
                s = i % 2
                dma("sp", wch[s].t[:], ada_w[l][:, cb * 512:(cb + 1) * 512].rearrange("(k p) n -> p k n", p=128),
                    "mw%d" % s, writes=[wch[s].b])
                dma("sp", bia[s].t[:], ada_b[l, cb * 512:(cb + 1) * 512].partition_broadcast(2), "mb%d" % s, writes=[bia[s].b])
                bk = BK[s]
                for k in range(8):
                    mm(bk.ap(0, 2), sT.t[:, k, :], wch[s].t[:, k, :], k == 0, k == 7, [sT.b, wch[s].b], bk.b)
                tt("dve", res[s].t[:], bk.ap(0, 2), bia[s].t[:], ALU.add, [bk.b, bia[s].b], [res[s].b])
                dma("pool", modv.t[l, :, cb * 512:(cb + 1) * 512], res[s].t[:], "ms%d" % s, reads=[res[s].b], pwrites=[modv.b])
                i += 1

    def load_modcols(l, n):
        gcol, _ = sba.tile([128, 8], F32)
        dma("sp", gcol.t[:], colvec(norm_gain[l, n]), "mc", writes=[gcol.b], allow_slow_non_contiguous=True)
        res = []
        for s in range(2):
            sc, _ = sba.tile([128, 8], F32)
            sh, _ = sba.tile([128, 8], F32)
            A, _ = sba.tile([128, 8], F32)
            dma("sp", sc.t[:], colvec(modv.t[l, s, (3 * n + 1) * D:(3 * n + 2) * D]), "mc", reads=[modv.b], writes=[sc.b],
                allow_slow_non_contiguous=True)
            dma("sp", sh.t[:], colvec(modv.t[l, s, (3 * n) * D:(3 * n + 1) * D]), "mc", reads=[modv.b], writes=[sh.b],
                allow_slow_non_contiguous=True)
            stt("dve", A.t[:], sc.t[:], 1.0, gcol.t[:], ALU.add, ALU.mult, [sc.b, gcol.b], [A.b])
            res.append((A, sh))
        return res

    class Prep:
        def __init__(self, cols, trpair):
            self.cols = cols
            self.trpair = trpair
            self.xin = []
            self._offs = []
            for _ in range(4):
                tl, off = sba.tile([128, D], F32)
                self.xin.append(tl)
                self._offs.append(off)
            self.xn = []
            self._xnoffs = []
            for _ in range(2):
                tl, off = sba.tile([128, D], F32)
                self.xn.append(tl)
                self._xnoffs.append(off)
            self.hT, _ = sba.tile([128, 8, 512], BF16)
            self.ss, _ = sba.tile([128, 4], F32)
            self.rs, _ = sba.tile([128, 4], F32)

        def load(self, stage, blk, i):
            ap, tl = tok_src(stage, blk[1][i])
            dma("sp", self.xin[i].t[:], ap, "xin%d" % i, reads=[tl.b] if tl else [], writes=[self.xin[i].b])

        def sumsq(self, blk, i):
            act(self.xn[i % 2].t[:], self.xin[i].t[:], AF.Square, [self.xin[i].b], writes=[self.xn[i % 2].b],
                pwrites=[self.ss.b], accum_out=self.ss.t[:, i:i + 1])

        def rstd(self, blk):
            nt = len(blk[1])
            act(self.rs.t[:, 0:nt], self.ss.t[:, 0:nt], AF.Sqrt, [self.ss.b, epsc.b], writes=[self.rs.b], scale=1.0 / D, bias=epsc.t[:])
            recip(self.rs.t[:, 0:nt], self.rs.t[:, 0:nt], [self.rs.b], writes=[self.rs.b])

        def transpose(self, blk, i):
            s = 0 if blk[0] == "x" else 1
            A, sh = self.cols[s]
            xn = self.xn[i % 2]
            ts("dve", xn.t[:], self.xin[i].t[:], self.rs.t[:, i:i + 1], None, ALU.mult, None, [self.xin[i].b, self.rs.b], [xn.b])
            pr = self.trpair
            bb = [BK[2 * pr].b, BK[2 * pr + 1].b]
            for k in range(8):
                P.add("pe", lambda e, k=k: e.transpose(out=pair_ap(pr, 0, 128, k * 128, (k + 1) * 128), in_=xn.t[:, k * 128:(k + 1) * 128],
                                                     identity=ident.t[:]),
                      [xn.b, ident.b], **({"writes": [bb[0]]} if k == 0 else ({"writes": [bb[1]]} if k == 4 else {"pwrites": [bb[k // 4]]})))
            for k in range(8):
                o = self.hT.t[:, k, i * 128:(i + 1) * 128]
                src = pair_ap(pr, 0, 128, k * 128, (k + 1) * 128)
                if k % 2 == 0:
                    act(o, src, AF.Identity, [bb[k // 4], A.b, sh.b], pwrites=[self.hT.b], scale=A.t[:, k:k + 1], bias=sh.t[:, k:k + 1])
                else:
                    ts("dve", o, src, A.t[:, k:k + 1], sh.t[:, k:k + 1], ALU.mult, ALU.add, [bb[k // 4], A.b, sh.b], pwrites=[self.hT.b])

        def full(self, stage, blk):
            for i in range(len(blk[1])):
                self.load(stage, blk, i)
            for i in range(len(blk[1])):
                self.sumsq(blk, i)
            self.rstd(blk)
            for i in range(len(blk[1])):
                self.transpose(blk, i)

    def load_w_cast(dst, k, src_rows_ap, key, ncols):
        c = 0
        while c < ncols:
            n = min(1024, ncols - c)
            dma("pool", dst.t[:, k, c:c + n], src_rows_ap[:, c:c + n], key, pwrites=[dst.b])
            c += n

    def residual_store(stage_src, stage_dst, blk, i, bks, gate_bc, xres, ytmp, final=None):
        t = blk[1][i]
        xr = xres[i % 2]
        ap, tl = tok_src(stage_src, t)
        dma("sp", xr.t[:], ap, "xres%d" % (i % 2), reads=[tl.b] if tl else [], writes=[xr.b])
        for j in range(2):
            yt = ytmp[j]
            tt("dve", yt.t[:], bks[j].ap(), gate_bc.t[:, j * 512:(j + 1) * 512], ALU.mult, [bks[j].b, gate_bc.b], [yt.b])
            tt("pool", xr.t[:, j * 512:(j + 1) * 512], xr.t[:, j * 512:(j + 1) * 512], yt.t[:], ALU.add, [yt.b], pwrites=[xr.b])
        if final is None:
            dst = XS[stage_dst]
            dma("pool", dst.t[t * 128:(t + 1) * 128, :], xr.t[:], "xst%d" % (i % 2), reads=[xr.b], pwrites=[dst.b])
        else:
            fn_bc, fss, frs, fjunk = final
            act(fjunk.t[:], xr.t[:], AF.Square, [xr.b], writes=[fjunk.b], pwrites=[fss.b], accum_out=fss.t[:, 0:1])
            act(frs.t[:], fss.t[:], AF.Sqrt, [fss.b, epsc.b], writes=[frs.b], scale=1.0 / D, bias=epsc.t[:])
            recip(frs.t[:], frs.t[:], [frs.b], writes=[frs.b])
            stt("dve", xr.t[:], xr.t[:], frs.t[:, 0:1], fn_bc.t[:], ALU.mult, ALU.mult, [frs.b, fn_bc.b], writes=[xr.b])
            dma("pool", out.t[t * 128:(t + 1) * 128, :], xr.t[:], "xst%d" % (i % 2), reads=[xr.b], pwrites=[out.b])

    def load_gate_bc(gate_bc, l, n, s, half=False):
        src = modv.t[l, s, (3 * n + 2) * D:(3 * n + 3) * D].partition_broadcast(128)
        dma("sp", gate_bc.t[:], src, "gbc", reads=[modv.b], writes=[gate_bc.b])
        if half:
            P.add("pool", lambda e: e.tensor_scalar(out=gate_bc.t[:], in0=gate_bc.t[:], scalar1=0.5, scalar2=None, op0=ALU.mult),
                  [], writes=[gate_bc.b])

    def pass_ffn(l, which, stage_src, stage_dst, with_ctx, final):
        new_pass()
        n = 0 if which == 1 else 2
        w1b, _ = sba.tile([128, 8, DFF], BF16)
        w3b = TL(sba.tile([128, 8, DFF], BF16)[0].t, w1b.b)
        w2b = TL(sba.tile([128, NF, D], BF16)[0].t, w1b.b)
        W1 = ffn_w["ffn%d_w1" % which][l]
        W3 = ffn_w["ffn%d_w3" % which][l]
        W2 = ffn_w["ffn%d_w2" % which][l]
        for k in range(8):
            load_w_cast(w1b, k, W1[k * 128:(k + 1) * 128, :], "wl", DFF)
            load_w_cast(w3b, k, W3[k * 128:(k + 1) * 128, :], "wl", DFF)
        for f in range(NF):
            load_w_cast(w2b, f, W2[f * 128:(f + 1) * 128, :], "wl", D)
        cols = load_modcols(l, n)
        prep = Prep(cols, 2)
        g, _ = sba.tile([128, NF, 512], BF16)
        gate_bc, _ = sba.tile([128, D], F32)
        xres = [sba.tile([128, D], F32)[0] for _ in range(2)]
        if final:
            y0 = sba.tile([128, 512], F32)[0]
            ytmp = [y0, y0]
            s0 = sba.tile([128, 512], BF16)[0]
            ssb = [s0, s0]
        else:
            ytmp = [sba.tile([128, 512], F32)[0] for _ in range(2)]
            ssb = [sba.tile([128, 512], BF16)[0] for _ in range(2)]
        fin = None
        if final:
            fn_bc, _ = sba.tile([128, D], F32)
            dma("sp", fn_bc.t[:], final_norm.partition_broadcast(128), "gbc", writes=[fn_bc.b])
            fss, _ = sba.tile([128, 1], F32)
            frs, _ = sba.tile([128, 1], F32)
            fjunk = TL(nc.alloc_sbuf_tensor_at("fjunk", [128, D], BF16, offset=prep._xnoffs[1]), prep.xn[1].b)
            fin = (fn_bc, fss, frs, fjunk)
        blocks = ([c_block] if with_ctx else []) + x_blocks
        prep.full(stage_src, blocks[0])
        cur_stream = None
        for bi, blk in enumerate(blocks):
            N = len(blk[1]) * 128
            nxt = blocks[bi + 1] if bi + 1 < len(blocks) else None
            if blk[0] != cur_stream:
                cur_stream = blk[0]
                load_gate_bc(gate_bc, l, n, 0 if cur_stream == "x" else 1, half=True)
            for f in range(NF):
                b1 = BK[f % 2]
                b3 = BK[2 + f % 2]
                for k in range(8):
                    mm(b1.ap(c1=N), w1b.t[:, k, f * 128:(f + 1) * 128], prep.hT.t[:, k, 0:N], k == 0, k == 7, [w1b.b, prep.hT.b], b1.b)
                for k in range(8):
                    mm(b3.ap(c1=N), w3b.t[:, k, f * 128:(f + 1) * 128], prep.hT.t[:, k, 0:N], k == 0, k == 7, [w1b.b, prep.hT.b], b3.b)
                s = ssb[f % 2]
                act(s.t[:, 0:N], b1.ap(c1=N), AF.Silu, [b1.b], [s.b])
                tt("dve", g.t[:, f, 0:N], s.t[:, 0:N], b3.ap(c1=N), ALU.mult, [s.b, b3.b], pwrites=[g.b])
                if nxt is not None:
                    nn = len(nxt[1])
                    if f < nn:
                        prep.load(stage_src, nxt, f)
                    if 4 <= f < 4 + nn:
                        prep.sumsq(nxt, f - 4)
                    if f == 9:
                        prep.rstd(nxt)
            if nxt is not None:
                for i in range(len(nxt[1])):
                    prep.transpose(nxt, i)
            for i in range(len(blk[1])):
                bks = [BK[6], BK[7]]
                for j in range(2):
                    for f in range(NF):
                        mm(bks[j].ap(), g.t[:, f, i * 128:(i + 1) * 128], w2b.t[:, f, j * 512:(j + 1) * 512], f == 0, f == NF - 1,
                           [g.b, w1b.b], bks[j].b)
                residual_store(stage_src, stage_dst, blk, i, bks, gate_bc, xres, ytmp, fin if blk[0] == "x" else None)

    def pass_proj(l, stage_src, ctx_out):
        new_pass()
        sc = SC[l]
        winb, _ = sba.tile([128, 8, PROJ], BF16)
        for k in range(8):
            load_w_cast(winb, k, w_in[l][k * 128:(k + 1) * 128, :], "wl", PROJ)
        wqb = TL(sba.tile([128, 3, 768], BF16)[0].t, winb.b)
        wkvb = TL(sba.tile([128, 2, 1024], BF16)[0].t, winb.b)
        perms = {}
        for nm in ("k_p16", "k_p32", "k_p64"):
            pt = TL(sba.tile([128, 128], BF16)[0].t, winb.b)
            dma("pool", pt.t[:], k_perm[nm], "wl", pwrites=[winb.b])
            perms[nm] = pt
        cols = load_modcols(l, 1)
        qg, _ = sba.tile([128, 3], F32)
        kg, _ = sba.tile([128, 2], F32)
        dma("sp", qg.t[:], mla_q_norm[l].rearrange("(k p) -> p k", p=128), "mc", writes=[qg.b], allow_slow_non_contiguous=True)
        dma("sp", kg.t[:], mla_kv_norm[l].rearrange("(k p) -> p k", p=128), "mc", writes=[kg.b], allow_slow_non_contiguous=True)
        prep = Prep(cols, 2)
        stq = TL(nc.alloc_sbuf_tensor_at("stq%d" % l, [128, 3, 768], F32, offset=prep._offs[0]))
        stkv = TL(nc.alloc_sbuf_tensor_at("stkv%d" % l, [128, 2, 1024], F32, offset=prep._xnoffs[0]))
        qbufs = [prep.xin[0].b, prep.xin[1].b, prep.xin[2].b]
        kvbufs = [prep.xn[0].b, prep.xn[1].b]
        for k in range(3):
            dma("sp", stq.t[:, k, :], mla_w_qb[l][k * 128:(k + 1) * 128, :], "xin0", pwrites=qbufs)
        for k in range(2):
            dma("sp", stkv.t[:, k, :], mla_w_kvb[l][k * 128:(k + 1) * 128, :], "xin1", pwrites=kvbufs)
        qsc = (64 + 32) ** -0.5
        for k in range(3):
            ts("dve", wqb.t[:, k, :], stq.t[:, k, :], qg.t[:, k:k + 1], qsc, ALU.mult, ALU.mult, qbufs + [qg.b], pwrites=[winb.b])
        for k in range(2):
            ts("dve", wkvb.t[:, k, :], stkv.t[:, k, :], kg.t[:, k:k + 1], None, ALU.mult, None, kvbufs + [kg.b], pwrites=[winb.b])
        for (o, w) in ((O_DQ, 512), (O_RK, 256)):
            P.add("pool", lambda e, o=o, w=w: e.tensor_scalar(out=winb.t[:, :, o:o + w], in0=winb.t[:, :, o:o + w], scalar1=0.125, scalar2=None,
                                                          op0=ALU.mult), [], pwrites=[winb.b])
        W = winb.b
        latq, _ = sba.tile([128, 3, 512], BF16)
        latkv, _ = sba.tile([128, 2, 512], BF16)
        sqt = [sba.tile([128, 512], BF16)[0] for _ in range(2)]
        rq_rep, _ = sba.tile([128, 512], F32)
        rkv_rep, _ = sba.tile([128, 512], F32)
        rkv_tm, _ = sba.tile([128, 4], F32)
        tabbuf = Buf()
        tabs = {nm: TL(sba.tile([128, 512], F32)[0].t, tabbuf) for nm in k_tab}
        asb = [sba.tile([128, 512], BF16)[0] for _ in range(2)]
        m1, _ = sba.tile([128, 512], F32)
        m2, _ = sba.tile([128, 512], F32)
        ost = [sba.tile([128, 512], BF16)[0] for _ in range(3)]
        vst = [sba.tile([128, 4, 512], BF16)[0] for _ in range(2)]
        kts, _ = sba.tile([128, 4, 256], BF16)
        rkf = [sba.tile([128, 512], BF16)[0] for _ in range(2)]
        cnt = {"o": 0, "a": 0, "bk": 0, "v": 0}

        pending = []

        def flush():
            while pending:
                pending.pop(0)()

        def nbk():
            cnt["bk"] += 1
            return BK[cnt["bk"] % int("6")]

        def nost():
            cnt["o"] += 1
            return ost[cnt["o"] % 3], "ost%d" % (cnt["o"] % 3)

        def fm_group(lhs_fn, nk, M, rhs_fn, N, rb):
            bk = nbk()
            for k in range(nk):
                mm(bk.ap(0, M, 0, N), lhs_fn(k), rhs_fn(k), k == 0, k == nk - 1, rb, bk.b)
            if "1" == "1":
                flush()
            return bk

        blocks = [c_block] + x_blocks
        prep.full(stage_src, blocks[0])
        for bi, blk in enumerate(blocks):
            isx = blk[0] == "x"
            nt = len(blk[1])
            N = nt * 128
            c0 = blk[1][0] * 128
            x0 = c0
            hT = prep.hT
            if isx:
                for nm in k_tab:
                    dma("sp", tabs[nm].t[:, 0:N], k_tab[nm][:, x0:x0 + N], "tab", pwrites=[tabbuf])
            nxt = blocks[bi + 1] if bi + 1 < len(blocks) else None
            EARLY = "1" == "1"
            if nxt is not None and EARLY:
                for i in range(len(nxt[1])):
                    prep.load(stage_src, nxt, i)

            def win_group(o, M=128):
                return fm_group(lambda k: winb.t[:, k, o:o + M], 8, M, lambda k: hT.t[:, k, 0:N], N, [W, hT.b])

            def rope_out(bk, M, permname, cname, sname, dst_list, p0=0, otile=None):
                if otile is None:
                    o, okey = nost()
                else:
                    o, okey = otile
                if isx:
                    a = asb[cnt["a"] % 2]
                    cnt["a"] += 1
                    cp("act", a.t[p0:p0 + M, 0:N], bk.ap(p0, p0 + M, 0, N), [bk.b], [a.b])

                    def tail():
                        pm = perms[permname]
                        b2 = nbk()
                        mm(b2.ap(p0, p0 + M, 0, N), pm.t[p0:p0 + M, p0:p0 + M], a.t[p0:p0 + M, 0:N], True, True, [W, a.b], b2.b)
                        tt("dve", m1.t[p0:p0 + M, 0:N], a.t[p0:p0 + M, 0:N], tabs[cname].t[p0:p0 + M, 0:N], ALU.mult, [a.b, tabs[cname].b], [m1.b])
                        tt("dve", m2.t[p0:p0 + M, 0:N], b2.ap(p0, p0 + M, 0, N), tabs[sname].t[p0:p0 + M, 0:N], ALU.mult, [b2.b, tabs[sname].b], [m2.b])
                        tt("dve", o.t[p0:p0 + M, 0:N], m1.t[p0:p0 + M, 0:N], m2.t[p0:p0 + M, 0:N], ALU.add, [m1.b, m2.b], [o.b])
                        for (dst_tl, dst_ap, r0, r1) in dst_list:
                            dma("pool", dst_ap, o.t[r0:r1, 0:N], okey, reads=[o.b], pwrites=[dst_tl.b])
                    pending.append(tail)
                    if "1" != "1":
                        flush()
                else:
                    cp("act", o.t[p0:p0 + M, 0:N], bk.ap(p0, p0 + M, 0, N), [bk.b], [o.b])
                    for (dst_tl, dst_ap, r0, r1) in dst_list:
                        dma("pool", dst_ap, o.t[r0:r1, 0:N], okey, reads=[o.b], pwrites=[dst_tl.b])

            for (lat, o_lat, nch, rrep, dim) in ((latq, O_QLAT, 3, rq_rep, 384), (latkv, O_KVLAT, 2, rkv_rep, 256)):
                ssb = BK[6]
                for c in range(nch):
                    bk = win_group(o_lat + c * 128)
                    cp("act", lat.t[:, c, 0:N], bk.ap(c1=N), [bk.b], pwrites=[lat.b])
                    sq = sqt[c % 2]
                    act(sq.t[:, 0:N], bk.ap(c1=N), AF.Square, [bk.b], [sq.b])
                    mm(ssb.ap(c1=N), onesb.t[:], sq.t[:, 0:N], c == 0, c == nch - 1, [onesb.b, sq.b], ssb.b)
                    if lat is latkv:
                        for i in range(nt):
                            mmp(BK[7].ap(0, 128, i, i + 1), sq.t[:, i * 128:(i + 1) * 128], onesb.t[:, 0:1], c == 0 and i == 0,
                                c == nch - 1 and i == nt - 1, [onesb.b, sq.b], BK[7].b)
                act(rrep.t[:, 0:N], ssb.ap(c1=N), AF.Sqrt, [ssb.b, epsc.b], [rrep.b], scale=1.0 / dim, bias=epsc.t[:])
                recip(rrep.t[:, 0:N], rrep.t[:, 0:N], [rrep.b], writes=[rrep.b])
            act(rkv_tm.t[:, 0:nt], BK[7].ap(0, 128, 0, nt), AF.Sqrt, [BK[7].b, epsc.b], [rkv_tm.b], scale=1.0 / 256, bias=epsc.t[:])
            recip(rkv_tm.t[:, 0:nt], rkv_tm.t[:, 0:nt], [rkv_tm.b], writes=[rkv_tm.b])

            bk = win_group(O_KROPE, 32)
            rope_out(bk, 32, "k_p16", "k_cm", "k_sm", [(sc["KMr"], sc["KMr"].t[:, c0:c0 + N], 0, 32)])
            for c in range(4):
                bk = win_group(O_DQ + c * 128)
                rope_out(bk, 128, "k_p32", "k_cd", "k_sd", [(sc["QD"], sc["QD"].t[c * 128:(c + 1) * 128, c0:c0 + N], 0, 128)])
            for c in range(4):
                bk = win_group(O_DK + c * 128)
                rope_out(bk, 128, "k_p32", "k_cd", "k_sd", [(sc["KD"], sc["KD"].t[c * 128:(c + 1) * 128, c0:c0 + N], 0, 128)])
            for c in range(2):
                bk = win_group(O_RQ + c * 128)
                rope_out(bk, 128, "k_p64", "k_cr", "k_sr", [(sc["RQ"], sc["RQ"].t[c * 128:(c + 1) * 128, c0:c0 + N], 0, 128)])
            for c in range(2):
                bk = win_group(O_RK + c * 128)
                rope_out(bk, 128, "k_p64", "k_cr", "k_sr", [(sc["RK"], sc["RK"].t[c * 128:(c + 1) * 128, c0:c0 + N], 0, 128)],
                         otile=(rkf[c], "rkf%d" % c))
            if isx or ctx_out:
                for c in range(4):
                    bk = win_group(O_RG + c * 128)
                    o, okey = nost()
                    act(o.t[:, 0:N], bk.ap(c1=N), AF.Silu, [bk.b], [o.b])
                    dma("pool", sc["RG"].t[c * 128:(c + 1) * 128, c0:c0 + N], o.t[:, 0:N], okey, reads=[o.b], pwrites=[sc["RG"].b])
                for c in range(24):
                    bk = win_group(O_GATE + c * 128)
                    o, okey = nost()
                    act(o.t[:, 0:N], bk.ap(c1=N), AF.Sigmoid, [bk.b], [o.b])
                    dma("pool", sc["GATE"].t[c * 128:(c + 1) * 128, c0:c0 + N], o.t[:, 0:N], okey, reads=[o.b], pwrites=[sc["GATE"].b])
            flush()
            tb = BK[4]
            tbv = PS[:, 2048:2560].bitcast(BF16)
            first = True
            for i in range(nt):
                for c in range(2):
                    P.add("pe", lambda e, i=i, c=c: e.transpose(out=tbv[:, (i * 2 + c) * 128:(i * 2 + c + 1) * 128],
                                                           in_=rkf[c].t[:, i * 128:(i + 1) * 128], identity=identb.t[:]),
                          [rkf[c].b, identb.b], **({"writes": [tb.b]} if first else {"pwrites": [tb.b]}))
                    first = False
            cp("dve", kts.t[:, 0:nt, :], tbv[:, 0:nt * 256].rearrange("p (i c) -> p i c", c=256), [tb.b], [kts.b])
            dma("pool", sc["RKt"].t[c0:c0 + N, :].rearrange("(i p) c -> p i c", p=128), kts.t[:, 0:nt, :], "kts", reads=[kts.b],
                pwrites=[sc["RKt"].b])
            if nxt is not None and EARLY:
                for i in range(len(nxt[1])):
                    prep.sumsq(nxt, i)
                prep.rstd(nxt)
            for (o_v, dst) in ((O_DV, sc["VD"]), (O_RV, sc["RV"])):
                v = vst[cnt["v"] % 2]
                vkey = "vst%d" % (cnt["v"] % 2)
                cnt["v"] += 1
                for i in range(nt):
                    bk = nbk()
                    for k in range(8):
                        mm(bk.ap(), hT.t[:, k, i * 128:(i + 1) * 128], winb.t[:, k, o_v:o_v + 512], k == 0, k == 7, [W, hT.b], bk.b)
                    cp("act" if i % 2 == 0 else "dve", v.t[:, i, :], bk.ap(), [bk.b], pwrites=[v.b])
                dma("pool", dst.t[c0:c0 + N, :].rearrange("(i p) c -> p i c", p=128), v.t[:, 0:nt, :], vkey, reads=[v.b], pwrites=[dst.b])
            if nxt is not None and EARLY:
                for i in range(len(nxt[1])):
                    prep.transpose(nxt, i)
            for h in range(8):
                bk = fm_group(lambda k: wqb.t[:, k, h * 96:h * 96 + 96], 3, 96, lambda k: latq.t[:, k, 0:N], N, [W, latq.b])
                o, okey = nost()
                if isx:
                    a = asb[cnt["a"] % 2]
                    cnt["a"] += 1
                    cp("act", a.t[0:96, 0:N], bk.ap(0, 96, 0, N), [bk.b], [a.b])
                    MLT = "0" == "1"
                    if MLT:
                        tt("dve", o.t[0:64, 0:N], bk.ap(0, 64, 0, N), rq_rep.t[0:64, 0:N], ALU.mult, [bk.b, rq_rep.b], [o.b])

                    def tail(a=a, o=o, okey=okey, h=h, bk=bk, MLT=MLT):
                        b2 = nbk()
                        mm(b2.ap(0, 96, 0, N), perms["k_p16"].t[0:96, 0:96], a.t[0:96, 0:N], True, True, [W, a.b], b2.b)
                        tt("dve", m1.t[64:96, 0:N], a.t[64:96, 0:N], tabs["k_cm"].t[64:96, 0:N], ALU.mult, [a.b, tabs["k_cm"].b], [m1.b])
                        tt("dve", m2.t[64:96, 0:N], b2.ap(64, 96, 0, N), tabs["k_sm"].t[64:96, 0:N], ALU.mult, [b2.b, tabs["k_sm"].b], [m2.b])
                        tt("dve", m1.t[64:96, 0:N], m1.t[64:96, 0:N], m2.t[64:96, 0:N], ALU.add, [m2.b], writes=[m1.b])
                        if not MLT:
                            tt("dve", o.t[0:64, 0:N], bk.ap(0, 64, 0, N), rq_rep.t[0:64, 0:N], ALU.mult, [bk.b, rq_rep.b], [o.b])
                        tt("dve", o.t[64:96, 0:N], m1.t[64:96, 0:N], rq_rep.t[64:96, 0:N], ALU.mult, [m1.b, rq_rep.b], pwrites=[o.b])
                        dma("pool", sc["QM"].t[h, :, c0:c0 + N], o.t[0:96, 0:N], okey, reads=[o.b], pwrites=[sc["QM"].b])
                    pending.append(tail)
                    if "1" != "1":
                        flush()
                else:
                    tt("dve", o.t[0:96, 0:N], bk.ap(0, 96, 0, N), rq_rep.t[0:96, 0:N], ALU.mult, [bk.b, rq_rep.b], [o.b])
                    dma("pool", sc["QM"].t[h, :, c0:c0 + N], o.t[0:96, 0:N], okey, reads=[o.b], pwrites=[sc["QM"].b])
            for h in range(8):
                bk = fm_group(lambda k: wkvb.t[:, k, h * 128:h * 128 + 64], 2, 64, lambda k: latkv.t[:, k, 0:N], N, [W, latkv.b])
                o, okey = nost()
                tt("dve", o.t[0:64, 0:N], bk.ap(0, 64, 0, N), rkv_rep.t[0:64, 0:N], ALU.mult, [bk.b, rkv_rep.b], [o.b])
                dma("pool", sc["KMn"].t[h, :, c0:c0 + N], o.t[0:64, 0:N], okey, reads=[o.b], pwrites=[sc["KMn"].b])
            v = vst[cnt["v"] % 2]
            vkey = "vst%d" % (cnt["v"] % 2)
            cnt["v"] += 1
            wv = wkvb.t[:].rearrange("p k (h c) -> p k h c", c=128)
            for i in range(nt):
                bk = nbk()
                for k in range(2):
                    mm(bk.ap().rearrange("p (h c) -> p h c", c=64), latkv.t[:, k, i * 128:(i + 1) * 128], wv[:, k, :, 64:128], k == 0, k == 1,
                       [W, latkv.b], bk.b)
                ts("dve", v.t[:, i, :], bk.ap(), rkv_tm.t[:, i:i + 1], None, ALU.mult, None, [bk.b, rkv_tm.b], pwrites=[v.b])
            dma("pool", sc["VM"].t[c0:c0 + N, :].rearrange("(i p) c -> p i c", p=128), v.t[:, 0:nt, :], vkey, reads=[v.b], pwrites=[sc["VM"].b])
            flush()
            if nxt is not None and not EARLY:
                prep.full(stage_src, nxt)

    def pass_attn(l, ctx_out):
        new_pass()
        sc = SC[l]
        lam_init = 0.8 - 0.6 * math.exp(-0.3 * l)
        QW = T if ctx_out else S
        KT = [sba.tile([128, T], BF16)[0] for _ in range(2)]
        QT = [sba.tile([128, T], BF16)[0] for _ in range(2)]
        VT = [sba.tile([128, NKT, 128], BF16)[0] for _ in range(2)]
        pt = [sba.tile([128, 1536], BF16)[0] for _ in range(4)]
        accD, _ = sba.tile([128, 1024], F32)
        accP, _ = sba.tile([128, 1024], F32)
        rc, _ = sba.tile([128, 512], F32)
        on = [sba.tile([128, 512], F32)[0] for _ in range(2)]
        dsq, _ = sba.tile([128, 512], F32)
        drs, _ = sba.tile([128, 512], F32)
        ob = [sba.tile([128, 512], BF16)[0] for _ in range(2)]
        dl, _ = sba.tile([128, 4, 64], F32)
        dma("sp", dl.t[:], diff_lambda[l].rearrange("a b -> (a b)").partition_broadcast(128).rearrange("p (a b) -> p a b", b=64),
            "mc", writes=[dl.b])
        dpr, _ = sba.tile([128, 2, 64], F32)
        tt("dve", dpr.t[:, 0, :], dl.t[:, 0, :], dl.t[:, 1, :], ALU.mult, [dl.b], pwrites=[dpr.b])
        tt("dve", dpr.t[:, 1, :], dl.t[:, 2, :], dl.t[:, 3, :], ALU.mult, [dl.b], pwrites=[dpr.b])
        dsum, _ = sba.tile([128, 2], F32)
        P.add("dve", lambda e: e.reduce_sum(out=dsum.t[:], in_=dpr.t[:], axis=mybir.AxisListType.X), [dpr.b], [dsum.b])
        dex, _ = sba.tile([128, 2], F32)
        act(dex.t[:], dsum.t[:], AF.Exp, [dsum.b], [dex.b])
        nlam, _ = sba.tile([128, 1], F32)
        stt("dve", nlam.t[:], dex.t[:, 1:2], -lam_init, dex.t[:, 0:1], ALU.add, ALU.subtract, [dex.b], [nlam.b])
        dng, _ = sba.tile([128, 1], F32)
        dma("sp", dng.t[:], diff_norm[l].rearrange("(p o) -> p o", o=1), "mc", writes=[dng.b], allow_slow_non_contiguous=True)
        ts("dve", dng.t[:], dng.t[:], 1.0 - lam_init, None, ALU.mult, None, [], writes=[dng.b])

        units = [("m", h) for h in range(8)] + [("d", h) for h in range(4)]
        grp = {"n": 0, "p": 0}
        retgen = ret_gen(l, ctx_out)

        def pull():
            pass

        def rstd_lnexp(out_ap, in_ap, dim, reads, wbuf):
            act(out_ap, in_ap, AF.Ln, list(reads) + [epsc.b], [wbuf], scale=1.0 / dim, bias=epsc.t[:])
            act(out_ap, out_ap, AF.Exp, [], writes=[wbuf], scale=-0.5)

        POOLACC = True

        def diff_block(K, Q, V, q0, N, kts):
            O1, O2, S1 = BK[6], BK[7], BK[4]
            P.add("dve", lambda e: e.memset(accD.t[:, 0:512], 0.0), [], writes=[accD.b])

            def issue_qk(kt):
                gi = grp["n"] % 2
                grp["n"] += 1
                for j in range(2):
                    bk = BK[2 * gi + j]
                    mm(bk.ap(c1=N), K.t[j * 64:(j + 1) * 64, kt * 128:(kt + 1) * 128], Q.t[j * 64:(j + 1) * 64, q0:q0 + N], True, True,
                       [K.b, Q.b], bk.b)
                return gi

            def issue_rest(kt, gi, idx, first, last):
                p = pt[grp["p"] % 4]
                grp["p"] += 1
                rb = [BK[2 * gi].b, BK[2 * gi + 1].b]
                if N == 512:
                    act(p.t[:, 0:1024], pair_ap(gi), AF.Exp, rb, [p.b])
                else:
                    act(p.t[:, 0:N], pair_ap(gi, 0, 128, 0, N), AF.Exp, [rb[0]], [p.b])
                    act(p.t[:, 512:512 + N], pair_ap(gi, 0, 128, 512, 512 + N), AF.Exp, [rb[1]], pwrites=[p.b])
                mm(O1.ap(c1=N), V.t[:, kt, :], p.t[:, 0:N], first, last, [V.b, p.b], O1.b)
                mm(S1.ap(c1=N), onesb.t[:], p.t[:, 0:N], first, last, [onesb.b, p.b], S1.b)
                mm(O2.ap(c1=N), V.t[:, kt, :], p.t[:, 512:512 + N], first, last, [V.b, p.b], O2.b)
                tt("dve", accD.t[:, 0:N], accD.t[:, 0:N], p.t[:, 512:512 + N], ALU.add, [p.b], writes=[accD.b])

            q = []
            for idx, kt in enumerate(kts):
                gi = issue_qk(kt)
                q.append((kt, gi, idx, idx == 0, idx == len(kts) - 1))
                if len(q) > 1:
                    issue_rest(*q.pop(0))
            while q:
                issue_rest(*q.pop(0))
            S2 = BK[5]
            mm(S2.ap(c1=N), onesf.t[:], accD.t[:, 0:N], True, True, [onesf.b, accD.b], S2.b)
            for j, (O, Sb) in enumerate(((O1, S1), (O2, S2))):
                recip(rc.t[:, 0:N], Sb.ap(c1=N), [Sb.b], [rc.b])
                tt("dve", on[j].t[:, 0:N], O.ap(c1=N), rc.t[:, 0:N], ALU.mult, [O.b, rc.b], [on[j].b])

        def load_unit(ui):
            kind, h = units[ui]
            s = ui % 2
            K, Q, V = KT[s], QT[s], VT[s]
            if kind == "m":
                dma("sp", K.t[0:64, :], sc["KMn"].t[h], "ak%d" % s, reads=[sc["KMn"].b], pwrites=[K.b])
                dma("sp", K.t[64:96, :], sc["KMr"].t[:, :], "ak%d" % s, reads=[sc["KMr"].b], pwrites=[K.b])
                dma("sp", Q.t[0:96, 0:QW], sc["QM"].t[h, :, 0:QW], "aq%d" % s, reads=[sc["QM"].b], writes=[Q.b])
                for t0 in range(0, NKT, 16):
                    t1 = min(NKT, t0 + 16)
                    dma("sp", V.t[:, t0:t1, 0:64], sc["VM"].t[t0 * 128:t1 * 128, h * 64:(h + 1) * 64].rearrange("(t p) c -> p t c", p=128),
                        "av%d" % s, reads=[sc["VM"].b], pwrites=[V.b])
                P.add("pool", lambda e: e.memset(V.t[:, :, 64:128], 1.0), [], pwrites=[V.b])
            else:
                dma("sp", K.t[:, :], sc["KD"].t[h * 128:(h + 1) * 128, :], "ak%d" % s, reads=[sc["KD"].b], writes=[K.b])
                dma("sp", Q.t[:, 0:QW], sc["QD"].t[h * 128:(h + 1) * 128, 0:QW], "aq%d" % s, reads=[sc["QD"].b], writes=[Q.b])
                for t0 in range(0, NKT, 16):
                    t1 = min(NKT, t0 + 16)
                    dma("sp", V.t[:, t0:t1, :], sc["VD"].t[t0 * 128:t1 * 128, h * 128:(h + 1) * 128].rearrange("(t p) c -> p t c", p=128),
                        "av%d" % s, reads=[sc["VD"].b], pwrites=[V.b])

        def softmax_block(K, Q, V, p0, p1, q0, N, kts, obk, sbk):
            assert sbk is None
            groups = [kts[i:i + 3] for i in range(0, len(kts), 3)]

            def issue_qk(g):
                gi = grp["n"] % 2
                grp["n"] += 1
                for jj, kt in enumerate(g):
                    bk = BK[3 * gi + jj]
                    mm(bk.ap(c1=N), K.t[p0:p1, kt * 128:(kt + 1) * 128], Q.t[p0:p1, q0:q0 + N], True, True, [K.b, Q.b], bk.b)
                return gi

            def issue_rest(g, gi, first, last):
                p = pt[grp["p"] % 4]
                grp["p"] += 1
                ng = len(g)
                rb = [BK[3 * gi + jj].b for jj in range(ng)]
                if N == 512:
                    act(p.t[:, 0:ng * 512], triple_ap(gi, 0, ng * 512), AF.Exp, rb, [p.b])
                else:
                    for jj in range(ng):
                        act(p.t[:, jj * 512:jj * 512 + N], triple_ap(gi, jj * 512, jj * 512 + N), AF.Exp, [rb[jj]],
                            **({"writes": [p.b]} if jj == 0 else {"pwrites": [p.b]}))
                for jj, kt in enumerate(g):
                    st = first and jj == 0
                    sp_ = last and jj == ng - 1
                    mm(obk.ap(c1=N), V.t[:, kt, :], p.t[:, jj * 512:jj * 512 + N], st, sp_, [V.b, p.b], obk.b)

            q = []
            for gidx, g in enumerate(groups):
                gi = issue_qk(g)
                q.append((g, gi, gidx == 0, gidx == len(groups) - 1))
                if len(q) > 1:
                    issue_rest(*q.pop(0))
            while q:
                issue_rest(*q.pop(0))

        def qblocks():
            res = [(b * 512, 512, list(range(NKT))) for b in range(S // 512)]
            if ctx_out:
                res.append((S, 256, [NXT, NXT + 1]))
            return res

        load_unit(0)
        for ui, (kind, h) in enumerate(units):
            s = ui % 2
            K, Q, V = KT[s], QT[s], VT[s]
            if ui + 1 < len(units):
                load_unit(ui + 1)
            for (q0, N, kts) in qblocks():
                if kind == "m":
                    obk = BK[6]
                    softmax_block(K, Q, V, 0, 96, q0, N, kts, obk, None)
                    recip(rc.t[0:64, 0:N], obk.ap(64, 128, 0, N), [obk.b], [rc.b])
                    o = ob[0]
                    tt("dve", o.t[0:64, 0:N], obk.ap(0, 64, 0, N), rc.t[0:64, 0:N], ALU.mult, [obk.b, rc.b], [o.b])
                    dma("pool", sc["YM"].t[h * 64:(h + 1) * 64, q0:q0 + N], o.t[0:64, 0:N], "ob0", reads=[o.b], pwrites=[sc["YM"].b])
                else:
                    diff_block(K, Q, V, q0, N, kts)
                    stt("dve", on[0].t[:, 0:N], on[1].t[:, 0:N], nlam.t[:, 0:1], on[0].t[:, 0:N], ALU.mult, ALU.add, [on[1].b, nlam.b],
                        writes=[on[0].b])
                    tt("pool", dsq.t[:, 0:N], on[0].t[:, 0:N], on[0].t[:, 0:N], ALU.mult, [on[0].b], [dsq.b])
                    nb = BK[7]
                    mm(nb.ap(c1=N), onesf.t[:], dsq.t[:, 0:N], True, True, [onesf.b, dsq.b], nb.b)
                    rstd_lnexp(drs.t[:, 0:N], nb.ap(c1=N), 128, [nb.b], drs.b)
                    o = ob[1]
                    stt("dve", o.t[:, 0:N], on[0].t[:, 0:N], dng.t[:, 0:1], drs.t[:, 0:N], ALU.mult, ALU.mult, [on[0].b, dng.b, drs.b], [o.b])
                    dma("pool", sc["YD"].t[h * 128:(h + 1) * 128, q0:q0 + N], o.t[:, 0:N], "ob1", reads=[o.b], pwrites=[sc["YD"].b])
        for _ in retgen:
            pass

    def ret_gen(l, ctx_out):
        sc = SC[l]
        RPE = "dve"
        rd, _ = sba.tile([128, 8], F32)
        dma("sp", rd.t[:], ret_decay[l].rearrange("a b -> (a b)").partition_broadcast(128), "mc", writes=[rd.b])
        lg, _ = sba.tile([128, 8], F32)
        act(lg.t[:], rd.t[:], AF.Exp, [rd.b], [lg.b])
        ts("dve", lg.t[:], lg.t[:], -1.0, None, ALU.mult, None, [], writes=[lg.b])
        cdec, _ = sba.tile([128, 8], F32)
        act(cdec.t[:], lg.t[:], AF.Exp, [lg.b], [cdec.b], scale=128.0)
        r4, _ = sba.tile([128, 4, 128], F32)
        dma("sp", r4.t[:], k_ret4.rearrange("a p q -> p a q"), "mc", writes=[r4.b])
        qdc, _ = sba.tile([128, 2, 128], F32)
        dma("sp", qdc.t[:], k_qd.rearrange("a p q -> p a q"), "mc", writes=[qdc.b])
        kdc, _ = sba.tile([128, 2], F32)
        dma("sp", kdc.t[:], k_kd, "mc", writes=[kdc.b])
        maskT, _ = sba.tile([128, 2, 4, 128], F32)
        qdT, _ = sba.tile([128, 2, 4, 128], F32)
        kdT, _ = sba.tile([128, 2, 4], F32)
        for d in range(2):
            for h in range(4):
                i = d * 4 + h
                act(maskT.t[:, d, h, :], r4.t[:, 2 * d, :], AF.Exp, [r4.b, lg.b], pwrites=[maskT.b], scale=lg.t[:, i:i + 1])
                tt("dve", maskT.t[:, d, h, :], maskT.t[:, d, h, :], r4.t[:, 2 * d + 1, :], ALU.mult, [r4.b], pwrites=[maskT.b])
                act(qdT.t[:, d, h, :], qdc.t[:, d, :], AF.Exp, [qdc.b, lg.b], pwrites=[qdT.b], scale=lg.t[:, i:i + 1])
            act(kdT.t[:, d, :], lg.t[:, d * 4:d * 4 + 4], AF.Exp, [lg.b, kdc.b], pwrites=[kdT.b], scale=kdc.t[:, d:d + 1])
        rng_col, _ = sba.tile([128, 1], F32)
        dma("sp", rng_col.t[:], ret_norm[l].rearrange("(p o) -> p o", o=1), "mc", writes=[rng_col.b], allow_slow_non_contiguous=True)

        qf = [sba.tile([64, 4, 128], BF16)[0] for _ in range(2)]
        kf = [sba.tile([64, 4, 128], BF16)[0] for _ in range(2)]
        ktm = [sba.tile([128, 256], BF16)[0] for _ in range(2)]
        vtm = [sba.tile([128, 512], BF16)[0] for _ in range(2)]
        am, _ = sba.tile([128, 4, 128], BF16)
        kdm, _ = sba.tile([128, 4, 64], BF16)
        qdm, _ = sba.tile([64, 4, 128], BF16)
        Sf, _ = sba.tile([64, 4, 128], F32)
        Sb16, _ = sba.tile([64, 4, 128], BF16)
        osb = [sba.tile([128, 4, 128], F32)[0] for _ in range(2)]
        ofl = [sba.tile([128, 4, 128], F32)[0] for _ in range(2)]
        gsb = [sba.tile([128, 4, 128], BF16)[0] for _ in range(2)]
        sqs, _ = sba.tile([128, 512], F32)
        rrs, _ = sba.tile([128, 512], F32)
        yo = [sba.tile([128, 4, 128], BF16)[0] for _ in range(2)]

        yield
        fwd = [NXT, NXT + 1] + list(range(NXT))
        bwd = [NXT + 1, NXT] + list(range(NXT - 1, -1, -1))
        step = 0
        posts = []

        def flush_posts():
            while posts:
                posts.pop(0)()

        for d, order in ((0, fwd), (1, bwd)):
            P.add("dve", lambda e: e.memset(Sf.t[:], 0.0), [], writes=[Sf.b])
            P.add("dve", lambda e: e.memset(Sb16.t[:], 0.0), [], writes=[Sb16.b])
            for t in order:
                s = step % 2
                step += 1
                isx = t < NXT
                need_out = isx or ctx_out
                c0 = t * 128
                dma("sp", qf[s].t[:], sc["RQ"].t[:, c0:c0 + 128].rearrange("(h d) q -> d h q", d=64), "rq%d" % s, reads=[sc["RQ"].b],
                    writes=[qf[s].b])
                dma("sp", kf[s].t[:], sc["RK"].t[:, c0:c0 + 128].rearrange("(h d) q -> d h q", d=64), "rk%d" % s, reads=[sc["RK"].b],
                    writes=[kf[s].b])
                dma("sp", ktm[s].t[:], sc["RKt"].t[c0:c0 + 128, :], "rkt%d" % s, reads=[sc["RKt"].b], writes=[ktm[s].b])
                dma("sp", vtm[s].t[:], sc["RV"].t[c0:c0 + 128, :], "rv%d" % s, reads=[sc["RV"].b], writes=[vtm[s].b])
                obk = BK[1] if s == 0 else BK[3]
                if need_out and d == 1:
                    dma("sp", ofl[s].t[:], sc["OF"].t[:, t, :].rearrange("p (h q) -> p h q", q=128), "ofl%d" % s, reads=[sc["OF"].b],
                        writes=[ofl[s].b])
                    dma("sp", gsb[s].t[:], sc["RG"].t[:, c0:c0 + 128].rearrange("(h p) q -> p h q", p=128), "gsb%d" % s, reads=[sc["RG"].b],
                        writes=[gsb[s].b])
                if need_out:
                    ab = BK[0]
                    for h in range(4):
                        mmp(ab.ap(0, 128, h * 128, (h + 1) * 128), kf[s].t[:, h, :], qf[s].t[:, h, :], True, True, [kf[s].b, qf[s].b], ab.b)
                    tt("dve", am.t[:], ab.ap().rearrange("p (h q) -> p h q", q=128), maskT.t[:, d, :, :], ALU.mult, [ab.b, maskT.b], [am.b])
                    tt(RPE, qdm.t[:], qf[s].t[:], qdT.t[0:64, d, :, :], ALU.mult, [qf[s].b, qdT.b], [qdm.b])
                    for h in range(4):
                        mmp(obk.ap(0, 128, h * 128, (h + 1) * 128), vtm[s].t[:, h * 128:(h + 1) * 128], am.t[:, h, :], True, False,
                            [vtm[s].b, am.b], obk.b)
                        mmp(obk.ap(0, 128, h * 128, (h + 1) * 128), Sb16.t[:, h, :], qdm.t[:, h, :], False, True, [Sb16.b, qdm.b], obk.b)
                tt(RPE, kdm.t[:], ktm[s].t[:].rearrange("p (h d) -> p h d", d=64), kdT.t[:, d, :].unsqueeze(2).to_broadcast([128, 4, 64]),
                   ALU.mult, [ktm[s].b, kdT.b], [kdm.b])
                ub = BK[2]
                for h in range(4):
                    mmp(ub.ap(0, 64, h * 128, (h + 1) * 128), kdm.t[:, h, :], vtm[s].t[:, h * 128:(h + 1) * 128], True, True, [kdm.b, vtm[s].b], ub.b)
                for h in range(4):
                    i = d * 4 + h
                    stt("dve", Sf.t[:, h, :], Sf.t[:, h, :], cdec.t[0:64, i:i + 1], ub.ap(0, 64, h * 128, (h + 1) * 128), ALU.mult, ALU.add,
                        [ub.b, cdec.b], pwrites=[Sf.b])
                cp("act", Sb16.t[:], Sf.t[:], [Sf.b], [Sb16.b])
                flush_posts()
                if not need_out:
                    continue

                def post(s=s, t=t, c0=c0, d=d, obk=obk):
                    o = osb[s]
                    if d == 0:
                        cp("act", o.t[:], obk.ap().rearrange("p (h q) -> p h q", q=128), [obk.b], [o.b])
                        dma("pool", sc["OF"].t[:, t, :].rearrange("p (h q) -> p h q", q=128), o.t[:], "osb%d" % s, reads=[o.b],
                            pwrites=[sc["OF"].b])
                        return
                    of = ofl[s]
                    gs = gsb[s]
                    tt("dve", o.t[:], obk.ap().rearrange("p (h q) -> p h q", q=128), of.t[:], ALU.add, [obk.b, of.b], [o.b])
                    of2 = o.t[:].rearrange("p h q -> p (h q)")
                    tt(RPE, sqs.t[:], of2, of2, ALU.mult, [o.b], [sqs.b])
                    nb = BK[4]
                    mm(nb.ap(), onesf.t[:], sqs.t[:], True, True, [onesf.b, sqs.b], nb.b)
                    act(rrs.t[:], nb.ap(), AF.Ln, [nb.b, epsc.b], [rrs.b], scale=1.0 / 128, bias=epsc.t[:])
                    act(rrs.t[:], rrs.t[:], AF.Exp, [], writes=[rrs.b], scale=-0.5)
                    stt("dve", of2, of2, rng_col.t[:, 0:1], rrs.t[:], ALU.mult, ALU.mult, [rng_col.b, rrs.b], writes=[o.b])
                    y = yo[s]
                    tt(RPE, y.t[:], o.t[:], gs.t[:], ALU.mult, [o.b, gs.b], [y.b])
                    dma("pool", sc["YR"].t[:, c0:c0 + 128].rearrange("(h p) q -> p h q", p=128), y.t[:], "yo%d" % s, reads=[y.b],
                        pwrites=[sc["YR"].b])
                posts.append(post)
            flush_posts()
        yield

    def pass_merge(l, stage_src, stage_dst, ctx_out):
        new_pass()
        sc = SC[l]
        wbb, _ = sba.tile([128, 12, D], BF16)
        wob = TL(sba.tile([128, 8, D], BF16)[0].t, wbb.b)
        for i in range(3):
            for k in range(4):
                load_w_cast(wbb, i * 4 + k, w_branch[l, i][k * 128:(k + 1) * 128, :], "wl", D)
        for k in range(8):
            load_w_cast(wob, k, w_out[l][k * 128:(k + 1) * 128, :], "wl", D)
        ysb = [sba.tile([128, 12, 512], BF16)[0] for _ in range(2)]
        gsb = [sba.tile([128, 24, 512], BF16)[0] for _ in range(2)]
        yT, _ = sba.tile([128, 8, 512], BF16)
        mt = [sba.tile([128, 512], F32)[0] for _ in range(3)]
        gate_bc, _ = sba.tile([128, D], F32)
        xres = [sba.tile([128, D], F32)[0] for _ in range(2)]
        ytmp = [sba.tile([128, 512], F32)[0] for _ in range(2)]
        blocks = ([c_block] if ctx_out else []) + x_blocks

        def load_blk(bi):
            blk = blocks[bi]
            N = len(blk[1]) * 128
            c0 = blk[1][0] * 128
            s = bi % 2
            for i, nm in enumerate(("YM", "YD", "YR")):
                dma("sp", ysb[s].t[:, i * 4:(i + 1) * 4, 0:N], sc[nm].t[:, c0:c0 + N].rearrange("(k p) n -> p k n", p=128), "my%d" % s,
                    reads=[sc[nm].b], pwrites=[ysb[s].b])
            for i in range(3):
                dma("sp", gsb[s].t[:, i * 8:(i + 1) * 8, 0:N], sc["GATE"].t[i * 1024:(i + 1) * 1024, c0:c0 + N].rearrange("(k p) n -> p k n", p=128),
                    "mg%d" % s, reads=[sc["GATE"].b], pwrites=[gsb[s].b])

        load_blk(0)
        cur_stream = None
        for bi, blk in enumerate(blocks):
            N = len(blk[1]) * 128
            s = bi % 2
            if bi + 1 < len(blocks):
                load_blk(bi + 1)
            if blk[0] != cur_stream:
                cur_stream = blk[0]
                load_gate_bc(gate_bc, l, 1, 0 if cur_stream == "x" else 1)
            for oc in range(8):
                zb = [BK[(oc % 2) * 3 + i] for i in range(3)]
                for i in range(3):
                    for k in range(4):
                        mm(zb[i].ap(c1=N), wbb.t[:, i * 4 + k, oc * 128:(oc + 1) * 128], ysb[s].t[:, i * 4 + k, 0:N], k == 0, k == 3,
                           [wbb.b, ysb[s].b], zb[i].b)
                for i in range(3):
                    tt("dve", mt[i].t[:, 0:N], zb[i].ap(c1=N), gsb[s].t[:, i * 8 + oc, 0:N], ALU.mult, [zb[i].b, gsb[s].b], [mt[i].b])
                tt("pool", mt[0].t[:, 0:N], mt[0].t[:, 0:N], mt[1].t[:, 0:N], ALU.add, [mt[1].b], writes=[mt[0].b])
                tt("pool", yT.t[:, oc, 0:N], mt[0].t[:, 0:N], mt[2].t[:, 0:N], ALU.add, [mt[0].b, mt[2].b], pwrites=[yT.b])
            for i in range(len(blk[1])):
                bks = [BK[6], BK[7]]
                for j in range(2):
                    for k in range(8):
                        mm(bks[j].ap(), yT.t[:, k, i * 128:(i + 1) * 128], wob.t[:, k, j * 512:(j + 1) * 512], k == 0, k == 7, [yT.b, wbb.b], bks[j].b)
                residual_store(stage_src, stage_dst, blk, i, bks, gate_bc, xres, ytmp)

    pass_mod()
    stage = None
    for l in range(DEPTH):
        last = l == DEPTH - 1
        ctx_out = not last
        pass_ffn(l, 1, stage, (l, 1), True, False)
        pass_proj(l, (l, 1), ctx_out)
        pass_attn(l, ctx_out)
        pass_merge(l, (l, 1), (l, 2), ctx_out)
        pass_ffn(l, 2, (l, 2), None if last else (l, 3), ctx_out, last)
        stage = (l, 3)
    P.barrier()
    stats = P.emit()
    return nc, stats


_CACHE = {}


def _get(S, DEBUG=()):
    key = (S, tuple(DEBUG))
    if key not in _CACHE:
        _CACHE[key] = (build(S, DEBUG), host_consts(S))
    return _CACHE[key]


def kernel(**inputs):
    x = np.asarray(inputs["x"], np.float32)
    B, S, _ = x.shape
    (nc, _), hc = _get(S)
    shared = {k: np.ascontiguousarray(np.asarray(v, np.float32)) for k, v in inputs.items() if k not in ("x", "c", "ctx")}
    in_maps = []
    for b in range(B):
        m = dict(shared)
        m.update(hc)
        m["x"] = np.ascontiguousarray(x[b])
        m["c"] = np.ascontiguousarray(np.asarray(inputs["c"], np.float32)[b])
        m["ctx"] = np.ascontiguousarray(np.asarray(inputs["ctx"], np.float32)[b])
        in_maps.append(m)
    res = run_bass_kernel_spmd(nc, in_maps, core_ids=list(range(B)))
    return np.stack([np.asarray(r["out"], np.float32) for r in res.results], axis=0)
```

```python
import math
import contextlib
import numpy as np
import ml_dtypes
import concourse.bass as bass
import concourse.mybir as mybir
from concourse.bass_utils import run_bass_kernel_spmd

F32 = mybir.dt.float32
BF16 = mybir.dt.bfloat16
AF = mybir.ActivationFunctionType
ALU = mybir.AluOpType

D = 1024
CTX = 256
DFF = 2816
NF = DFF // 128
PROJ = 6816
EPS = 1e-6
O_QLAT, O_KVLAT, O_KROPE, O_DQ, O_DK, O_DV = 0, 384, 640, 672, 1184, 1696
O_RQ, O_RK, O_RV, O_RG, O_GATE = 2208, 2464, 2720, 3232, 3744
DEPTH = 2


class Buf:
    __slots__ = ("writers", "readers")

    def __init__(self):
        self.writers = {}
        self.readers = {}


class Op:
    __slots__ = ("eng", "fn", "deps", "key", "is_dma", "signal", "val")


class Prog:
    def __init__(self, nc):
        self.nc = nc
        self.ops = []
        self.last = {}

    def add(self, eng, fn, reads=(), writes=(), pwrites=(), dma=None, extra=None, serial=True, track=True):
        op = Op()
        op.eng = eng
        op.fn = fn
        op.is_dma = dma is not None
        op.key = ("d", dma) if dma is not None else ("e", eng)
        op.signal = False
        op.val = 0
        idx = len(self.ops)
        deps = {}

        def need(d):
            for k, i in d.items():
                if deps.get(k, -1) < i:
                    deps[k] = i

        for b in reads:
            need(b.writers)
        for b in writes:
            need(b.writers)
            need(b.readers)
        for b in pwrites:
            need(b.writers)
            need(b.readers)
        if extra:
            need(extra)
        if op.is_dma and serial and op.key in self.last:
            need({op.key: self.last[op.key]})
        if eng == "pe" and not op.is_dma:
            deps.pop(("e", "pe"), None)
        op.deps = deps
        for b in reads:
            if b.readers.get(op.key, -1) < idx:
                b.readers[op.key] = idx
        for b in writes:
            b.writers = {op.key: idx}
            b.readers = {}
        for b in pwrites:
            if b.readers:
                b.writers = {op.key: idx}
                b.readers = {}
            else:
                b.writers[op.key] = idx
        self.ops.append(op)
        if track:
            self.last[op.key] = idx
        return idx

    def barrier(self):
        snap = dict(self.last)
        for eng in ("pe", "act", "dve", "pool", "sp"):
            self.add(eng, lambda e: None, extra=snap, track=False)

    def emit(self):
        nc = self.nc
        ops = self.ops
        for op in ops:
            for k, i in op.deps.items():
                ops[i].signal = True
        cnt = {}
        for op in ops:
            if op.signal:
                cnt[op.key] = cnt.get(op.key, 0) + 1
                op.val = cnt[op.key] * (16 if op.is_dma else 1)
        keys = sorted(cnt.keys(), key=str)
        with contextlib.ExitStack() as st:
            sems = {}
            for k in keys:
                sems[k] = st.enter_context(nc.semaphore("s_" + str(k[1])))
            block = st.enter_context(nc.Block())

            def run(engname, engobj):
                known = {}
                for op in ops:
                    if op.eng != engname:
                        continue
                    for k, i in op.deps.items():
                        v = ops[i].val
                        if known.get(k, 0) < v:
                            engobj.wait_ge(sems[k], v)
                            known[k] = v
                    ins = op.fn(engobj)
                    if op.signal:
                        assert ins is not None
                        ins.then_inc(sems[op.key], 16 if op.is_dma else 1)

            @block.tensor
            def _(e):
                run("pe", e)

            @block.scalar
            def _(e):
                run("act", e)

            @block.vector
            def _(e):
                run("dve", e)

            @block.gpsimd
            def _(e):
                run("pool", e)

            @block.sync
            def _(e):
                run("sp", e)
        return len(ops), len(keys)


class TL:
    __slots__ = ("t", "b")

    def __init__(self, t, b=None):
        self.t = t
        self.b = b if b is not None else Buf()


def _dtsize(dt):
    return 2 if dt == BF16 else 4


class SBAlloc:
    def __init__(self, nc):
        self.nc = nc
        self.base = (nc.sbuf_base + 63) // 64 * 64
        self.top = nc.sbuf_top
        self.cur = self.base
        self.n = 0

    def tile(self, shape, dt, at=None, buf=None):
        size = int(np.prod(shape[1:])) * _dtsize(dt)
        size = (size + 63) // 64 * 64
        off = self.cur if at is None else at
        self.n += 1
        t = self.nc.alloc_sbuf_tensor_at("sb%d" % self.n, list(shape), dt, offset=off)
        if at is None:
            self.cur += size
            assert self.cur <= self.top, ("SBUF overflow", self.cur - self.base, self.top - self.base)
        tl = TL(t, buf)
        return tl, off


def host_consts(S):
    t = np.arange(S)
    row = (t // 64).astype(np.float32)
    col = (t % 64).astype(np.float32)
    tt = t.astype(np.float32)

    def tab(bs, posf):
        C = np.zeros((128, S), np.float32)
        Sn = np.zeros((128, S), np.float32)
        h = bs // 2
        for r in range(128):
            i = r % bs
            f = i % h
            inv = np.float32(10000.0) ** (-(np.float32(2 * f) / np.float32(bs)))
            ang = (posf(r) * np.float32(inv)).astype(np.float32)
            C[r] = np.cos(ang.astype(np.float64)).astype(np.float32)
            Sn[r] = np.sin(ang.astype(np.float64)).astype(np.float32)
        return C, Sn

    cm, sm = tab(16, lambda r: row if (r % 32) < 16 else col)
    cd, sd = tab(32, lambda r: row if (r % 64) < 32 else col)
    cr, sr = tab(64, lambda r: tt)

    def perm(bs):
        h = bs // 2
        Pm = np.zeros((128, 128), np.float32)
        for i in range(128):
            if i % bs < h:
                Pm[i, i + h] = -1.0
            else:
                Pm[i, i - h] = 1.0
        return np.ascontiguousarray(Pm.T)

    k = np.arange(128)[:, None].astype(np.float32)
    q = np.arange(128)[None, :].astype(np.float32)
    relF = np.maximum(q - k, 0.0)
    mskF = (q >= k).astype(np.float32)
    relB = np.maximum(k - q, 0.0)
    mskB = (k >= q).astype(np.float32)
    ret4 = np.stack([relF, mskF, relB, mskB]).astype(np.float32)
    qd = np.stack([np.broadcast_to(q + 1.0, (128, 128)), np.broadcast_to(128.0 - q, (128, 128))]).astype(np.float32)
    kd = np.stack([127.0 - k[:, 0], k[:, 0]], axis=1).astype(np.float32)
    return {
        "k_cm": cm, "k_sm": sm, "k_cd": cd, "k_sd": sd, "k_cr": cr, "k_sr": sr,
        "k_p16": perm(16), "k_p32": perm(32), "k_p64": perm(64),
        "k_ident": np.eye(128, dtype=np.float32),
        "k_ret4": ret4, "k_qd": np.ascontiguousarray(qd), "k_kd": np.ascontiguousarray(kd),
    }


def build(S=8192, DEBUG=()):
    nc = bass.Bass("TRN2", target_bir_lowering=False)
    NXT = S // 128
    T = S + CTX
    NTT = NXT + 2
    NKT = NTT
    P = Prog(nc)
    sba = SBAlloc(nc)

    def dram_in(name, shape, dt=F32):
        return nc.dram_tensor(name, list(shape), dt, kind="ExternalInput").ap()

    def dram_scr(name, shape, dt):
        kind = "ExternalOutput" if name in DEBUG else "Internal"
        return TL(nc.dram_tensor(name, list(shape), dt, kind=kind).ap())

    x_in = dram_in("x", [S, D])
    c_in = dram_in("c", [D])
    ctx_in = dram_in("ctx", [CTX, D])
    cctx_in = dram_in("c_ctx", [D])
    ada_w = dram_in("ada_w", [DEPTH, D, 9 * D])
    ada_b = dram_in("ada_b", [DEPTH, 9 * D])
    norm_gain = dram_in("norm_gain", [DEPTH, 3, D])
    ffn_w = {}
    for nm in ("ffn1_w1", "ffn1_w3", "ffn2_w1", "ffn2_w3"):
        ffn_w[nm] = dram_in(nm, [DEPTH, D, DFF])
    for nm in ("ffn1_w2", "ffn2_w2"):
        ffn_w[nm] = dram_in(nm, [DEPTH, DFF, D])
    w_in = dram_in("w_in", [DEPTH, D, PROJ])
    mla_q_norm = dram_in("mla_q_norm", [DEPTH, 384])
    mla_w_qb = dram_in("mla_w_qb", [DEPTH, 384, 768])
    mla_kv_norm = dram_in("mla_kv_norm", [DEPTH, 256])
    mla_w_kvb = dram_in("mla_w_kvb", [DEPTH, 256, 1024])
    diff_lambda = dram_in("diff_lambda", [DEPTH, 4, 64])
    diff_norm = dram_in("diff_norm", [DEPTH, 128])
    ret_decay = dram_in("ret_decay", [DEPTH, 2, 4])
    ret_norm = dram_in("ret_norm", [DEPTH, 128])
    w_branch = dram_in("w_branch", [DEPTH, 3, 512, D])
    w_out = dram_in("w_out", [DEPTH, D, D])
    final_norm = dram_in("final_norm", [D])
    k_tab = {n: dram_in(n, [128, S]) for n in ("k_cm", "k_sm", "k_cd", "k_sd", "k_cr", "k_sr")}
    k_perm = {n: dram_in(n, [128, 128]) for n in ("k_p16", "k_p32", "k_p64")}
    k_ident = dram_in("k_ident", [128, 128])
    k_ret4 = dram_in("k_ret4", [4, 128, 128])
    k_qd = dram_in("k_qd", [2, 128, 128])
    k_kd = dram_in("k_kd", [128, 2])
    out = TL(nc.dram_tensor("out", [S, D], F32, kind="ExternalOutput").ap())

    modv = dram_scr("modv", [DEPTH, 2, 9 * D], F32)
    XS = {}
    for l in range(DEPTH):
        for st in (1, 2, 3):
            if l == DEPTH - 1 and st == 3:
                continue
            XS[(l, st)] = dram_scr("xs%d_%d" % (l, st), [T, D], F32)
    SC = {}
    for l in range(DEPTH):
        SC[l] = dict(
            QM=dram_scr("QM%d" % l, [8, 96, T], BF16),
            KMn=dram_scr("KMn%d" % l, [8, 64, T], BF16),
            KMr=dram_scr("KMr%d" % l, [32, T], BF16),
            VM=dram_scr("VM%d" % l, [T, 512], BF16),
            QD=dram_scr("QD%d" % l, [512, T], BF16),
            KD=dram_scr("KD%d" % l, [512, T], BF16),
            VD=dram_scr("VD%d" % l, [T, 512], BF16),
            RQ=dram_scr("RQ%d" % l, [256, T], BF16),
            RK=dram_scr("RK%d" % l, [256, T], BF16),
            RKt=dram_scr("RKt%d" % l, [T, 256], BF16),
            RV=dram_scr("RV%d" % l, [T, 512], BF16),
            RG=dram_scr("RG%d" % l, [512, T], BF16),
            GATE=dram_scr("GATE%d" % l, [3072, T], BF16),
            YM=dram_scr("YM%d" % l, [512, T], BF16),
            YD=dram_scr("YD%d" % l, [512, T], BF16),
            YR=dram_scr("YR%d" % l, [512, T], BF16),
            OF=dram_scr("OF%d" % l, [128, NTT, 512], F32),
        )

    PA = [TL(nc.alloc_psum_tensor("pa%d" % i, [128, 1024], F32)) for i in range(3)]
    PB = [TL(nc.alloc_psum_tensor("pb%d" % i, [128, 512], F32)) for i in range(2)]
    bankbufs = [Buf() for _ in range(8)]

    class Bank:
        def __init__(self, j):
            self.j = j
            self.b = bankbufs[j]
            if j < 6:
                self.t = PA[j // 2].t
                self.o = (j % 2) * 512
            else:
                self.t = PB[j - 6].t
                self.o = 0

        def ap(self, p0=0, p1=128, c0=0, c1=512):
            return self.t[p0:p1, self.o + c0:self.o + c1]

    BK = [Bank(j) for j in range(8)]

    def pair_ap(i, p0=0, p1=128, c0=0, c1=1024):
        return PA[i].t[p0:p1, c0:c1]

    GROUP_KEYS = ("wl", "tab", "m0")

    def dma(q, out_ap, in_ap, key, reads=(), writes=(), pwrites=(), **kw):
        grp_ = key in GROUP_KEYS or key[:2] in ("ak", "av", "my", "mg", "wl")
        P.add(q, lambda e: e.dma_start(out=out_ap, in_=in_ap, **kw), reads, writes, pwrites, dma=key, serial=not grp_)

    def mm(out_ap, lhsT, rhs, start, stop, reads, bank):
        if start:
            P.add("pe", lambda e: e.matmul(out_ap, lhsT=lhsT, rhs=rhs, start=start, stop=stop), reads, writes=[bank])
        else:
            P.add("pe", lambda e: e.matmul(out_ap, lhsT=lhsT, rhs=rhs, start=start, stop=stop), reads, pwrites=[bank])

    def mmp(out_ap, lhsT, rhs, start, stop, reads, bank):
        P.add("pe", lambda e: e.matmul(out_ap, lhsT=lhsT, rhs=rhs, start=start, stop=stop), reads, pwrites=[bank])

    def act(out_ap, in_ap, func, reads, writes=(), pwrites=(), **kw):
        P.add("act", lambda e: e.activation(out=out_ap, in_=in_ap, func=func, **kw), reads, writes, pwrites)

    def tt(eng, out_ap, a, b, op, reads, writes=(), pwrites=()):
        P.add(eng, lambda e: e.tensor_tensor(out=out_ap, in0=a, in1=b, op=op), reads, writes, pwrites)

    def ts(eng, out_ap, a, s1, s2, op0, op1, reads, writes=(), pwrites=()):
        if s2 is None:
            P.add(eng, lambda e: e.tensor_scalar(out=out_ap, in0=a, scalar1=s1, scalar2=None, op0=op0), reads, writes, pwrites)
        else:
            P.add(eng, lambda e: e.tensor_scalar(out=out_ap, in0=a, scalar1=s1, scalar2=s2, op0=op0, op1=op1), reads, writes, pwrites)

    def stt(eng, out_ap, a, s, b, op0, op1, reads, writes=(), pwrites=()):
        P.add(eng, lambda e: e.scalar_tensor_tensor(out=out_ap, in0=a, scalar=s, in1=b, op0=op0, op1=op1), reads, writes, pwrites)

    def cp(eng, out_ap, in_ap, reads, writes=(), pwrites=()):
        if eng == "act":
            P.add("act", lambda e: e.activation(out=out_ap, in_=in_ap, func=AF.Copy), reads, writes, pwrites)
        else:
            P.add(eng, lambda e: e.tensor_copy(out=out_ap, in_=in_ap), reads, writes, pwrites)

    def recip(out_ap, in_ap, reads, writes=(), pwrites=()):
        P.add("dve", lambda e: e.reciprocal(out=out_ap, in_=in_ap), reads, writes, pwrites)

    def colvec(v_ap):
        return v_ap.rearrange("(k p) -> p k", p=128)

    ident, _ = sba.tile([128, 128], F32)
    identb, _ = sba.tile([128, 128], BF16)
    onesb, _ = sba.tile([128, 128], BF16)
    onesf, _ = sba.tile([128, 128], F32)
    epsc, _ = sba.tile([128, 1], F32)
    dma("sp", ident.t[:], k_ident, "c0", writes=[ident.b])
    cp("dve", identb.t[:], ident.t[:], [ident.b], [identb.b])
    P.add("dve", lambda e: e.memset(onesb.t[:], 1.0), writes=[onesb.b])
    P.add("dve", lambda e: e.memset(onesf.t[:], 1.0), writes=[onesf.b])
    P.add("dve", lambda e: e.memset(epsc.t[:], EPS), writes=[epsc.b])
    pass_base = sba.cur

    def new_pass():
        P.barrier()
        sba.cur = pass_base

    def tok_src(stage, t):
        if stage is None:
            if t < NXT:
                return x_in[t * 128:(t + 1) * 128, :], None
            return ctx_in[(t - NXT) * 128:(t - NXT + 1) * 128, :], None
        tl = XS[stage]
        return tl.t[t * 128:(t + 1) * 128, :], tl

    x_blocks = [("x", list(range(4 * b, 4 * b + 4))) for b in range(S // 512)]
    c_block = ("c", [NXT, NXT + 1])

    def col0(t):
        return t * 128

    def pass_mod():
        cT, _ = sba.tile([128, 8, 2], F32)
        dma("sp", cT.t[:, :, 0], colvec(c_in), "m0", pwrites=[cT.b], allow_slow_non_contiguous=True)
        dma("sp", cT.t[:, :, 1], colvec(cctx_in), "m0", pwrites=[cT.b], allow_slow_non_contiguous=True)
        sT, _ = sba.tile([128, 8, 2], F32)
        act(sT.t[:], cT.t[:], AF.Silu, [cT.b], [sT.b])
        wch = [sba.tile([128, 8, 512], F32)[0] for _ in range(2)]
        bia = [sba.tile([2, 512], F32)[0] for _ in range(2)]
        res = [sba.tile([2, 512], F32)[0] for _ in range(2)]
        i = 0
        for l in range(DEPTH):
            for cb in range(18):
                s = i % 2
                dma("sp", wch[s].t[:], ada_w[l][:, cb * 512:(cb + 1) * 512].rearrange("(k p) n -> p k n", p=128),
                    "mw%d" % s, writes=[wch[s].b])
                dma("sp", bia[s].t[:], ada_b[l, cb * 512:(cb + 1) * 512].partition_broadcast(2), "mb%d" % s, writes=[bia[s].b])
                bk = BK[s]
                for k in range(8):
                    mm(bk.ap(0, 2), sT.t[:, k, :], wch[s].t[:, k, :], k == 0, k == 7, [sT.b, wch[s].b], bk.b)
                tt("dve", res[s].t[:], bk.ap(0, 2), bia[s].t[:], ALU.add, [bk.b, bia[s].b], [res[s].b])
                dma("pool", modv.t[l, :, cb * 512:(cb + 1) * 512], res[s].t[:], "ms%d" % s, reads=[res[s].b], pwrites=[modv.b])
                i += 1

    def load_modcols(l, n):
        gcol, _ = sba.tile([128, 8], F32)
        dma("sp", gcol.t[:], colvec(norm_gain[l, n]), "mc", writes=[gcol.b], allow_slow_non_contiguous=True)
        res = []
        for s in range(2):
            sc, _ = sba.tile([128, 8], F32)
            sh, _ = sba.tile([128, 8], F32)
            A, _ = sba.tile([128, 8], F32)
            dma("sp", sc.t[:], colvec(modv.t[l, s, (3 * n + 1) * D:(3 * n + 2) * D]), "mc", reads=[modv.b], writes=[sc.b],
                allow_slow_non_contiguous=True)
            dma("sp", sh.t[:], colvec(modv.t[l, s, (3 * n) * D:(3 * n + 1) * D]), "mc", reads=[modv.b], writes=[sh.b],
                allow_slow_non_contiguous=True)
            stt("dve", A.t[:], sc.t[:], 1.0, gcol.t[:], ALU.add, ALU.mult, [sc.b, gcol.b], [A.b])
            res.append((A, sh))
        return res

    class Prep:
        def __init__(self, cols, trpair):
            self.cols = cols
            self.trpair = trpair
            self.xin = []
            self._offs = []
            for _ in range(4):
                tl, off = sba.tile([128, D], F32)
                self.xin.append(tl)
                self._offs.append(off)
            self.xn = []
            self._xnoffs = []
            for _ in range(2):
                tl, off = sba.tile([128, D], F32)
                self.xn.append(tl)
                self._xnoffs.append(off)
            self.hT, _ = sba.tile([128, 8, 512], BF16)
            self.ss, _ = sba.tile([128, 4], F32)
            self.rs, _ = sba.tile([128, 4], F32)

        def load(self, stage, blk, i):
            ap, tl = tok_src(stage, blk[1][i])
            dma("sp", self.xin[i].t[:], ap, "xin%d" % i, reads=[tl.b] if tl else [], writes=[self.xin[i].b])

        def sumsq(self, blk, i):
            act(self.xn[i % 2].t[:], self.xin[i].t[:], AF.Square, [self.xin[i].b], writes=[self.xn[i % 2].b],
                pwrites=[self.ss.b], accum_out=self.ss.t[:, i:i + 1])

        def rstd(self, blk):
            nt = len(blk[1])
            act(self.rs.t[:, 0:nt], self.ss.t[:, 0:nt], AF.Sqrt, [self.ss.b, epsc.b], writes=[self.rs.b], scale=1.0 / D, bias=epsc.t[:])
            recip(self.rs.t[:, 0:nt], self.rs.t[:, 0:nt], [self.rs.b], writes=[self.rs.b])

        def transpose(self, blk, i):
            s = 0 if blk[0] == "x" else 1
            A, sh = self.cols[s]
            xn = self.xn[i % 2]
            ts("dve", xn.t[:], self.xin[i].t[:], self.rs.t[:, i:i + 1], None, ALU.mult, None, [self.xin[i].b, self.rs.b], [xn.b])
            pr = self.trpair
            bb = [BK[2 * pr].b, BK[2 * pr + 1].b]
            for k in range(8):
                P.add("pe", lambda e, k=k: e.transpose(out=pair_ap(pr, 0, 128, k * 128, (k + 1) * 128), in_=xn.t[:, k * 128:(k + 1) * 128],
                                                     identity=ident.t[:]),
                      [xn.b, ident.b], **({"writes": [bb[0]]} if k == 0 else ({"writes": [bb[1]]} if k == 4 else {"pwrites": [bb[k // 4]]})))
            for k in range(8):
                o = self.hT.t[:, k, i * 128:(i + 1) * 128]
                src = pair_ap(pr, 0, 128, k * 128, (k + 1) * 128)
                if k % 2 == 0:
                    act(o, src, AF.Identity, [bb[k // 4], A.b, sh.b], pwrites=[self.hT.b], scale=A.t[:, k:k + 1], bias=sh.t[:, k:k + 1])
                else:
                    ts("dve", o, src, A.t[:, k:k + 1], sh.t[:, k:k + 1], ALU.mult, ALU.add, [bb[k // 4], A.b, sh.b], pwrites=[self.hT.b])

        def full(self, stage, blk):
            for i in range(len(blk[1])):
                self.load(stage, blk, i)
            for i in range(len(blk[1])):
                self.sumsq(blk, i)
            self.rstd(blk)
            for i in range(len(blk[1])):
                self.transpose(blk, i)

    def load_w_cast(dst, k, src_rows_ap, key, ncols):
        c = 0
        while c < ncols:
            n = min(1024, ncols - c)
            dma("pool", dst.t[:, k, c:c + n], src_rows_ap[:, c:c + n], key, pwrites=[dst.b])
            c += n

    def residual_store(stage_src, stage_dst, blk, i, bks, gate_bc, xres, ytmp, final=None):
        t = blk[1][i]
        xr = xres[i % 2]
        ap, tl = tok_src(stage_src, t)
        dma("sp", xr.t[:], ap, "xres%d" % (i % 2), reads=[tl.b] if tl else [], writes=[xr.b])
        for j in range(2):
            yt = ytmp[j]
            tt("dve", yt.t[:], bks[j].ap(), gate_bc.t[:, j * 512:(j + 1) * 512], ALU.mult, [bks[j].b, gate_bc.b], [yt.b])
            tt("pool", xr.t[:, j * 512:(j + 1) * 512], xr.t[:, j * 512:(j + 1) * 512], yt.t[:], ALU.add, [yt.b], pwrites=[xr.b])
        if final is None:
            dst = XS[stage_dst]
            dma("pool", dst.t[t * 128:(t + 1) * 128, :], xr.t[:], "xst%d" % (i % 2), reads=[xr.b], pwrites=[dst.b])
        else:
            fn_bc, fss, frs, fjunk = final
            act(fjunk.t[:], xr.t[:], AF.Square, [xr.b], writes=[fjunk.b], pwrites=[fss.b], accum_out=fss.t[:, 0:1])
            act(frs.t[:], fss.t[:], AF.Sqrt, [fss.b, epsc.b], writes=[frs.b], scale=1.0 / D, bias=epsc.t[:])
            recip(frs.t[:], frs.t[:], [frs.b], writes=[frs.b])
            stt("dve", xr.t[:], xr.t[:], frs.t[:, 0:1], fn_bc.t[:], ALU.mult, ALU.mult, [frs.b, fn_bc.b], writes=[xr.b])
            dma("pool", out.t[t * 128:(t + 1) * 128, :], xr.t[:], "xst%d" % (i % 2), reads=[xr.b], pwrites=[out.b])

    def load_gate_bc(gate_bc, l, n, s, half=False):
        src = modv.t[l, s, (3 * n + 2) * D:(3 * n + 3) * D].partition_broadcast(128)
        dma("sp", gate_bc.t[:], src, "gbc", reads=[modv.b], writes=[gate_bc.b])
        if half:
            P.add("pool", lambda e: e.tensor_scalar(out=gate_bc.t[:], in0=gate_bc.t[:], scalar1=0.5, scalar2=None, op0=ALU.mult),
                  [], writes=[gate_bc.b])

    def pass_ffn(l, which, stage_src, stage_dst, with_ctx, final):
        new_pass()
        n = 0 if which == 1 else 2
        w1b, _ = sba.tile([128, 8, DFF], BF16)
        w3b = TL(sba.tile([128, 8, DFF], BF16)[0].t, w1b.b)
        w2b = TL(sba.tile([128, NF, D], BF16)[0].t, w1b.b)
        W1 = ffn_w["ffn%d_w1" % which][l]
        W3 = ffn_w["ffn%d_w3" % which][l]
        W2 = ffn_w["ffn%d_w2" % which][l]
        wbufs = [Buf() for _ in range(4)]
        for cb in range(3):
            cc0 = cb * 1024
            cn = min(1024, DFF - cc0)
            for k in range(8):
                dma("pool", w1b.t[:, k, cc0:cc0 + cn], W1[k * 128:(k + 1) * 128, cc0:cc0 + cn], "wl%d" % cb, pwrites=[wbufs[cb]])
                dma("pool", w3b.t[:, k, cc0:cc0 + cn], W3[k * 128:(k + 1) * 128, cc0:cc0 + cn], "wl%d" % cb, pwrites=[wbufs[cb]])
        for f in range(NF):
            dma("pool", w2b.t[:, f, :], W2[f * 128:(f + 1) * 128, :], "wl3", pwrites=[wbufs[3]])
        cols = load_modcols(l, n)
        prep = Prep(cols, 2)
        g, _ = sba.tile([128, NF, 512], BF16)
        gate_bc, _ = sba.tile([128, D], F32)
        xres = [sba.tile([128, D], F32)[0] for _ in range(2)]
        if final:
            y0 = sba.tile([128, 512], F32)[0]
            ytmp = [y0, y0]
            s0 = sba.tile([128, 512], BF16)[0]
            ssb = [s0, s0]
        else:
            ytmp = [sba.tile([128, 512], F32)[0] for _ in range(2)]
            ssb = [sba.tile([128, 512], BF16)[0] for _ in range(2)]
        fin = None
        if final:
            fn_bc, _ = sba.tile([128, D], F32)
            dma("sp", fn_bc.t[:], final_norm.partition_broadcast(128), "gbc", writes=[fn_bc.b])
            fss, _ = sba.tile([128, 1], F32)
            frs, _ = sba.tile([128, 1], F32)
            fjunk = TL(nc.alloc_sbuf_tensor_at("fjunk", [128, D], BF16, offset=prep._xnoffs[1]), prep.xn[1].b)
            fin = (fn_bc, fss, frs, fjunk)
        blocks = ([c_block] if with_ctx else []) + x_blocks
        prep.full(stage_src, blocks[0])
        cur_stream = None
        for bi, blk in enumerate(blocks):
            N = len(blk[1]) * 128
            nxt = blocks[bi + 1] if bi + 1 < len(blocks) else None
            if blk[0] != cur_stream:
                cur_stream = blk[0]
                load_gate_bc(gate_bc, l, n, 0 if cur_stream == "x" else 1, half=True)
            for f in range(NF):
                b1 = BK[f % 2]
                b3 = BK[2 + f % 2]
                wbf = wbufs[f // 8]
                for k in range(8):
                    mm(b1.ap(c1=N), w1b.t[:, k, f * 128:(f + 1) * 128], prep.hT.t[:, k, 0:N], k == 0, k == 7, [wbf, prep.hT.b], b1.b)
                for k in range(8):
                    mm(b3.ap(c1=N), w3b.t[:, k, f * 128:(f + 1) * 128], prep.hT.t[:, k, 0:N], k == 0, k == 7, [wbf, prep.hT.b], b3.b)
                s = ssb[f % 2]
                act(s.t[:, 0:N], b1.ap(c1=N), AF.Silu, [b1.b], [s.b])
                tt("dve", g.t[:, f, 0:N], s.t[:, 0:N], b3.ap(c1=N), ALU.mult, [s.b, b3.b], pwrites=[g.b])
                if nxt is not None:
                    nn = len(nxt[1])
                    if f < nn:
                        prep.load(stage_src, nxt, f)
                    if 4 <= f < 4 + nn:
                        prep.sumsq(nxt, f - 4)
                    if f == 9:
                        prep.rstd(nxt)
            if nxt is not None:
                for i in range(len(nxt[1])):
                    prep.transpose(nxt, i)
            for i in range(len(blk[1])):
                bks = [BK[6], BK[7]]
                for j in range(2):
                    for f in range(NF):
                        mm(bks[j].ap(), g.t[:, f, i * 128:(i + 1) * 128], w2b.t[:, f, j * 512:(j + 1) * 512], f == 0, f == NF - 1,
                           [g.b, wbufs[3]], bks[j].b)
                residual_store(stage_src, stage_dst, blk, i, bks, gate_bc, xres, ytmp, fin if blk[0] == "x" else None)

    def pass_proj(l, stage_src, ctx_out):
        new_pass()
        sc = SC[l]
        winb, _ = sba.tile([128, 8, PROJ], BF16)
        winB = Buf()
        for (ca, cbn, key, buf) in ((0, O_GATE, "wl0", winb.b), (O_GATE, PROJ, "wl1", winB)):
            for k in range(8):
                c = ca
                while c < cbn:
                    n = min(1024, cbn - c)
                    dma("pool", winb.t[:, k, c:c + n], w_in[l][k * 128:(k + 1) * 128, c:c + n], key, pwrites=[buf])
                    c += n
        wqb = TL(sba.tile([128, 3, 768], BF16)[0].t, winb.b)
        wkvb = TL(sba.tile([128, 2, 1024], BF16)[0].t, winb.b)
        perms = {}
        for nm in ("k_p16", "k_p32", "k_p64"):
            pt = TL(sba.tile([128, 128], BF16)[0].t, winb.b)
            dma("pool", pt.t[:], k_perm[nm], "wl", pwrites=[winb.b])
            perms[nm] = pt
        cols = load_modcols(l, 1)
        qg, _ = sba.tile([128, 3], F32)
        kg, _ = sba.tile([128, 2], F32)
        dma("sp", qg.t[:], mla_q_norm[l].rearrange("(k p) -> p k", p=128), "mc", writes=[qg.b], allow_slow_non_contiguous=True)
        dma("sp", kg.t[:], mla_kv_norm[l].rearrange("(k p) -> p k", p=128), "mc", writes=[kg.b], allow_slow_non_contiguous=True)
        prep = Prep(cols, 2)
        stq = TL(nc.alloc_sbuf_tensor_at("stq%d" % l, [128, 3, 768], F32, offset=prep._offs[0]))
        stkv = TL(nc.alloc_sbuf_tensor_at("stkv%d" % l, [128, 2, 1024], F32, offset=prep._xnoffs[0]))
        qbufs = [prep.xin[0].b, prep.xin[1].b, prep.xin[2].b]
        kvbufs = [prep.xn[0].b, prep.xn[1].b]
        for k in range(3):
            dma("sp", stq.t[:, k, :], mla_w_qb[l][k * 128:(k + 1) * 128, :], "xin0", pwrites=qbufs)
        for k in range(2):
            dma("sp", stkv.t[:, k, :], mla_w_kvb[l][k * 128:(k + 1) * 128, :], "xin1", pwrites=kvbufs)
        qsc = (64 + 32) ** -0.5
        for k in range(3):
            ts("dve", wqb.t[:, k, :], stq.t[:, k, :], qg.t[:, k:k + 1], qsc, ALU.mult, ALU.mult, qbufs + [qg.b], pwrites=[winb.b])
        for k in range(2):
            ts("dve", wkvb.t[:, k, :], stkv.t[:, k, :], kg.t[:, k:k + 1], None, ALU.mult, None, kvbufs + [kg.b], pwrites=[winb.b])
        for (o, w) in ((O_DQ, 512), (O_RK, 256)):
            P.add("pool", lambda e, o=o, w=w: e.tensor_scalar(out=winb.t[:, :, o:o + w], in0=winb.t[:, :, o:o + w], scalar1=0.125, scalar2=None,
                                                          op0=ALU.mult), [], pwrites=[winb.b])
        W = winb.b
        latq, _ = sba.tile([128, 3, 512], BF16)
        latkv, _ = sba.tile([128, 2, 512], BF16)
        sqt = [sba.tile([128, 512], BF16)[0] for _ in range(2)]
        rq_rep, _ = sba.tile([128, 512], F32)
        rkv_rep, _ = sba.tile([128, 512], F32)
        rkv_tm, _ = sba.tile([128, 4], F32)
        tabbuf = Buf()
        tabs = {nm: TL(sba.tile([128, 512], F32)[0].t, tabbuf) for nm in k_tab}
        asb = [sba.tile([128, 512], BF16)[0] for _ in range(2)]
        m1, _ = sba.tile([128, 512], F32)
        m2, _ = sba.tile([128, 512], F32)
        ost = [sba.tile([128, 512], BF16)[0] for _ in range(3)]
        vst = [sba.tile([128, 4, 512], BF16)[0] for _ in range(2)]
        kts, _ = sba.tile([128, 4, 256], BF16)
        rkf = [sba.tile([128, 512], BF16)[0] for _ in range(2)]
        cnt = {"o": 0, "a": 0, "bk": 0, "v": 0}

        pending = []

        def flush():
            while pending:
                pending.pop(0)()

        def nbk():
            cnt["bk"] += 1
            return BK[cnt["bk"] % int("6")]

        def nost():
            cnt["o"] += 1
            return ost[cnt["o"] % 3], "ost%d" % (cnt["o"] % 3)

        def fm_group(lhs_fn, nk, M, rhs_fn, N, rb):
            bk = nbk()
            for k in range(nk):
                mm(bk.ap(0, M, 0, N), lhs_fn(k), rhs_fn(k), k == 0, k == nk - 1, rb, bk.b)
            if "1" == "1":
                flush()
            return bk

        blocks = [c_block] + x_blocks
        prep.full(stage_src, blocks[0])
        for bi, blk in enumerate(blocks):
            isx = blk[0] == "x"
            nt = len(blk[1])
            N = nt * 128
            c0 = blk[1][0] * 128
            x0 = c0
            hT = prep.hT
            if isx:
                for nm in k_tab:
                    dma("sp", tabs[nm].t[:, 0:N], k_tab[nm][:, x0:x0 + N], "tab", pwrites=[tabbuf])
            nxt = blocks[bi + 1] if bi + 1 < len(blocks) else None
            EARLY = "1" == "1"
            if nxt is not None and EARLY:
                for i in range(len(nxt[1])):
                    prep.load(stage_src, nxt, i)

            def win_group(o, M=128):
                return fm_group(lambda k: winb.t[:, k, o:o + M], 8, M, lambda k: hT.t[:, k, 0:N], N, [W if o < O_GATE else winB, hT.b])

            def rope_out(bk, M, permname, cname, sname, dst_list, p0=0, otile=None):
                if otile is None:
                    o, okey = nost()
                else:
                    o, okey = otile
                if isx:
                    a = asb[cnt["a"] % 2]
                    cnt["a"] += 1
                    cp("act", a.t[p0:p0 + M, 0:N], bk.ap(p0, p0 + M, 0, N), [bk.b], [a.b])

                    def tail():
                        pm = perms[permname]
                        b2 = nbk()
                        mm(b2.ap(p0, p0 + M, 0, N), pm.t[p0:p0 + M, p0:p0 + M], a.t[p0:p0 + M, 0:N], True, True, [W, a.b], b2.b)
                        tt("dve", m1.t[p0:p0 + M, 0:N], a.t[p0:p0 + M, 0:N], tabs[cname].t[p0:p0 + M, 0:N], ALU.mult, [a.b, tabs[cname].b], [m1.b])
                        tt("dve", m2.t[p0:p0 + M, 0:N], b2.ap(p0, p0 + M, 0, N), tabs[sname].t[p0:p0 + M, 0:N], ALU.mult, [b2.b, tabs[sname].b], [m2.b])
                        tt("dve", o.t[p0:p0 + M, 0:N], m1.t[p0:p0 + M, 0:N], m2.t[p0:p0 + M, 0:N], ALU.add, [m1.b, m2.b], [o.b])
                        for (dst_tl, dst_ap, r0, r1) in dst_list:
                            dma("pool", dst_ap, o.t[r0:r1, 0:N], okey, reads=[o.b], pwrites=[dst_tl.b])
                    pending.append(tail)
                    if "1" != "1":
                        flush()
                else:
                    cp("act", o.t[p0:p0 + M, 0:N], bk.ap(p0, p0 + M, 0, N), [bk.b], [o.b])
                    for (dst_tl, dst_ap, r0, r1) in dst_list:
                        dma("pool", dst_ap, o.t[r0:r1, 0:N], okey, reads=[o.b], pwrites=[dst_tl.b])

            for (lat, o_lat, nch, rrep, dim) in ((latq, O_QLAT, 3, rq_rep, 384), (latkv, O_KVLAT, 2, rkv_rep, 256)):
                ssb = BK[6]
                for c in range(nch):
                    bk = win_group(o_lat + c * 128)
                    cp("act", lat.t[:, c, 0:N], bk.ap(c1=N), [bk.b], pwrites=[lat.b])
                    sq = sqt[c % 2]
                    act(sq.t[:, 0:N], bk.ap(c1=N), AF.Square, [bk.b], [sq.b])
                    mm(ssb.ap(c1=N), onesb.t[:], sq.t[:, 0:N], c == 0, c == nch - 1, [onesb.b, sq.b], ssb.b)
                    if lat is latkv:
                        for i in range(nt):
                            mmp(BK[7].ap(0, 128, i, i + 1), sq.t[:, i * 128:(i + 1) * 128], onesb.t[:, 0:1], c == 0 and i == 0,
                                c == nch - 1 and i == nt - 1, [onesb.b, sq.b], BK[7].b)
                act(rrep.t[:, 0:N], ssb.ap(c1=N), AF.Sqrt, [ssb.b, epsc.b], [rrep.b], scale=1.0 / dim, bias=epsc.t[:])
                recip(rrep.t[:, 0:N], rrep.t[:, 0:N], [rrep.b], writes=[rrep.b])
            act(rkv_tm.t[:, 0:nt], BK[7].ap(0, 128, 0, nt), AF.Sqrt, [BK[7].b, epsc.b], [rkv_tm.b], scale=1.0 / 256, bias=epsc.t[:])
            recip(rkv_tm.t[:, 0:nt], rkv_tm.t[:, 0:nt], [rkv_tm.b], writes=[rkv_tm.b])

            bk = win_group(O_KROPE, 32)
            rope_out(bk, 32, "k_p16", "k_cm", "k_sm", [(sc["KMr"], sc["KMr"].t[:, c0:c0 + N], 0, 32)])
            for c in range(4):
                bk = win_group(O_DQ + c * 128)
                rope_out(bk, 128, "k_p32", "k_cd", "k_sd", [(sc["QD"], sc["QD"].t[c * 128:(c + 1) * 128, c0:c0 + N], 0, 128)])
            for c in range(4):
                bk = win_group(O_DK + c * 128)
                rope_out(bk, 128, "k_p32", "k_cd", "k_sd", [(sc["KD"], sc["KD"].t[c * 128:(c + 1) * 128, c0:c0 + N], 0, 128)])
            for c in range(2):
                bk = win_group(O_RQ + c * 128)
                rope_out(bk, 128, "k_p64", "k_cr", "k_sr", [(sc["RQ"], sc["RQ"].t[c * 128:(c + 1) * 128, c0:c0 + N], 0, 128)])
            for c in range(2):
                bk = win_group(O_RK + c * 128)
                rope_out(bk, 128, "k_p64", "k_cr", "k_sr", [(sc["RK"], sc["RK"].t[c * 128:(c + 1) * 128, c0:c0 + N], 0, 128)],
                         otile=(rkf[c], "rkf%d" % c))
            if isx or ctx_out:
                for c in range(4):
                    bk = win_group(O_RG + c * 128)
                    o, okey = nost()
                    act(o.t[:, 0:N], bk.ap(c1=N), AF.Silu, [bk.b], [o.b])
                    dma("pool", sc["RG"].t[c * 128:(c + 1) * 128, c0:c0 + N], o.t[:, 0:N], okey, reads=[o.b], pwrites=[sc["RG"].b])
                for c in range(24):
                    bk = win_group(O_GATE + c * 128)
                    o, okey = nost()
                    act(o.t[:, 0:N], bk.ap(c1=N), AF.Sigmoid, [bk.b], [o.b])
                    dma("pool", sc["GATE"].t[c * 128:(c + 1) * 128, c0:c0 + N], o.t[:, 0:N], okey, reads=[o.b], pwrites=[sc["GATE"].b])
            flush()
            tb = BK[4]
            tbv = PA[2].t[:, 0:512].bitcast(BF16)
            first = True
            for i in range(nt):
                for c in range(2):
                    P.add("pe", lambda e, i=i, c=c: e.transpose(out=tbv[:, (i * 2 + c) * 128:(i * 2 + c + 1) * 128],
                                                           in_=rkf[c].t[:, i * 128:(i + 1) * 128], identity=identb.t[:]),
                          [rkf[c].b, identb.b], **({"writes": [tb.b]} if first else {"pwrites": [tb.b]}))
                    first = False
            cp("dve", kts.t[:, 0:nt, :], tbv[:, 0:nt * 256].rearrange("p (i c) -> p i c", c=256), [tb.b], [kts.b])
            dma("pool", sc["RKt"].t[c0:c0 + N, :].rearrange("(i p) c -> p i c", p=128), kts.t[:, 0:nt, :], "kts", reads=[kts.b],
                pwrites=[sc["RKt"].b])
            if nxt is not None and EARLY:
                for i in range(len(nxt[1])):
                    prep.sumsq(nxt, i)
                prep.rstd(nxt)
            for (o_v, dst) in ((O_DV, sc["VD"]), (O_RV, sc["RV"])):
                v = vst[cnt["v"] % 2]
                vkey = "vst%d" % (cnt["v"] % 2)
                cnt["v"] += 1
                for i in range(nt):
                    bk = nbk()
                    for k in range(8):
                        mm(bk.ap(), hT.t[:, k, i * 128:(i + 1) * 128], winb.t[:, k, o_v:o_v + 512], k == 0, k == 7, [W, hT.b], bk.b)
                    cp("act" if i % 2 == 0 else "dve", v.t[:, i, :], bk.ap(), [bk.b], pwrites=[v.b])
                dma("pool", dst.t[c0:c0 + N, :].rearrange("(i p) c -> p i c", p=128), v.t[:, 0:nt, :], vkey, reads=[v.b], pwrites=[dst.b])
            if nxt is not None and EARLY:
                for i in range(len(nxt[1])):
                    prep.transpose(nxt, i)
            for h in range(8):
                bk = fm_group(lambda k: wqb.t[:, k, h * 96:h * 96 + 96], 3, 96, lambda k: latq.t[:, k, 0:N], N, [W, latq.b])
                o, okey = nost()
                if isx:
                    a = asb[cnt["a"] % 2]
                    cnt["a"] += 1
                    cp("act", a.t[0:96, 0:N], bk.ap(0, 96, 0, N), [bk.b], [a.b])
                    MLT = "0" == "1"
                    if MLT:
                        tt("dve", o.t[0:64, 0:N], bk.ap(0, 64, 0, N), rq_rep.t[0:64, 0:N], ALU.mult, [bk.b, rq_rep.b], [o.b])

                    def tail(a=a, o=o, okey=okey, h=h, bk=bk, MLT=MLT):
                        b2 = nbk()
                        mm(b2.ap(0, 96, 0, N), perms["k_p16"].t[0:96, 0:96], a.t[0:96, 0:N], True, True, [W, a.b], b2.b)
                        tt("dve", m1.t[64:96, 0:N], a.t[64:96, 0:N], tabs["k_cm"].t[64:96, 0:N], ALU.mult, [a.b, tabs["k_cm"].b], [m1.b])
                        tt("dve", m2.t[64:96, 0:N], b2.ap(64, 96, 0, N), tabs["k_sm"].t[64:96, 0:N], ALU.mult, [b2.b, tabs["k_sm"].b], [m2.b])
                        tt("dve", m1.t[64:96, 0:N], m1.t[64:96, 0:N], m2.t[64:96, 0:N], ALU.add, [m2.b], writes=[m1.b])
                        if not MLT:
                            tt("dve", o.t[0:64, 0:N], bk.ap(0, 64, 0, N), rq_rep.t[0:64, 0:N], ALU.mult, [bk.b, rq_rep.b], [o.b])
                        tt("dve", o.t[64:96, 0:N], m1.t[64:96, 0:N], rq_rep.t[64:96, 0:N], ALU.mult, [m1.b, rq_rep.b], pwrites=[o.b])
                        dma("pool", sc["QM"].t[h, :, c0:c0 + N], o.t[0:96, 0:N], okey, reads=[o.b], pwrites=[sc["QM"].b])
                    pending.append(tail)
                    if "1" != "1":
                        flush()
                else:
                    tt("dve", o.t[0:96, 0:N], bk.ap(0, 96, 0, N), rq_rep.t[0:96, 0:N], ALU.mult, [bk.b, rq_rep.b], [o.b])
                    dma("pool", sc["QM"].t[h, :, c0:c0 + N], o.t[0:96, 0:N], okey, reads=[o.b], pwrites=[sc["QM"].b])
            for h in range(8):
                bk = fm_group(lambda k: wkvb.t[:, k, h * 128:h * 128 + 64], 2, 64, lambda k: latkv.t[:, k, 0:N], N, [W, latkv.b])
                o, okey = nost()
                tt("dve", o.t[0:64, 0:N], bk.ap(0, 64, 0, N), rkv_rep.t[0:64, 0:N], ALU.mult, [bk.b, rkv_rep.b], [o.b])
                dma("pool", sc["KMn"].t[h, :, c0:c0 + N], o.t[0:64, 0:N], okey, reads=[o.b], pwrites=[sc["KMn"].b])
            v = vst[cnt["v"] % 2]
            vkey = "vst%d" % (cnt["v"] % 2)
            cnt["v"] += 1
            wv = wkvb.t[:].rearrange("p k (h c) -> p k h c", c=128)
            for i in range(nt):
                bk = nbk()
                for k in range(2):
                    mm(bk.ap().rearrange("p (h c) -> p h c", c=64), latkv.t[:, k, i * 128:(i + 1) * 128], wv[:, k, :, 64:128], k == 0, k == 1,
                       [W, latkv.b], bk.b)
                ts("dve", v.t[:, i, :], bk.ap(), rkv_tm.t[:, i:i + 1], None, ALU.mult, None, [bk.b, rkv_tm.b], pwrites=[v.b])
            dma("pool", sc["VM"].t[c0:c0 + N, :].rearrange("(i p) c -> p i c", p=128), v.t[:, 0:nt, :], vkey, reads=[v.b], pwrites=[sc["VM"].b])
            flush()
            if nxt is not None and not EARLY:
                prep.full(stage_src, nxt)

    def pass_attn(l, ctx_out):
        new_pass()
        sc = SC[l]
        lam_init = 0.8 - 0.6 * math.exp(-0.3 * l)
        QW = T if ctx_out else S
        KT = [sba.tile([128, T], BF16)[0] for _ in range(2)]
        QT = [sba.tile([128, T], BF16)[0] for _ in range(2)]
        VT = [sba.tile([128, NKT, 128], BF16)[0] for _ in range(2)]
        pt = [sba.tile([128, 1024], BF16)[0] for _ in range(4)]
        accD, _ = sba.tile([128, 1024], F32)
        accP, _ = sba.tile([128, 1024], F32)
        rc, _ = sba.tile([128, 512], F32)
        on = [sba.tile([128, 512], F32)[0] for _ in range(2)]
        dsq, _ = sba.tile([128, 512], F32)
        drs, _ = sba.tile([128, 512], F32)
        ob = [sba.tile([128, 512], BF16)[0] for _ in range(2)]
        dl, _ = sba.tile([128, 4, 64], F32)
        dma("sp", dl.t[:], diff_lambda[l].rearrange("a b -> (a b)").partition_broadcast(128).rearrange("p (a b) -> p a b", b=64),
            "mc", writes=[dl.b])
        dpr, _ = sba.tile([128, 2, 64], F32)
        tt("dve", dpr.t[:, 0, :], dl.t[:, 0, :], dl.t[:, 1, :], ALU.mult, [dl.b], pwrites=[dpr.b])
        tt("dve", dpr.t[:, 1, :], dl.t[:, 2, :], dl.t[:, 3, :], ALU.mult, [dl.b], pwrites=[dpr.b])
        dsum, _ = sba.tile([128, 2], F32)
        P.add("dve", lambda e: e.reduce_sum(out=dsum.t[:], in_=dpr.t[:], axis=mybir.AxisListType.X), [dpr.b], [dsum.b])
        dex, _ = sba.tile([128, 2], F32)
        act(dex.t[:], dsum.t[:], AF.Exp, [dsum.b], [dex.b])
        nlam, _ = sba.tile([128, 1], F32)
        stt("dve", nlam.t[:], dex.t[:, 1:2], -lam_init, dex.t[:, 0:1], ALU.add, ALU.subtract, [dex.b], [nlam.b])
        dng, _ = sba.tile([128, 1], F32)
        dma("sp", dng.t[:], diff_norm[l].rearrange("(p o) -> p o", o=1), "mc", writes=[dng.b], allow_slow_non_contiguous=True)
        ts("dve", dng.t[:], dng.t[:], 1.0 - lam_init, None, ALU.mult, None, [], writes=[dng.b])

        units = [("m", h) for h in range(8)] + [("d", h) for h in range(4)]
        grp = {"n": 0, "p": 0}
        retgen = ret_gen(l, ctx_out)

        def pull():
            pass

        def rstd_lnexp(out_ap, in_ap, dim, reads, wbuf):
            act(out_ap, in_ap, AF.Ln, list(reads) + [epsc.b], [wbuf], scale=1.0 / dim, bias=epsc.t[:])
            act(out_ap, out_ap, AF.Exp, [], writes=[wbuf], scale=-0.5)

        POOLACC = True

        def diff_block(K, Q, V, q0, N, kts):
            O1, O2, S1 = BK[6], BK[7], BK[4]
            P.add("dve", lambda e: e.memset(accD.t[:, 0:512], 0.0), [], writes=[accD.b])

            def issue_qk(kt):
                gi = grp["n"] % 2
                grp["n"] += 1
                for j in range(2):
                    bk = BK[2 * gi + j]
                    mm(bk.ap(c1=N), K.t[j * 64:(j + 1) * 64, kt * 128:(kt + 1) * 128], Q.t[j * 64:(j + 1) * 64, q0:q0 + N], True, True,
                       [K.b, Q.b], bk.b)
                return gi

            def issue_rest(kt, gi, idx, first, last):
                p = pt[grp["p"] % 4]
                grp["p"] += 1
                rb = [BK[2 * gi].b, BK[2 * gi + 1].b]
                if N == 512:
                    act(p.t[:, :], pair_ap(gi), AF.Exp, rb, [p.b])
                else:
                    act(p.t[:, 0:N], pair_ap(gi, 0, 128, 0, N), AF.Exp, [rb[0]], [p.b])
                    act(p.t[:, 512:512 + N], pair_ap(gi, 0, 128, 512, 512 + N), AF.Exp, [rb[1]], pwrites=[p.b])
                mm(O1.ap(c1=N), V.t[:, kt, :], p.t[:, 0:N], first, last, [V.b, p.b], O1.b)
                mm(S1.ap(c1=N), onesb.t[:], p.t[:, 0:N], first, last, [onesb.b, p.b], S1.b)
                mm(O2.ap(c1=N), V.t[:, kt, :], p.t[:, 512:512 + N], first, last, [V.b, p.b], O2.b)
                tt("dve", accD.t[:, 0:N], accD.t[:, 0:N], p.t[:, 512:512 + N], ALU.add, [p.b], writes=[accD.b])

            q = []
            for idx, kt in enumerate(kts):
                gi = issue_qk(kt)
                q.append((kt, gi, idx, idx == 0, idx == len(kts) - 1))
                if len(q) > 1:
                    issue_rest(*q.pop(0))
            while q:
                issue_rest(*q.pop(0))
            S2 = BK[5]
            mm(S2.ap(c1=N), onesf.t[:], accD.t[:, 0:N], True, True, [onesf.b, accD.b], S2.b)
            for j, (O, Sb) in enumerate(((O1, S1), (O2, S2))):
                recip(rc.t[:, 0:N], Sb.ap(c1=N), [Sb.b], [rc.b])
                tt("dve", on[j].t[:, 0:N], O.ap(c1=N), rc.t[:, 0:N], ALU.mult, [O.b, rc.b], [on[j].b])

        def load_unit(ui):
            kind, h = units[ui]
            s = ui % 2
            K, Q, V = KT[s], QT[s], VT[s]
            if kind == "m":
                dma("sp", K.t[0:64, :], sc["KMn"].t[h], "ak%d" % s, reads=[sc["KMn"].b], pwrites=[K.b])
                dma("sp", K.t[64:96, :], sc["KMr"].t[:, :], "ak%d" % s, reads=[sc["KMr"].b], pwrites=[K.b])
                dma("sp", Q.t[0:96, 0:QW], sc["QM"].t[h, :, 0:QW], "aq%d" % s, reads=[sc["QM"].b], writes=[Q.b])
                for t0 in range(0, NKT, 16):
                    t1 = min(NKT, t0 + 16)
                    dma("sp", V.t[:, t0:t1, 0:64], sc["VM"].t[t0 * 128:t1 * 128, h * 64:(h + 1) * 64].rearrange("(t p) c -> p t c", p=128),
                        "av%d" % s, reads=[sc["VM"].b], pwrites=[V.b])
                P.add("pool", lambda e: e.memset(V.t[:, :, 64:128], 1.0), [], pwrites=[V.b])
            else:
                dma("sp", K.t[:, :], sc["KD"].t[h * 128:(h + 1) * 128, :], "ak%d" % s, reads=[sc["KD"].b], writes=[K.b])
                dma("sp", Q.t[:, 0:QW], sc["QD"].t[h * 128:(h + 1) * 128, 0:QW], "aq%d" % s, reads=[sc["QD"].b], writes=[Q.b])
                for t0 in range(0, NKT, 16):
                    t1 = min(NKT, t0 + 16)
                    dma("sp", V.t[:, t0:t1, :], sc["VD"].t[t0 * 128:t1 * 128, h * 128:(h + 1) * 128].rearrange("(t p) c -> p t c", p=128),
                        "av%d" % s, reads=[sc["VD"].b], pwrites=[V.b])

        def softmax_block(K, Q, V, p0, p1, q0, N, kts, obk, sbk):
            groups = [kts[i:i + 2] for i in range(0, len(kts), 2)]
            pend = []

            def issue_qk(g):
                gi = grp["n"] % 3
                grp["n"] += 1
                for jj, kt in enumerate(g):
                    bk = BK[2 * gi + jj]
                    mm(bk.ap(c1=N), K.t[p0:p1, kt * 128:(kt + 1) * 128], Q.t[p0:p1, q0:q0 + N], True, True, [K.b, Q.b], bk.b)
                return gi

            def issue_rest(g, gi, first, last):
                p = pt[grp["p"] % 4]
                grp["p"] += 1
                ng = len(g)
                rb = [BK[2 * gi + jj].b for jj in range(ng)]
                if N == 512:
                    act(p.t[:, 0:ng * 512], pair_ap(gi, 0, 128, 0, ng * 512), AF.Exp, rb, [p.b])
                else:
                    for jj in range(ng):
                        act(p.t[:, jj * 512:jj * 512 + N], pair_ap(gi, 0, 128, jj * 512, jj * 512 + N), AF.Exp, [rb[jj]],
                            **({"writes": [p.b]} if jj == 0 else {"pwrites": [p.b]}))
                for jj, kt in enumerate(g):
                    st = first and jj == 0
                    sp_ = last and jj == ng - 1
                    mm(obk.ap(c1=N), V.t[:, kt, :], p.t[:, jj * 512:jj * 512 + N], st, sp_, [V.b, p.b], obk.b)
                    if sbk is not None:
                        mm(sbk.ap(c1=N), onesb.t[:], p.t[:, jj * 512:jj * 512 + N], st, sp_, [onesb.b, p.b], sbk.b)
                pull()

            q = []
            for gidx, g in enumerate(groups):
                gi = issue_qk(g)
                q.append((g, gi, gidx == 0, gidx == len(groups) - 1))
                if len(q) > 2:
                    issue_rest(*q.pop(0))
            while q:
                issue_rest(*q.pop(0))

        def qblocks():
            res = [(b * 512, 512, list(range(NKT))) for b in range(S // 512)]
            if ctx_out:
                res.append((S, 256, [NXT, NXT + 1]))
            return res

        load_unit(0)
        for ui, (kind, h) in enumerate(units):
            s = ui % 2
            K, Q, V = KT[s], QT[s], VT[s]
            if ui + 1 < len(units):
                load_unit(ui + 1)
            for (q0, N, kts) in qblocks():
                if kind == "m":
                    obk = BK[6]
                    softmax_block(K, Q, V, 0, 96, q0, N, kts, obk, None)
                    recip(rc.t[0:64, 0:N], obk.ap(64, 128, 0, N), [obk.b], [rc.b])
                    o = ob[0]
                    tt("dve", o.t[0:64, 0:N], obk.ap(0, 64, 0, N), rc.t[0:64, 0:N], ALU.mult, [obk.b, rc.b], [o.b])
                    dma("pool", sc["YM"].t[h * 64:(h + 1) * 64, q0:q0 + N], o.t[0:64, 0:N], "ob0", reads=[o.b], pwrites=[sc["YM"].b])
                else:
                    diff_block(K, Q, V, q0, N, kts)
                    stt("dve", on[0].t[:, 0:N], on[1].t[:, 0:N], nlam.t[:, 0:1], on[0].t[:, 0:N], ALU.mult, ALU.add, [on[1].b, nlam.b],
                        writes=[on[0].b])
                    tt("pool", dsq.t[:, 0:N], on[0].t[:, 0:N], on[0].t[:, 0:N], ALU.mult, [on[0].b], [dsq.b])
                    nb = BK[7]
                    mm(nb.ap(c1=N), onesf.t[:], dsq.t[:, 0:N], True, True, [onesf.b, dsq.b], nb.b)
                    rstd_lnexp(drs.t[:, 0:N], nb.ap(c1=N), 128, [nb.b], drs.b)
                    o = ob[1]
                    stt("dve", o.t[:, 0:N], on[0].t[:, 0:N], dng.t[:, 0:1], drs.t[:, 0:N], ALU.mult, ALU.mult, [on[0].b, dng.b, drs.b], [o.b])
                    dma("pool", sc["YD"].t[h * 128:(h + 1) * 128, q0:q0 + N], o.t[:, 0:N], "ob1", reads=[o.b], pwrites=[sc["YD"].b])
        for _ in retgen:
            pass

    def ret_gen(l, ctx_out):
        sc = SC[l]
        RPE = "dve"
        rd, _ = sba.tile([128, 8], F32)
        dma("sp", rd.t[:], ret_decay[l].rearrange("a b -> (a b)").partition_broadcast(128), "mc", writes=[rd.b])
        lg, _ = sba.tile([128, 8], F32)
        act(lg.t[:], rd.t[:], AF.Exp, [rd.b], [lg.b])
        ts("dve", lg.t[:], lg.t[:], -1.0, None, ALU.mult, None, [], writes=[lg.b])
        cdec, _ = sba.tile([128, 8], F32)
        act(cdec.t[:], lg.t[:], AF.Exp, [lg.b], [cdec.b], scale=128.0)
        r4, _ = sba.tile([128, 4, 128], F32)
        dma("sp", r4.t[:], k_ret4.rearrange("a p q -> p a q"), "mc", writes=[r4.b])
        qdc, _ = sba.tile([128, 2, 128], F32)
        dma("sp", qdc.t[:], k_qd.rearrange("a p q -> p a q"), "mc", writes=[qdc.b])
        kdc, _ = sba.tile([128, 2], F32)
        dma("sp", kdc.t[:], k_kd, "mc", writes=[kdc.b])
        maskT, _ = sba.tile([128, 2, 4, 128], F32)
        qdT, _ = sba.tile([128, 2, 4, 128], F32)
        kdT, _ = sba.tile([128, 2, 4], F32)
        for d in range(2):
            for h in range(4):
                i = d * 4 + h
                act(maskT.t[:, d, h, :], r4.t[:, 2 * d, :], AF.Exp, [r4.b, lg.b], pwrites=[maskT.b], scale=lg.t[:, i:i + 1])
                tt("dve", maskT.t[:, d, h, :], maskT.t[:, d, h, :], r4.t[:, 2 * d + 1, :], ALU.mult, [r4.b], pwrites=[maskT.b])
                act(qdT.t[:, d, h, :], qdc.t[:, d, :], AF.Exp, [qdc.b, lg.b], pwrites=[qdT.b], scale=lg.t[:, i:i + 1])
            act(kdT.t[:, d, :], lg.t[:, d * 4:d * 4 + 4], AF.Exp, [lg.b, kdc.b], pwrites=[kdT.b], scale=kdc.t[:, d:d + 1])
        rng_col, _ = sba.tile([128, 1], F32)
        dma("sp", rng_col.t[:], ret_norm[l].rearrange("(p o) -> p o", o=1), "mc", writes=[rng_col.b], allow_slow_non_contiguous=True)

        qf = [sba.tile([64, 4, 128], BF16)[0] for _ in range(2)]
        kf = [sba.tile([64, 4, 128], BF16)[0] for _ in range(2)]
        ktm = [sba.tile([128, 256], BF16)[0] for _ in range(2)]
        vtm = [sba.tile([128, 512], BF16)[0] for _ in range(2)]
        am, _ = sba.tile([128, 4, 128], BF16)
        kdm, _ = sba.tile([128, 4, 64], BF16)
        qdm, _ = sba.tile([64, 4, 128], BF16)
        Sf, _ = sba.tile([64, 4, 128], F32)
        Sb16, _ = sba.tile([64, 4, 128], BF16)
        osb = [sba.tile([128, 4, 128], F32)[0] for _ in range(2)]
        ofl = [sba.tile([128, 4, 128], F32)[0] for _ in range(2)]
        gsb = [sba.tile([128, 4, 128], BF16)[0] for _ in range(2)]
        sqs, _ = sba.tile([128, 512], F32)
        rrs, _ = sba.tile([128, 512], F32)
        yo = [sba.tile([128, 4, 128], BF16)[0] for _ in range(2)]

        yield
        fwd = [NXT, NXT + 1] + list(range(NXT))
        bwd = [NXT + 1, NXT] + list(range(NXT - 1, -1, -1))
        step = 0
        posts = []

        def flush_posts():
            while posts:
                posts.pop(0)()

        for d, order in ((0, fwd), (1, bwd)):
            P.add("dve", lambda e: e.memset(Sf.t[:], 0.0), [], writes=[Sf.b])
            P.add("dve", lambda e: e.memset(Sb16.t[:], 0.0), [], writes=[Sb16.b])
            for t in order:
                s = step % 2
                step += 1
                isx = t < NXT
                need_out = isx or ctx_out
                c0 = t * 128
                dma("sp", qf[s].t[:], sc["RQ"].t[:, c0:c0 + 128].rearrange("(h d) q -> d h q", d=64), "rq%d" % s, reads=[sc["RQ"].b],
                    writes=[qf[s].b])
                dma("sp", kf[s].t[:], sc["RK"].t[:, c0:c0 + 128].rearrange("(h d) q -> d h q", d=64), "rk%d" % s, reads=[sc["RK"].b],
                    writes=[kf[s].b])
                dma("sp", ktm[s].t[:], sc["RKt"].t[c0:c0 + 128, :], "rkt%d" % s, reads=[sc["RKt"].b], writes=[ktm[s].b])
                dma("sp", vtm[s].t[:], sc["RV"].t[c0:c0 + 128, :], "rv%d" % s, reads=[sc["RV"].b], writes=[vtm[s].b])
                obk = BK[1] if s == 0 else BK[3]
                if need_out and d == 1:
                    dma("sp", ofl[s].t[:], sc["OF"].t[:, t, :].rearrange("p (h q) -> p h q", q=128), "ofl%d" % s, reads=[sc["OF"].b],
                        writes=[ofl[s].b])
                    dma("sp", gsb[s].t[:], sc["RG"].t[:, c0:c0 + 128].rearrange("(h p) q -> p h q", p=128), "gsb%d" % s, reads=[sc["RG"].b],
                        writes=[gsb[s].b])
                if need_out:
                    ab = BK[0]
                    for h in range(4):
                        mmp(ab.ap(0, 128, h * 128, (h + 1) * 128), kf[s].t[:, h, :], qf[s].t[:, h, :], True, True, [kf[s].b, qf[s].b], ab.b)
                    tt("dve", am.t[:], ab.ap().rearrange("p (h q) -> p h q", q=128), maskT.t[:, d, :, :], ALU.mult, [ab.b, maskT.b], [am.b])
                    tt(RPE, qdm.t[:], qf[s].t[:], qdT.t[0:64, d, :, :], ALU.mult, [qf[s].b, qdT.b], [qdm.b])
                    for h in range(4):
                        mmp(obk.ap(0, 128, h * 128, (h + 1) * 128), vtm[s].t[:, h * 128:(h + 1) * 128], am.t[:, h, :], True, False,
                            [vtm[s].b, am.b], obk.b)
                        mmp(obk.ap(0, 128, h * 128, (h + 1) * 128), Sb16.t[:, h, :], qdm.t[:, h, :], False, True, [Sb16.b, qdm.b], obk.b)
                tt(RPE, kdm.t[:], ktm[s].t[:].rearrange("p (h d) -> p h d", d=64), kdT.t[:, d, :].unsqueeze(2).to_broadcast([128, 4, 64]),
                   ALU.mult, [ktm[s].b, kdT.b], [kdm.b])
                ub = BK[2]
                for h in range(4):
                    mmp(ub.ap(0, 64, h * 128, (h + 1) * 128), kdm.t[:, h, :], vtm[s].t[:, h * 128:(h + 1) * 128], True, True, [kdm.b, vtm[s].b], ub.b)
                for h in range(4):
                    i = d * 4 + h
                    stt("dve", Sf.t[:, h, :], Sf.t[:, h, :], cdec.t[0:64, i:i + 1], ub.ap(0, 64, h * 128, (h + 1) * 128), ALU.mult, ALU.add,
                        [ub.b, cdec.b], pwrites=[Sf.b])
                cp("act", Sb16.t[:], Sf.t[:], [Sf.b], [Sb16.b])
                flush_posts()
                if not need_out:
                    continue

                def post(s=s, t=t, c0=c0, d=d, obk=obk):
                    o = osb[s]
                    if d == 0:
                        cp("act", o.t[:], obk.ap().rearrange("p (h q) -> p h q", q=128), [obk.b], [o.b])
                        dma("pool", sc["OF"].t[:, t, :].rearrange("p (h q) -> p h q", q=128), o.t[:], "osb%d" % s, reads=[o.b],
                            pwrites=[sc["OF"].b])
                        return
                    of = ofl[s]
                    gs = gsb[s]
                    tt("dve", o.t[:], obk.ap().rearrange("p (h q) -> p h q", q=128), of.t[:], ALU.add, [obk.b, of.b], [o.b])
                    of2 = o.t[:].rearrange("p h q -> p (h q)")
                    tt(RPE, sqs.t[:], of2, of2, ALU.mult, [o.b], [sqs.b])
                    nb = BK[4]
                    mm(nb.ap(), onesf.t[:], sqs.t[:], True, True, [onesf.b, sqs.b], nb.b)
                    act(rrs.t[:], nb.ap(), AF.Ln, [nb.b, epsc.b], [rrs.b], scale=1.0 / 128, bias=epsc.t[:])
                    act(rrs.t[:], rrs.t[:], AF.Exp, [], writes=[rrs.b], scale=-0.5)
                    stt("dve", of2, of2, rng_col.t[:, 0:1], rrs.t[:], ALU.mult, ALU.mult, [rng_col.b, rrs.b], writes=[o.b])
                    y = yo[s]
                    tt(RPE, y.t[:], o.t[:], gs.t[:], ALU.mult, [o.b, gs.b], [y.b])
                    dma("pool", sc["YR"].t[:, c0:c0 + 128].rearrange("(h p) q -> p h q", p=128), y.t[:], "yo%d" % s, reads=[y.b],
                        pwrites=[sc["YR"].b])
                posts.append(post)
            flush_posts()
        yield

    def pass_merge(l, stage_src, stage_dst, ctx_out):
        new_pass()
        sc = SC[l]
        wbb, _ = sba.tile([128, 12, D], BF16)
        wob = TL(sba.tile([128, 8, D], BF16)[0].t, wbb.b)
        for i in range(3):
            for k in range(4):
                load_w_cast(wbb, i * 4 + k, w_branch[l, i][k * 128:(k + 1) * 128, :], "wl", D)
        for k in range(8):
            load_w_cast(wob, k, w_out[l][k * 128:(k + 1) * 128, :], "wl", D)
        ysb = [sba.tile([128, 12, 512], BF16)[0] for _ in range(2)]
        gsb = [sba.tile([128, 24, 512], BF16)[0] for _ in range(2)]
        yT, _ = sba.tile([128, 8, 512], BF16)
        mt = [sba.tile([128, 512], F32)[0] for _ in range(3)]
        gate_bc, _ = sba.tile([128, D], F32)
        xres = [sba.tile([128, D], F32)[0] for _ in range(2)]
        ytmp = [sba.tile([128, 512], F32)[0] for _ in range(2)]
        blocks = ([c_block] if ctx_out else []) + x_blocks

        def load_blk(bi):
            blk = blocks[bi]
            N = len(blk[1]) * 128
            c0 = blk[1][0] * 128
            s = bi % 2
            for i, nm in enumerate(("YM", "YD", "YR")):
                dma("sp", ysb[s].t[:, i * 4:(i + 1) * 4, 0:N], sc[nm].t[:, c0:c0 + N].rearrange("(k p) n -> p k n", p=128), "my%d" % s,
                    reads=[sc[nm].b], pwrites=[ysb[s].b])
            for i in range(3):
                dma("sp", gsb[s].t[:, i * 8:(i + 1) * 8, 0:N], sc["GATE"].t[i * 1024:(i + 1) * 1024, c0:c0 + N].rearrange("(k p) n -> p k n", p=128),
                    "mg%d" % s, reads=[sc["GATE"].b], pwrites=[gsb[s].b])

        load_blk(0)
        cur_stream = None
        for bi, blk in enumerate(blocks):
            N = len(blk[1]) * 128
            s = bi % 2
            if bi + 1 < len(blocks):
                load_blk(bi + 1)
            if blk[0] != cur_stream:
                cur_stream = blk[0]
                load_gate_bc(gate_bc, l, 1, 0 if cur_stream == "x" else 1)
            for oc in range(8):
                zb = [BK[(oc % 2) * 3 + i] for i in range(3)]
                for i in range(3):
                    for k in range(4):
                        mm(zb[i].ap(c1=N), wbb.t[:, i * 4 + k, oc * 128:(oc + 1) * 128], ysb[s].t[:, i * 4 + k, 0:N], k == 0, k == 3,
                           [wbb.b, ysb[s].b], zb[i].b)
                for i in range(3):
                    tt("dve", mt[i].t[:, 0:N], zb[i].ap(c1=N), gsb[s].t[:, i * 8 + oc, 0:N], ALU.mult, [zb[i].b, gsb[s].b], [mt[i].b])
                tt("pool", mt[0].t[:, 0:N], mt[0].t[:, 0:N], mt[1].t[:, 0:N], ALU.add, [mt[1].b], writes=[mt[0].b])
                tt("pool", yT.t[:, oc, 0:N], mt[0].t[:, 0:N], mt[2].t[:, 0:N], ALU.add, [mt[0].b, mt[2].b], pwrites=[yT.b])
            for i in range(len(blk[1])):
                bks = [BK[6], BK[7]]
                for j in range(2):
                    for k in range(8):
                        mm(bks[j].ap(), yT.t[:, k, i * 128:(i + 1) * 128], wob.t[:, k, j * 512:(j + 1) * 512], k == 0, k == 7, [yT.b, wbb.b], bks[j].b)
                residual_store(stage_src, stage_dst, blk, i, bks, gate_bc, xres, ytmp)

    pass_mod()
    stage = None
    for l in range(DEPTH):
        last = l == DEPTH - 1
        ctx_out = not last
        pass_ffn(l, 1, stage, (l, 1), True, False)
        pass_proj(l, (l, 1), ctx_out)
        pass_attn(l, ctx_out)
        pass_merge(l, (l, 1), (l, 2), ctx_out)
        pass_ffn(l, 2, (l, 2), None if last else (l, 3), ctx_out, last)
        stage = (l, 3)
    P.barrier()
    stats = P.emit()
    return nc, stats


_CACHE = {}


def _get(S, DEBUG=()):
    key = (S, tuple(DEBUG))
    if key not in _CACHE:
        _CACHE[key] = (build(S, DEBUG), host_consts(S))
    return _CACHE[key]


def kernel(**inputs):
    x = np.asarray(inputs["x"], np.float32)
    B, S, _ = x.shape
    (nc, _), hc = _get(S)
    shared = {k: np.ascontiguousarray(np.asarray(v, np.float32)) for k, v in inputs.items() if k not in ("x", "c", "ctx")}
    in_maps = []
    for b in range(B):
        m = dict(shared)
        m.update(hc)
        m["x"] = np.ascontiguousarray(x[b])
        m["c"] = np.ascontiguousarray(np.asarray(inputs["c"], np.float32)[b])
        m["ctx"] = np.ascontiguousarray(np.asarray(inputs["ctx"], np.float32)[b])
        in_maps.append(m)
    res = run_bass_kernel_spmd(nc, in_maps, core_ids=list(range(B)))
    return np.stack([np.asarray(r["out"], np.float32) for r in res.results], axis=0)
```

```python
import math
import contextlib
import numpy as np
import ml_dtypes
import concourse.bass as bass
import concourse.mybir as mybir
from concourse.bass_utils import run_bass_kernel_spmd

F32 = mybir.dt.float32
BF16 = mybir.dt.bfloat16
AF = mybir.ActivationFunctionType
ALU = mybir.AluOpType

D = 1024
CTX = 256
DFF = 2816
NF = DFF // 128
PROJ = 6816
EPS = 1e-6
O_QLAT, O_KVLAT, O_KROPE, O_DQ, O_DK, O_DV = 0, 384, 640, 672, 1184, 1696
O_RQ, O_RK, O_RV, O_RG, O_GATE = 2208, 2464, 2720, 3232, 3744
DEPTH = 2


class Buf:
    __slots__ = ("writers", "readers")

    def __init__(self):
        self.writers = {}
        self.readers = {}


class Op:
    __slots__ = ("eng", "fn", "deps", "key", "is_dma", "signal", "val")


class Prog:
    def __init__(self, nc):
        self.nc = nc
        self.ops = []
        self.last = {}

    def add(self, eng, fn, reads=(), writes=(), pwrites=(), dma=None, extra=None, serial=True, track=True):
        op = Op()
        op.eng = eng
        op.fn = fn
        op.is_dma = dma is not None
        op.key = ("d", dma) if dma is not None else ("e", eng)
        op.signal = False
        op.val = 0
        idx = len(self.ops)
        deps = {}

        def need(d):
            for k, i in d.items():
                if deps.get(k, -1) < i:
                    deps[k] = i

        for b in reads:
            need(b.writers)
        for b in writes:
            need(b.writers)
            need(b.readers)
        for b in pwrites:
            need(b.writers)
            need(b.readers)
        if extra:
            need(extra)
        if op.is_dma and serial and op.key in self.last:
            need({op.key: self.last[op.key]})
        if eng == "pe" and not op.is_dma:
            deps.pop(("e", "pe"), None)
        op.deps = deps
        for b in reads:
            if b.readers.get(op.key, -1) < idx:
                b.readers[op.key] = idx
        for b in writes:
            b.writers = {op.key: idx}
            b.readers = {}
        for b in pwrites:
            if b.readers:
                b.writers = {op.key: idx}
                b.readers = {}
            else:
                b.writers[op.key] = idx
        self.ops.append(op)
        if track:
            self.last[op.key] = idx
        return idx

    def barrier(self):
        snap = dict(self.last)
        for eng in ("pe", "act", "dve", "pool", "sp"):
            self.add(eng, lambda e: None, extra=snap, track=False)

    def emit(self):
        nc = self.nc
        ops = self.ops
        for op in ops:
            for k, i in op.deps.items():
                ops[i].signal = True
        cnt = {}
        for op in ops:
            if op.signal:
                cnt[op.key] = cnt.get(op.key, 0) + 1
                op.val = cnt[op.key] * (16 if op.is_dma else 1)
        keys = sorted(cnt.keys(), key=str)
        with contextlib.ExitStack() as st:
            sems = {}
            for k in keys:
                sems[k] = st.enter_context(nc.semaphore("s_" + str(k[1])))
            block = st.enter_context(nc.Block())

            def run(engname, engobj):
                known = {}
                for op in ops:
                    if op.eng != engname:
                        continue
                    for k, i in op.deps.items():
                        v = ops[i].val
                        if known.get(k, 0) < v:
                            engobj.wait_ge(sems[k], v)
                            known[k] = v
                    ins = op.fn(engobj)
                    if op.signal:
                        assert ins is not None
                        ins.then_inc(sems[op.key], 16 if op.is_dma else 1)

            @block.tensor
            def _(e):
                run("pe", e)

            @block.scalar
            def _(e):
                run("act", e)

            @block.vector
            def _(e):
                run("dve", e)

            @block.gpsimd
            def _(e):
                run("pool", e)

            @block.sync
            def _(e):
                run("sp", e)
        return len(ops), len(keys)


class TL:
    __slots__ = ("t", "b")

    def __init__(self, t, b=None):
        self.t = t
        self.b = b if b is not None else Buf()


def _dtsize(dt):
    return 2 if dt == BF16 else 4


class SBAlloc:
    def __init__(self, nc):
        self.nc = nc
        self.base = (nc.sbuf_base + 63) // 64 * 64
        self.top = nc.sbuf_top
        self.cur = self.base
        self.n = 0

    def tile(self, shape, dt, at=None, buf=None):
        size = int(np.prod(shape[1:])) * _dtsize(dt)
        size = (size + 63) // 64 * 64
        off = self.cur if at is None else at
        self.n += 1
        t = self.nc.alloc_sbuf_tensor_at("sb%d" % self.n, list(shape), dt, offset=off)
        if at is None:
            self.cur += size
            assert self.cur <= self.top, ("SBUF overflow", self.cur - self.base, self.top - self.base)
        tl = TL(t, buf)
        return tl, off


def host_consts(S):
    t = np.arange(S)
    row = (t // 64).astype(np.float32)
    col = (t % 64).astype(np.float32)
    tt = t.astype(np.float32)

    def tab(bs, posf):
        C = np.zeros((128, S), np.float32)
        Sn = np.zeros((128, S), np.float32)
        h = bs // 2
        for r in range(128):
            i = r % bs
            f = i % h
            inv = np.float32(10000.0) ** (-(np.float32(2 * f) / np.float32(bs)))
            ang = (posf(r) * np.float32(inv)).astype(np.float32)
            C[r] = np.cos(ang.astype(np.float64)).astype(np.float32)
            Sn[r] = np.sin(ang.astype(np.float64)).astype(np.float32)
        return C, Sn

    cm, sm = tab(16, lambda r: row if (r % 32) < 16 else col)
    cd, sd = tab(32, lambda r: row if (r % 64) < 32 else col)
    cr, sr = tab(64, lambda r: tt)

    def perm(bs):
        h = bs // 2
        Pm = np.zeros((128, 128), np.float32)
        for i in range(128):
            if i % bs < h:
                Pm[i, i + h] = -1.0
            else:
                Pm[i, i - h] = 1.0
        return np.ascontiguousarray(Pm.T)

    k = np.arange(128)[:, None].astype(np.float32)
    q = np.arange(128)[None, :].astype(np.float32)
    relF = np.maximum(q - k, 0.0)
    mskF = (q >= k).astype(np.float32)
    relB = np.maximum(k - q, 0.0)
    mskB = (k >= q).astype(np.float32)
    ret4 = np.stack([relF, mskF, relB, mskB]).astype(np.float32)
    qd = np.stack([np.broadcast_to(q + 1.0, (128, 128)), np.broadcast_to(128.0 - q, (128, 128))]).astype(np.float32)
    kd = np.stack([127.0 - k[:, 0], k[:, 0]], axis=1).astype(np.float32)
    return {
        "k_cm": cm, "k_sm": sm, "k_cd": cd, "k_sd": sd, "k_cr": cr, "k_sr": sr,
        "k_p16": perm(16), "k_p32": perm(32), "k_p64": perm(64),
        "k_ident": np.eye(128, dtype=np.float32),
        "k_ret4": ret4, "k_qd": np.ascontiguousarray(qd), "k_kd": np.ascontiguousarray(kd),
    }


def build(S=8192, DEBUG=()):
    nc = bass.Bass("TRN2", target_bir_lowering=False)
    NXT = S // 128
    T = S + CTX
    NTT = NXT + 2
    NKT = NTT
    P = Prog(nc)
    sba = SBAlloc(nc)

    def dram_in(name, shape, dt=F32):
        return nc.dram_tensor(name, list(shape), dt, kind="ExternalInput").ap()

    def dram_scr(name, shape, dt):
        kind = "ExternalOutput" if name in DEBUG else "Internal"
        return TL(nc.dram_tensor(name, list(shape), dt, kind=kind).ap())

    x_in = dram_in("x", [S, D])
    c_in = dram_in("c", [D])
    ctx_in = dram_in("ctx", [CTX, D])
    cctx_in = dram_in("c_ctx", [D])
    ada_w = dram_in("ada_w", [DEPTH, D, 9 * D])
    ada_b = dram_in("ada_b", [DEPTH, 9 * D])
    norm_gain = dram_in("norm_gain", [DEPTH, 3, D])
    ffn_w = {}
    for nm in ("ffn1_w1", "ffn1_w3", "ffn2_w1", "ffn2_w3"):
        ffn_w[nm] = dram_in(nm, [DEPTH, D, DFF])
    for nm in ("ffn1_w2", "ffn2_w2"):
        ffn_w[nm] = dram_in(nm, [DEPTH, DFF, D])
    w_in = dram_in("w_in", [DEPTH, D, PROJ])
    mla_q_norm = dram_in("mla_q_norm", [DEPTH, 384])
    mla_w_qb = dram_in("mla_w_qb", [DEPTH, 384, 768])
    mla_kv_norm = dram_in("mla_kv_norm", [DEPTH, 256])
    mla_w_kvb = dram_in("mla_w_kvb", [DEPTH, 256, 1024])
    diff_lambda = dram_in("diff_lambda", [DEPTH, 4, 64])
    diff_norm = dram_in("diff_norm", [DEPTH, 128])
    ret_decay = dram_in("ret_decay", [DEPTH, 2, 4])
    ret_norm = dram_in("ret_norm", [DEPTH, 128])
    w_branch = dram_in("w_branch", [DEPTH, 3, 512, D])
    w_out = dram_in("w_out", [DEPTH, D, D])
    final_norm = dram_in("final_norm", [D])
    k_tab = {n: dram_in(n, [128, S]) for n in ("k_cm", "k_sm", "k_cd", "k_sd", "k_cr", "k_sr")}
    k_perm = {n: dram_in(n, [128, 128]) for n in ("k_p16", "k_p32", "k_p64")}
    k_ident = dram_in("k_ident", [128, 128])
    k_ret4 = dram_in("k_ret4", [4, 128, 128])
    k_qd = dram_in("k_qd", [2, 128, 128])
    k_kd = dram_in("k_kd", [128, 2])
    out = TL(nc.dram_tensor("out", [S, D], F32, kind="ExternalOutput").ap())

    modv = dram_scr("modv", [DEPTH, 2, 9 * D], F32)
    XS = {}
    for l in range(DEPTH):
        for st in (1, 2, 3):
            if l == DEPTH - 1 and st == 3:
                continue
            XS[(l, st)] = dram_scr("xs%d_%d" % (l, st), [T, D], F32)
    SC = {}
    for l in range(DEPTH):
        SC[l] = dict(
            QM=dram_scr("QM%d" % l, [8, 96, T], BF16),
            KMn=dram_scr("KMn%d" % l, [8, 64, T], BF16),
            KMr=dram_scr("KMr%d" % l, [32, T], BF16),
            VM=dram_scr("VM%d" % l, [T, 512], BF16),
            QD=dram_scr("QD%d" % l, [512, T], BF16),
            KD=dram_scr("KD%d" % l, [512, T], BF16),
            VD=dram_scr("VD%d" % l, [T, 512], BF16),
            RQ=dram_scr("RQ%d" % l, [256, T], BF16),
            RK=dram_scr("RK%d" % l, [256, T], BF16),
            RKt=dram_scr("RKt%d" % l, [T, 256], BF16),
            RV=dram_scr("RV%d" % l, [T, 512], BF16),
            RG=dram_scr("RG%d" % l, [512, T], BF16),
            GATE=dram_scr("GATE%d" % l, [3072, T], BF16),
            YM=dram_scr("YM%d" % l, [512, T], BF16),
            YD=dram_scr("YD%d" % l, [512, T], BF16),
            YR=dram_scr("YR%d" % l, [512, T], BF16),
            OF=dram_scr("OF%d" % l, [128, NTT, 512], F32),
        )

    PA = [TL(nc.alloc_psum_tensor("pa%d" % i, [128, 1024], F32)) for i in range(3)]
    PB = [TL(nc.alloc_psum_tensor("pb%d" % i, [128, 512], F32)) for i in range(2)]
    bankbufs = [Buf() for _ in range(8)]

    class Bank:
        def __init__(self, j):
            self.j = j
            self.b = bankbufs[j]
            if j < 6:
                self.t = PA[j // 2].t
                self.o = (j % 2) * 512
            else:
                self.t = PB[j - 6].t
                self.o = 0

        def ap(self, p0=0, p1=128, c0=0, c1=512):
            return self.t[p0:p1, self.o + c0:self.o + c1]

    BK = [Bank(j) for j in range(8)]

    def pair_ap(i, p0=0, p1=128, c0=0, c1=1024):
        return PA[i].t[p0:p1, c0:c1]

    GROUP_KEYS = ("wl", "tab", "m0")

    def dma(q, out_ap, in_ap, key, reads=(), writes=(), pwrites=(), **kw):
        grp_ = key in GROUP_KEYS or key[:2] in ("ak", "av", "my", "mg", "wl")
        P.add(q, lambda e: e.dma_start(out=out_ap, in_=in_ap, **kw), reads, writes, pwrites, dma=key, serial=not grp_)

    def mm(out_ap, lhsT, rhs, start, stop, reads, bank):
        if start:
            P.add("pe", lambda e: e.matmul(out_ap, lhsT=lhsT, rhs=rhs, start=start, stop=stop), reads, writes=[bank])
        else:
            P.add("pe", lambda e: e.matmul(out_ap, lhsT=lhsT, rhs=rhs, start=start, stop=stop), reads, pwrites=[bank])

    def mmp(out_ap, lhsT, rhs, start, stop, reads, bank):
        P.add("pe", lambda e: e.matmul(out_ap, lhsT=lhsT, rhs=rhs, start=start, stop=stop), reads, pwrites=[bank])

    def act(out_ap, in_ap, func, reads, writes=(), pwrites=(), **kw):
        P.add("act", lambda e: e.activation(out=out_ap, in_=in_ap, func=func, **kw), reads, writes, pwrites)

    def tt(eng, out_ap, a, b, op, reads, writes=(), pwrites=()):
        P.add(eng, lambda e: e.tensor_tensor(out=out_ap, in0=a, in1=b, op=op), reads, writes, pwrites)

    def ts(eng, out_ap, a, s1, s2, op0, op1, reads, writes=(), pwrites=()):
        if s2 is None:
            P.add(eng, lambda e: e.tensor_scalar(out=out_ap, in0=a, scalar1=s1, scalar2=None, op0=op0), reads, writes, pwrites)
        else:
            P.add(eng, lambda e: e.tensor_scalar(out=out_ap, in0=a, scalar1=s1, scalar2=s2, op0=op0, op1=op1), reads, writes, pwrites)

    def stt(eng, out_ap, a, s, b, op0, op1, reads, writes=(), pwrites=()):
        P.add(eng, lambda e: e.scalar_tensor_tensor(out=out_ap, in0=a, scalar=s, in1=b, op0=op0, op1=op1), reads, writes, pwrites)

    def cp(eng, out_ap, in_ap, reads, writes=(), pwrites=()):
        if eng == "act":
            P.add("act", lambda e: e.activation(out=out_ap, in_=in_ap, func=AF.Copy), reads, writes, pwrites)
        else:
            P.add(eng, lambda e: e.tensor_copy(out=out_ap, in_=in_ap), reads, writes, pwrites)

    def recip(out_ap, in_ap, reads, writes=(), pwrites=()):
        P.add("dve", lambda e: e.reciprocal(out=out_ap, in_=in_ap), reads, writes, pwrites)

    def colvec(v_ap):
        return v_ap.rearrange("(k p) -> p k", p=128)

    ident, _ = sba.tile([128, 128], F32)
    identb, _ = sba.tile([128, 128], BF16)
    onesb, _ = sba.tile([128, 128], BF16)
    onesf, _ = sba.tile([128, 128], F32)
    epsc, _ = sba.tile([128, 1], F32)
    dma("sp", ident.t[:], k_ident, "c0", writes=[ident.b])
    cp("dve", identb.t[:], ident.t[:], [ident.b], [identb.b])
    P.add("dve", lambda e: e.memset(onesb.t[:], 1.0), writes=[onesb.b])
    P.add("dve", lambda e: e.memset(onesf.t[:], 1.0), writes=[onesf.b])
    P.add("dve", lambda e: e.memset(epsc.t[:], EPS), writes=[epsc.b])
    pass_base = sba.cur

    def new_pass():
        P.barrier()
        sba.cur = pass_base

    def tok_src(stage, t):
        if stage is None:
            if t < NXT:
                return x_in[t * 128:(t + 1) * 128, :], None
            return ctx_in[(t - NXT) * 128:(t - NXT + 1) * 128, :], None
        tl = XS[stage]
        return tl.t[t * 128:(t + 1) * 128, :], tl

    x_blocks = [("x", list(range(4 * b, 4 * b + 4))) for b in range(S // 512)]
    c_block = ("c", [NXT, NXT + 1])

    def col0(t):
        return t * 128

    def pass_mod():
        cT, _ = sba.tile([128, 8, 2], F32)
        dma("sp", cT.t[:, :, 0], colvec(c_in), "m0", pwrites=[cT.b], allow_slow_non_contiguous=True)
        dma("sp", cT.t[:, :, 1], colvec(cctx_in), "m0", pwrites=[cT.b], allow_slow_non_contiguous=True)
        sT, _ = sba.tile([128, 8, 2], F32)
        act(sT.t[:], cT.t[:], AF.Silu, [cT.b], [sT.b])
        wch = [sba.tile([128, 8, 512], F32)[0] for _ in range(2)]
        bia = [sba.tile([2, 512], F32)[0] for _ in range(2)]
        res = [sba.tile([2, 512], F32)[0] for _ in range(2)]
        i = 0
        for l in range(DEPTH):
            for cb in range(18):
                s = i % 2
                dma("sp", wch[s].t[:], ada_w[l][:, cb * 512:(cb + 1) * 512].rearrange("(k p) n -> p k n", p=128),
                    "mw%d" % s, writes=[wch[s].b])
                dma("sp", bia[s].t[:], ada_b[l, cb * 512:(cb + 1) * 512].partition_broadcast(2), "mb%d" % s, writes=[bia[s].b])
                bk = BK[s]
                for k in range(8):
                    mm(bk.ap(0, 2), sT.t[:, k, :], wch[s].t[:, k, :], k == 0, k == 7, [sT.b, wch[s].b], bk.b)
                tt("dve", res[s].t[:], bk.ap(0, 2), bia[s].t[:], ALU.add, [bk.b, bia[s].b], [res[s].b])
                dma("pool", modv.t[l, :, cb * 512:(cb + 1) * 512], res[s].t[:], "ms%d" % s, reads=[res[s].b], pwrites=[modv.b])
                i += 1

    def load_modcols(l, n):
        gcol, _ = sba.tile([128, 8], F32)
        dma("sp", gcol.t[:], colvec(norm_gain[l, n]), "mc", writes=[gcol.b], allow_slow_non_contiguous=True)
        res = []
        for s in range(2):
            sc, _ = sba.tile([128, 8], F32)
            sh, _ = sba.tile([128, 8], F32)
            A, _ = sba.tile([128, 8], F32)
            dma("sp", sc.t[:], colvec(modv.t[l, s, (3 * n + 1) * D:(3 * n + 2) * D]), "mc", reads=[modv.b], writes=[sc.b],
                allow_slow_non_contiguous=True)
            dma("sp", sh.t[:], colvec(modv.t[l, s, (3 * n) * D:(3 * n + 1) * D]), "mc", reads=[modv.b], writes=[sh.b],
                allow_slow_non_contiguous=True)
            stt("dve", A.t[:], sc.t[:], 1.0, gcol.t[:], ALU.add, ALU.mult, [sc.b, gcol.b], [A.b])
            res.append((A, sh))
        return res

    class Prep:
        def __init__(self, cols, trpair):
            self.cols = cols
            self.trpair = trpair
            self.xin = []
            self._offs = []
            for _ in range(4):
                tl, off = sba.tile([128, D], F32)
                self.xin.append(tl)
                self._offs.append(off)
            self.xn = []
            self._xnoffs = []
            for _ in range(2):
                tl, off = sba.tile([128, D], F32)
                self.xn.append(tl)
                self._xnoffs.append(off)
            self.hT, _ = sba.tile([128, 8, 512], BF16)
            self.ss, _ = sba.tile([128, 4], F32)
            self.rs, _ = sba.tile([128, 4], F32)

        def load(self, stage, blk, i):
            ap, tl = tok_src(stage, blk[1][i])
            dma("sp", self.xin[i].t[:], ap, "xin%d" % i, reads=[tl.b] if tl else [], writes=[self.xin[i].b])

        def sumsq(self, blk, i):
            act(self.xn[i % 2].t[:], self.xin[i].t[:], AF.Square, [self.xin[i].b], writes=[self.xn[i % 2].b],
                pwrites=[self.ss.b], accum_out=self.ss.t[:, i:i + 1])

        def rstd(self, blk):
            nt = len(blk[1])
            act(self.rs.t[:, 0:nt], self.ss.t[:, 0:nt], AF.Sqrt, [self.ss.b, epsc.b], writes=[self.rs.b], scale=1.0 / D, bias=epsc.t[:])
            recip(self.rs.t[:, 0:nt], self.rs.t[:, 0:nt], [self.rs.b], writes=[self.rs.b])

        def transpose(self, blk, i):
            s = 0 if blk[0] == "x" else 1
            A, sh = self.cols[s]
            xn = self.xn[i % 2]
            ts("dve", xn.t[:], self.xin[i].t[:], self.rs.t[:, i:i + 1], None, ALU.mult, None, [self.xin[i].b, self.rs.b], [xn.b])
            pr = self.trpair
            bb = [BK[2 * pr].b, BK[2 * pr + 1].b]
            for k in range(8):
                P.add("pe", lambda e, k=k: e.transpose(out=pair_ap(pr, 0, 128, k * 128, (k + 1) * 128), in_=xn.t[:, k * 128:(k + 1) * 128],
                                                     identity=ident.t[:]),
                      [xn.b, ident.b], **({"writes": [bb[0]]} if k == 0 else ({"writes": [bb[1]]} if k == 4 else {"pwrites": [bb[k // 4]]})))
            for k in range(8):
                o = self.hT.t[:, k, i * 128:(i + 1) * 128]
                src = pair_ap(pr, 0, 128, k * 128, (k + 1) * 128)
                if k % 2 == 0:
                    act(o, src, AF.Identity, [bb[k // 4], A.b, sh.b], pwrites=[self.hT.b], scale=A.t[:, k:k + 1], bias=sh.t[:, k:k + 1])
                else:
                    ts("dve", o, src, A.t[:, k:k + 1], sh.t[:, k:k + 1], ALU.mult, ALU.add, [bb[k // 4], A.b, sh.b], pwrites=[self.hT.b])

        def full(self, stage, blk):
            for i in range(len(blk[1])):
                self.load(stage, blk, i)
            for i in range(len(blk[1])):
                self.sumsq(blk, i)
            self.rstd(blk)
            for i in range(len(blk[1])):
                self.transpose(blk, i)

    def load_w_cast(dst, k, src_rows_ap, key, ncols):
        c = 0
        while c < ncols:
            n = min(1024, ncols - c)
            dma("pool", dst.t[:, k, c:c + n], src_rows_ap[:, c:c + n], key, pwrites=[dst.b])
            c += n

    def residual_store(stage_src, stage_dst, blk, i, bks, gate_bc, xres, ytmp, final=None):
        t = blk[1][i]
        xr = xres[i % 2]
        ap, tl = tok_src(stage_src, t)
        dma("sp", xr.t[:], ap, "xres%d" % (i % 2), reads=[tl.b] if tl else [], writes=[xr.b])
        for j in range(2):
            yt = ytmp[j]
            tt("dve", yt.t[:], bks[j].ap(), gate_bc.t[:, j * 512:(j + 1) * 512], ALU.mult, [bks[j].b, gate_bc.b], [yt.b])
            tt("pool", xr.t[:, j * 512:(j + 1) * 512], xr.t[:, j * 512:(j + 1) * 512], yt.t[:], ALU.add, [yt.b], pwrites=[xr.b])
        if final is None:
            dst = XS[stage_dst]
            dma("pool", dst.t[t * 128:(t + 1) * 128, :], xr.t[:], "xst%d" % (i % 2), reads=[xr.b], pwrites=[dst.b])
        else:
            fn_bc, fss, frs, fjunk = final
            act(fjunk.t[:], xr.t[:], AF.Square, [xr.b], writes=[fjunk.b], pwrites=[fss.b], accum_out=fss.t[:, 0:1])
            act(frs.t[:], fss.t[:], AF.Sqrt, [fss.b, epsc.b], writes=[frs.b], scale=1.0 / D, bias=epsc.t[:])
            recip(frs.t[:], frs.t[:], [frs.b], writes=[frs.b])
            stt("dve", xr.t[:], xr.t[:], frs.t[:, 0:1], fn_bc.t[:], ALU.mult, ALU.mult, [frs.b, fn_bc.b], writes=[xr.b])
            dma("pool", out.t[t * 128:(t + 1) * 128, :], xr.t[:], "xst%d" % (i % 2), reads=[xr.b], pwrites=[out.b])

    def load_gate_bc(gate_bc, l, n, s, half=False):
        src = modv.t[l, s, (3 * n + 2) * D:(3 * n + 3) * D].partition_broadcast(128)
        dma("sp", gate_bc.t[:], src, "gbc", reads=[modv.b], writes=[gate_bc.b])
        if half:
            P.add("pool", lambda e: e.tensor_scalar(out=gate_bc.t[:], in0=gate_bc.t[:], scalar1=0.5, scalar2=None, op0=ALU.mult),
                  [], writes=[gate_bc.b])

    def pass_ffn(l, which, stage_src, stage_dst, with_ctx, final):
        new_pass()
        n = 0 if which == 1 else 2
        w1b, _ = sba.tile([128, 8, DFF], BF16)
        w3b = TL(sba.tile([128, 8, DFF], BF16)[0].t, w1b.b)
        w2b = TL(sba.tile([128, NF, D], BF16)[0].t, w1b.b)
        W1 = ffn_w["ffn%d_w1" % which][l]
        W3 = ffn_w["ffn%d_w3" % which][l]
        W2 = ffn_w["ffn%d_w2" % which][l]
        wbufs = [Buf() for _ in range(4)]
        for cb in range(3):
            cc0 = cb * 1024
            cn = min(1024, DFF - cc0)
            for k in range(8):
                dma("pool", w1b.t[:, k, cc0:cc0 + cn], W1[k * 128:(k + 1) * 128, cc0:cc0 + cn], "wl%d" % cb, pwrites=[wbufs[cb]])
                dma("pool", w3b.t[:, k, cc0:cc0 + cn], W3[k * 128:(k + 1) * 128, cc0:cc0 + cn], "wl%d" % cb, pwrites=[wbufs[cb]])
        for f in range(NF):
            dma("pool", w2b.t[:, f, :], W2[f * 128:(f + 1) * 128, :], "wl3", pwrites=[wbufs[3]])
        cols = load_modcols(l, n)
        prep = Prep(cols, 2)
        g, _ = sba.tile([128, NF, 512], BF16)
        gate_bc, _ = sba.tile([128, D], F32)
        xres = [sba.tile([128, D], F32)[0] for _ in range(2)]
        if final:
            y0 = sba.tile([128, 512], F32)[0]
            ytmp = [y0, y0]
            s0 = sba.tile([128, 512], BF16)[0]
            ssb = [s0, s0]
        else:
            ytmp = [sba.tile([128, 512], F32)[0] for _ in range(2)]
            ssb = [sba.tile([128, 512], BF16)[0] for _ in range(2)]
        fin = None
        if final:
            fn_bc, _ = sba.tile([128, D], F32)
            dma("sp", fn_bc.t[:], final_norm.partition_broadcast(128), "gbc", writes=[fn_bc.b])
            fss, _ = sba.tile([128, 1], F32)
            frs, _ = sba.tile([128, 1], F32)
            fjunk = TL(nc.alloc_sbuf_tensor_at("fjunk", [128, D], BF16, offset=prep._xnoffs[1]), prep.xn[1].b)
            fin = (fn_bc, fss, frs, fjunk)
        blocks = ([c_block] if with_ctx else []) + x_blocks
        prep.full(stage_src, blocks[0])
        cur_stream = None
        for bi, blk in enumerate(blocks):
            N = len(blk[1]) * 128
            nxt = blocks[bi + 1] if bi + 1 < len(blocks) else None
            if blk[0] != cur_stream:
                cur_stream = blk[0]
                load_gate_bc(gate_bc, l, n, 0 if cur_stream == "x" else 1, half=True)
            for f in range(NF):
                b1 = BK[f % 2]
                b3 = BK[2 + f % 2]
                wbf = wbufs[f // 8]
                for k in range(8):
                    mm(b1.ap(c1=N), w1b.t[:, k, f * 128:(f + 1) * 128], prep.hT.t[:, k, 0:N], k == 0, k == 7, [wbf, prep.hT.b], b1.b)
                for k in range(8):
                    mm(b3.ap(c1=N), w3b.t[:, k, f * 128:(f + 1) * 128], prep.hT.t[:, k, 0:N], k == 0, k == 7, [wbf, prep.hT.b], b3.b)
                s = ssb[f % 2]
                act(s.t[:, 0:N], b1.ap(c1=N), AF.Silu, [b1.b], [s.b])
                tt("dve", g.t[:, f, 0:N], s.t[:, 0:N], b3.ap(c1=N), ALU.mult, [s.b, b3.b], pwrites=[g.b])
                if nxt is not None:
                    nn = len(nxt[1])
                    if f < nn:
                        prep.load(stage_src, nxt, f)
                    if 4 <= f < 4 + nn:
                        prep.sumsq(nxt, f - 4)
                    if f == 9:
                        prep.rstd(nxt)
            if nxt is not None:
                for i in range(len(nxt[1])):
                    prep.transpose(nxt, i)
            for i in range(len(blk[1])):
                bks = [BK[6], BK[7]]
                for j in range(2):
                    for f in range(NF):
                        mm(bks[j].ap(), g.t[:, f, i * 128:(i + 1) * 128], w2b.t[:, f, j * 512:(j + 1) * 512], f == 0, f == NF - 1,
                           [g.b, wbufs[3]], bks[j].b)
                residual_store(stage_src, stage_dst, blk, i, bks, gate_bc, xres, ytmp, fin if blk[0] == "x" else None)

    def pass_proj(l, stage_src, ctx_out):
        new_pass()
        sc = SC[l]
        winb, _ = sba.tile([128, 8, PROJ], BF16)
        winB = Buf()
        for (ca, cbn, key, buf) in ((0, O_GATE, "wl0", winb.b), (O_GATE, PROJ, "wl1", winB)):
            for k in range(8):
                c = ca
                while c < cbn:
                    n = min(1024, cbn - c)
                    dma("pool", winb.t[:, k, c:c + n], w_in[l][k * 128:(k + 1) * 128, c:c + n], key, pwrites=[buf])
                    c += n
        wqb = TL(sba.tile([128, 3, 768], BF16)[0].t, winb.b)
        wkvb = TL(sba.tile([128, 2, 1024], BF16)[0].t, winb.b)
        perms = {}
        for nm in ("k_p16", "k_p32", "k_p64"):
            pt = TL(sba.tile([128, 128], BF16)[0].t, winb.b)
            dma("pool", pt.t[:], k_perm[nm], "wl", pwrites=[winb.b])
            perms[nm] = pt
        cols = load_modcols(l, 1)
        qg, _ = sba.tile([128, 3], F32)
        kg, _ = sba.tile([128, 2], F32)
        dma("sp", qg.t[:], mla_q_norm[l].rearrange("(k p) -> p k", p=128), "mc", writes=[qg.b], allow_slow_non_contiguous=True)
        dma("sp", kg.t[:], mla_kv_norm[l].rearrange("(k p) -> p k", p=128), "mc", writes=[kg.b], allow_slow_non_contiguous=True)
        prep = Prep(cols, 2)
        stq = TL(nc.alloc_sbuf_tensor_at("stq%d" % l, [128, 3, 768], F32, offset=prep._offs[0]))
        stkv = TL(nc.alloc_sbuf_tensor_at("stkv%d" % l, [128, 2, 1024], F32, offset=prep._xnoffs[0]))
        qbufs = [prep.xin[0].b, prep.xin[1].b, prep.xin[2].b]
        kvbufs = [prep.xn[0].b, prep.xn[1].b]
        for k in range(3):
            dma("sp", stq.t[:, k, :], mla_w_qb[l][k * 128:(k + 1) * 128, :], "xin0", pwrites=qbufs)
        for k in range(2):
            dma("sp", stkv.t[:, k, :], mla_w_kvb[l][k * 128:(k + 1) * 128, :], "xin1", pwrites=kvbufs)
        qsc = (64 + 32) ** -0.5
        for k in range(3):
            ts("dve", wqb.t[:, k, :], stq.t[:, k, :], qg.t[:, k:k + 1], qsc, ALU.mult, ALU.mult, qbufs + [qg.b], pwrites=[winb.b])
        for k in range(2):
            ts("dve", wkvb.t[:, k, :], stkv.t[:, k, :], kg.t[:, k:k + 1], None, ALU.mult, None, kvbufs + [kg.b], pwrites=[winb.b])
        for (o, w) in ((O_DQ, 512), (O_RK, 256)):
            P.add("pool", lambda e, o=o, w=w: e.tensor_scalar(out=winb.t[:, :, o:o + w], in0=winb.t[:, :, o:o + w], scalar1=0.125, scalar2=None,
                                                          op0=ALU.mult), [], pwrites=[winb.b])
        W = winb.b
        latq, _ = sba.tile([128, 3, 512], BF16)
        latkv, _ = sba.tile([128, 2, 512], BF16)
        sqt = [sba.tile([128, 512], BF16)[0] for _ in range(2)]
        rq_rep, _ = sba.tile([128, 512], F32)
        rkv_rep, _ = sba.tile([128, 512], F32)
        rkv_tm, _ = sba.tile([128, 4], F32)
        tabbuf = Buf()
        tabs = {nm: TL(sba.tile([128, 512], F32)[0].t, tabbuf) for nm in k_tab}
        asb = [sba.tile([128, 512], BF16)[0] for _ in range(2)]
        m1, _ = sba.tile([128, 512], F32)
        m2, _ = sba.tile([128, 512], F32)
        ost = [sba.tile([128, 512], BF16)[0] for _ in range(3)]
        vst = [sba.tile([128, 4, 512], BF16)[0] for _ in range(2)]
        kts, _ = sba.tile([128, 4, 256], BF16)
        rkf = [sba.tile([128, 512], BF16)[0] for _ in range(2)]
        cnt = {"o": 0, "a": 0, "bk": 0, "v": 0}

        pending = []

        def flush():
            while pending:
                pending.pop(0)()

        def nbk():
            cnt["bk"] += 1
            return BK[cnt["bk"] % int("6")]

        def nost():
            cnt["o"] += 1
            return ost[cnt["o"] % 3], "ost%d" % (cnt["o"] % 3)

        def fm_group(lhs_fn, nk, M, rhs_fn, N, rb):
            bk = nbk()
            for k in range(nk):
                mm(bk.ap(0, M, 0, N), lhs_fn(k), rhs_fn(k), k == 0, k == nk - 1, rb, bk.b)
            if "1" == "1":
                flush()
            return bk

        blocks = [c_block] + x_blocks
        prep.full(stage_src, blocks[0])
        for bi, blk in enumerate(blocks):
            isx = blk[0] == "x"
            nt = len(blk[1])
            N = nt * 128
            c0 = blk[1][0] * 128
            x0 = c0
            hT = prep.hT
            if isx:
                for nm in k_tab:
                    dma("sp", tabs[nm].t[:, 0:N], k_tab[nm][:, x0:x0 + N], "tab", pwrites=[tabbuf])
            nxt = blocks[bi + 1] if bi + 1 < len(blocks) else None
            EARLY = "1" == "1"
            if nxt is not None and EARLY:
                for i in range(len(nxt[1])):
                    prep.load(stage_src, nxt, i)

            def win_group(o, M=128):
                return fm_group(lambda k: winb.t[:, k, o:o + M], 8, M, lambda k: hT.t[:, k, 0:N], N, [W if o < O_GATE else winB, hT.b])

            def rope_out(bk, M, permname, cname, sname, dst_list, p0=0, otile=None):
                if otile is None:
                    o, okey = nost()
                else:
                    o, okey = otile
                if isx:
                    a = asb[cnt["a"] % 2]
                    cnt["a"] += 1
                    cp("act", a.t[p0:p0 + M, 0:N], bk.ap(p0, p0 + M, 0, N), [bk.b], [a.b])

                    def tail():
                        pm = perms[permname]
                        b2 = nbk()
                        mm(b2.ap(p0, p0 + M, 0, N), pm.t[p0:p0 + M, p0:p0 + M], a.t[p0:p0 + M, 0:N], True, True, [W, a.b], b2.b)
                        tt("dve", m1.t[p0:p0 + M, 0:N], a.t[p0:p0 + M, 0:N], tabs[cname].t[p0:p0 + M, 0:N], ALU.mult, [a.b, tabs[cname].b], [m1.b])
                        tt("dve", m2.t[p0:p0 + M, 0:N], b2.ap(p0, p0 + M, 0, N), tabs[sname].t[p0:p0 + M, 0:N], ALU.mult, [b2.b, tabs[sname].b], [m2.b])
                        tt("dve", o.t[p0:p0 + M, 0:N], m1.t[p0:p0 + M, 0:N], m2.t[p0:p0 + M, 0:N], ALU.add, [m1.b, m2.b], [o.b])
                        for (dst_tl, dst_ap, r0, r1) in dst_list:
                            dma("pool", dst_ap, o.t[r0:r1, 0:N], okey, reads=[o.b], pwrites=[dst_tl.b])
                    pending.append(tail)
                    if "1" != "1":
                        flush()
                else:
                    cp("act", o.t[p0:p0 + M, 0:N], bk.ap(p0, p0 + M, 0, N), [bk.b], [o.b])
                    for (dst_tl, dst_ap, r0, r1) in dst_list:
                        dma("pool", dst_ap, o.t[r0:r1, 0:N], okey, reads=[o.b], pwrites=[dst_tl.b])

            for (lat, o_lat, nch, rrep, dim) in ((latq, O_QLAT, 3, rq_rep, 384), (latkv, O_KVLAT, 2, rkv_rep, 256)):
                ssb = BK[6]
                for c in range(nch):
                    bk = win_group(o_lat + c * 128)
                    cp("act", lat.t[:, c, 0:N], bk.ap(c1=N), [bk.b], pwrites=[lat.b])
                    sq = sqt[c % 2]
                    act(sq.t[:, 0:N], bk.ap(c1=N), AF.Square, [bk.b], [sq.b])
                    mm(ssb.ap(c1=N), onesb.t[:], sq.t[:, 0:N], c == 0, c == nch - 1, [onesb.b, sq.b], ssb.b)
                    if lat is latkv:
                        for i in range(nt):
                            mmp(BK[7].ap(0, 128, i, i + 1), sq.t[:, i * 128:(i + 1) * 128], onesb.t[:, 0:1], c == 0 and i == 0,
                                c == nch - 1 and i == nt - 1, [onesb.b, sq.b], BK[7].b)
                act(rrep.t[:, 0:N], ssb.ap(c1=N), AF.Sqrt, [ssb.b, epsc.b], [rrep.b], scale=1.0 / dim, bias=epsc.t[:])
                recip(rrep.t[:, 0:N], rrep.t[:, 0:N], [rrep.b], writes=[rrep.b])
            act(rkv_tm.t[:, 0:nt], BK[7].ap(0, 128, 0, nt), AF.Sqrt, [BK[7].b, epsc.b], [rkv_tm.b], scale=1.0 / 256, bias=epsc.t[:])
            recip(rkv_tm.t[:, 0:nt], rkv_tm.t[:, 0:nt], [rkv_tm.b], writes=[rkv_tm.b])

            bk = win_group(O_KROPE, 32)
            rope_out(bk, 32, "k_p16", "k_cm", "k_sm", [(sc["KMr"], sc["KMr"].t[:, c0:c0 + N], 0, 32)])
            for c in range(4):
                bk = win_group(O_DQ + c * 128)
                rope_out(bk, 128, "k_p32", "k_cd", "k_sd", [(sc["QD"], sc["QD"].t[c * 128:(c + 1) * 128, c0:c0 + N], 0, 128)])
            for c in range(4):
                bk = win_group(O_DK + c * 128)
                rope_out(bk, 128, "k_p32", "k_cd", "k_sd", [(sc["KD"], sc["KD"].t[c * 128:(c + 1) * 128, c0:c0 + N], 0, 128)])
            for c in range(2):
                bk = win_group(O_RQ + c * 128)
                rope_out(bk, 128, "k_p64", "k_cr", "k_sr", [(sc["RQ"], sc["RQ"].t[c * 128:(c + 1) * 128, c0:c0 + N], 0, 128)])
            for c in range(2):
                bk = win_group(O_RK + c * 128)
                rope_out(bk, 128, "k_p64", "k_cr", "k_sr", [(sc["RK"], sc["RK"].t[c * 128:(c + 1) * 128, c0:c0 + N], 0, 128)],
                         otile=(rkf[c], "rkf%d" % c))
            if isx or ctx_out:
                for c in range(4):
                    bk = win_group(O_RG + c * 128)
                    o, okey = nost()
                    act(o.t[:, 0:N], bk.ap(c1=N), AF.Silu, [bk.b], [o.b])
                    dma("pool", sc["RG"].t[c * 128:(c + 1) * 128, c0:c0 + N], o.t[:, 0:N], okey, reads=[o.b], pwrites=[sc["RG"].b])
                for c in range(24):
                    bk = win_group(O_GATE + c * 128)
                    o, okey = nost()
                    act(o.t[:, 0:N], bk.ap(c1=N), AF.Sigmoid, [bk.b], [o.b])
                    dma("pool", sc["GATE"].t[c * 128:(c + 1) * 128, c0:c0 + N], o.t[:, 0:N], okey, reads=[o.b], pwrites=[sc["GATE"].b])
            flush()
            tb = BK[4]
            tbv = PA[2].t[:, 0:512].bitcast(BF16)
            first = True
            for i in range(nt):
                for c in range(2):
                    P.add("pe", lambda e, i=i, c=c: e.transpose(out=tbv[:, (i * 2 + c) * 128:(i * 2 + c + 1) * 128],
                                                           in_=rkf[c].t[:, i * 128:(i + 1) * 128], identity=identb.t[:]),
                          [rkf[c].b, identb.b], **({"writes": [tb.b]} if first else {"pwrites": [tb.b]}))
                    first = False
            cp("dve", kts.t[:, 0:nt, :], tbv[:, 0:nt * 256].rearrange("p (i c) -> p i c", c=256), [tb.b], [kts.b])
            dma("pool", sc["RKt"].t[c0:c0 + N, :].rearrange("(i p) c -> p i c", p=128), kts.t[:, 0:nt, :], "kts", reads=[kts.b],
                pwrites=[sc["RKt"].b])
            if nxt is not None and EARLY:
                for i in range(len(nxt[1])):
                    prep.sumsq(nxt, i)
                prep.rstd(nxt)
            for (o_v, dst) in ((O_DV, sc["VD"]), (O_RV, sc["RV"])):
                v = vst[cnt["v"] % 2]
                vkey = "vst%d" % (cnt["v"] % 2)
                cnt["v"] += 1
                for i in range(nt):
                    bk = nbk()
                    for k in range(8):
                        mm(bk.ap(), hT.t[:, k, i * 128:(i + 1) * 128], winb.t[:, k, o_v:o_v + 512], k == 0, k == 7, [W, hT.b], bk.b)
                    cp("act" if i % 2 == 0 else "dve", v.t[:, i, :], bk.ap(), [bk.b], pwrites=[v.b])
                dma("pool", dst.t[c0:c0 + N, :].rearrange("(i p) c -> p i c", p=128), v.t[:, 0:nt, :], vkey, reads=[v.b], pwrites=[dst.b])
            if nxt is not None and EARLY:
                for i in range(len(nxt[1])):
                    prep.transpose(nxt, i)
            for h in range(8):
                bk = fm_group(lambda k: wqb.t[:, k, h * 96:h * 96 + 96], 3, 96, lambda k: latq.t[:, k, 0:N], N, [W, latq.b])
                o, okey = nost()
                if isx:
                    a = asb[cnt["a"] % 2]
                    cnt["a"] += 1
                    cp("act", a.t[0:96, 0:N], bk.ap(0, 96, 0, N), [bk.b], [a.b])
                    MLT = "0" == "1"
                    if MLT:
                        tt("dve", o.t[0:64, 0:N], bk.ap(0, 64, 0, N), rq_rep.t[0:64, 0:N], ALU.mult, [bk.b, rq_rep.b], [o.b])

                    def tail(a=a, o=o, okey=okey, h=h, bk=bk, MLT=MLT):
                        b2 = nbk()
                        mm(b2.ap(0, 96, 0, N), perms["k_p16"].t[0:96, 0:96], a.t[0:96, 0:N], True, True, [W, a.b], b2.b)
                        tt("dve", m1.t[64:96, 0:N], a.t[64:96, 0:N], tabs["k_cm"].t[64:96, 0:N], ALU.mult, [a.b, tabs["k_cm"].b], [m1.b])
                        tt("dve", m2.t[64:96, 0:N], b2.ap(64, 96, 0, N), tabs["k_sm"].t[64:96, 0:N], ALU.mult, [b2.b, tabs["k_sm"].b], [m2.b])
                        tt("dve", m1.t[64:96, 0:N], m1.t[64:96, 0:N], m2.t[64:96, 0:N], ALU.add, [m2.b], writes=[m1.b])
                        if not MLT:
                            tt("dve", o.t[0:64, 0:N], bk.ap(0, 64, 0, N), rq_rep.t[0:64, 0:N], ALU.mult, [bk.b, rq_rep.b], [o.b])
                        tt("dve", o.t[64:96, 0:N], m1.t[64:96, 0:N], rq_rep.t[64:96, 0:N], ALU.mult, [m1.b, rq_rep.b], pwrites=[o.b])
                        dma("pool", sc["QM"].t[h, :, c0:c0 + N], o.t[0:96, 0:N], okey, reads=[o.b], pwrites=[sc["QM"].b])
                    pending.append(tail)
                    if "1" != "1":
                        flush()
                else:
                    tt("dve", o.t[0:96, 0:N], bk.ap(0, 96, 0, N), rq_rep.t[0:96, 0:N], ALU.mult, [bk.b, rq_rep.b], [o.b])
                    dma("pool", sc["QM"].t[h, :, c0:c0 + N], o.t[0:96, 0:N], okey, reads=[o.b], pwrites=[sc["QM"].b])
            for h in range(8):
                bk = fm_group(lambda k: wkvb.t[:, k, h * 128:h * 128 + 64], 2, 64, lambda k: latkv.t[:, k, 0:N], N, [W, latkv.b])
                o, okey = nost()
                tt("dve", o.t[0:64, 0:N], bk.ap(0, 64, 0, N), rkv_rep.t[0:64, 0:N], ALU.mult, [bk.b, rkv_rep.b], [o.b])
                dma("pool", sc["KMn"].t[h, :, c0:c0 + N], o.t[0:64, 0:N], okey, reads=[o.b], pwrites=[sc["KMn"].b])
            v = vst[cnt["v"] % 2]
            vkey = "vst%d" % (cnt["v"] % 2)
            cnt["v"] += 1
            wv = wkvb.t[:].rearrange("p k (h c) -> p k h c", c=128)
            for i in range(nt):
                bk = nbk()
                for k in range(2):
                    mm(bk.ap().rearrange("p (h c) -> p h c", c=64), latkv.t[:, k, i * 128:(i + 1) * 128], wv[:, k, :, 64:128], k == 0, k == 1,
                       [W, latkv.b], bk.b)
                ts("dve", v.t[:, i, :], bk.ap(), rkv_tm.t[:, i:i + 1], None, ALU.mult, None, [bk.b, rkv_tm.b], pwrites=[v.b])
            dma("pool", sc["VM"].t[c0:c0 + N, :].rearrange("(i p) c -> p i c", p=128), v.t[:, 0:nt, :], vkey, reads=[v.b], pwrites=[sc["VM"].b])
            flush()
            if nxt is not None and not EARLY:
                prep.full(stage_src, nxt)

    def pass_attn(l, ctx_out):
        new_pass()
        sc = SC[l]
        lam_init = 0.8 - 0.6 * math.exp(-0.3 * l)
        QW = T if ctx_out else S
        KT = [sba.tile([128, T], BF16)[0] for _ in range(2)]
        QT = [sba.tile([128, T], BF16)[0] for _ in range(2)]
        VT = [sba.tile([128, NKT, 128], BF16)[0] for _ in range(2)]
        pt = [sba.tile([128, 1024], BF16)[0] for _ in range(4)]
        accD, _ = sba.tile([128, 1024], F32)
        accP, _ = sba.tile([128, 1024], F32)
        rc, _ = sba.tile([128, 512], F32)
        on = [sba.tile([128, 512], F32)[0] for _ in range(2)]
        dsq, _ = sba.tile([128, 512], F32)
        drs, _ = sba.tile([128, 512], F32)
        ob = [sba.tile([128, 512], BF16)[0] for _ in range(2)]
        dl, _ = sba.tile([128, 4, 64], F32)
        dma("sp", dl.t[:], diff_lambda[l].rearrange("a b -> (a b)").partition_broadcast(128).rearrange("p (a b) -> p a b", b=64),
            "mc", writes=[dl.b])
        dpr, _ = sba.tile([128, 2, 64], F32)
        tt("dve", dpr.t[:, 0, :], dl.t[:, 0, :], dl.t[:, 1, :], ALU.mult, [dl.b], pwrites=[dpr.b])
        tt("dve", dpr.t[:, 1, :], dl.t[:, 2, :], dl.t[:, 3, :], ALU.mult, [dl.b], pwrites=[dpr.b])
        dsum, _ = sba.tile([128, 2], F32)
        P.add("dve", lambda e: e.reduce_sum(out=dsum.t[:], in_=dpr.t[:], axis=mybir.AxisListType.X), [dpr.b], [dsum.b])
        dex, _ = sba.tile([128, 2], F32)
        act(dex.t[:], dsum.t[:], AF.Exp, [dsum.b], [dex.b])
        nlam, _ = sba.tile([128, 1], F32)
        stt("dve", nlam.t[:], dex.t[:, 1:2], -lam_init, dex.t[:, 0:1], ALU.add, ALU.subtract, [dex.b], [nlam.b])
        dng, _ = sba.tile([128, 1], F32)
        dma("sp", dng.t[:], diff_norm[l].rearrange("(p o) -> p o", o=1), "mc", writes=[dng.b], allow_slow_non_contiguous=True)
        ts("dve", dng.t[:], dng.t[:], 1.0 - lam_init, None, ALU.mult, None, [], writes=[dng.b])

        units = [("m", h) for h in range(8)] + [("d", h) for h in range(4)]
        grp = {"n": 0, "p": 0}
        retgen = ret_gen(l, ctx_out)

        def pull():
            pass

        def rstd_lnexp(out_ap, in_ap, dim, reads, wbuf):
            act(out_ap, in_ap, AF.Ln, list(reads) + [epsc.b], [wbuf], scale=1.0 / dim, bias=epsc.t[:])
            act(out_ap, out_ap, AF.Exp, [], writes=[wbuf], scale=-0.5)

        POOLACC = True

        diff_pending = []

        def flush_diff_post():
            while diff_pending:
                diff_pending.pop(0)()

        def diff_block(K, Q, V, q0, N, kts):
            O1, O2, S1 = BK[6], BK[7], BK[4]
            P.add("dve", lambda e: e.memset(accD.t[:, 0:512], 0.0), [], writes=[accD.b])

            def issue_qk(kt):
                gi = grp["n"] % 2
                grp["n"] += 1
                for j in range(2):
                    bk = BK[2 * gi + j]
                    mm(bk.ap(c1=N), K.t[j * 64:(j + 1) * 64, kt * 128:(kt + 1) * 128], Q.t[j * 64:(j + 1) * 64, q0:q0 + N], True, True,
                       [K.b, Q.b], bk.b)
                return gi

            def issue_rest(kt, gi, idx, first, last):
                p = pt[grp["p"] % 4]
                grp["p"] += 1
                rb = [BK[2 * gi].b, BK[2 * gi + 1].b]
                if N == 512:
                    act(p.t[:, :], pair_ap(gi), AF.Exp, rb, [p.b])
                else:
                    act(p.t[:, 0:N], pair_ap(gi, 0, 128, 0, N), AF.Exp, [rb[0]], [p.b])
                    act(p.t[:, 512:512 + N], pair_ap(gi, 0, 128, 512, 512 + N), AF.Exp, [rb[1]], pwrites=[p.b])
                mm(O1.ap(c1=N), V.t[:, kt, :], p.t[:, 0:N], first, last, [V.b, p.b], O1.b)
                mm(S1.ap(c1=N), onesb.t[:], p.t[:, 0:N], first, last, [onesb.b, p.b], S1.b)
                mm(O2.ap(c1=N), V.t[:, kt, :], p.t[:, 512:512 + N], first, last, [V.b, p.b], O2.b)
                tt("dve", accD.t[:, 0:N], accD.t[:, 0:N], p.t[:, 512:512 + N], ALU.add, [p.b], writes=[accD.b])
                if idx == 2:
                    flush_diff_post()

            q = []
            for idx, kt in enumerate(kts):
                gi = issue_qk(kt)
                q.append((kt, gi, idx, idx == 0, idx == len(kts) - 1))
                if len(q) > 1:
                    issue_rest(*q.pop(0))
            while q:
                issue_rest(*q.pop(0))
            flush_diff_post()
            S2 = BK[5]
            mm(S2.ap(c1=N), onesf.t[:], accD.t[:, 0:N], True, True, [onesf.b, accD.b], S2.b)
            for j, (O, Sb) in enumerate(((O1, S1), (O2, S2))):
                recip(rc.t[:, 0:N], Sb.ap(c1=N), [Sb.b], [rc.b])
                tt("dve", on[j].t[:, 0:N], O.ap(c1=N), rc.t[:, 0:N], ALU.mult, [O.b, rc.b], [on[j].b])

        def load_unit(ui):
            kind, h = units[ui]
            s = ui % 2
            K, Q, V = KT[s], QT[s], VT[s]
            if kind == "m":
                dma("sp", K.t[0:64, :], sc["KMn"].t[h], "ak%d" % s, reads=[sc["KMn"].b], pwrites=[K.b])
                dma("sp", K.t[64:96, :], sc["KMr"].t[:, :], "ak%d" % s, reads=[sc["KMr"].b], pwrites=[K.b])
                dma("sp", Q.t[0:96, 0:QW], sc["QM"].t[h, :, 0:QW], "aq%d" % s, reads=[sc["QM"].b], writes=[Q.b])
                for t0 in range(0, NKT, 16):
                    t1 = min(NKT, t0 + 16)
                    dma("sp", V.t[:, t0:t1, 0:64], sc["VM"].t[t0 * 128:t1 * 128, h * 64:(h + 1) * 64].rearrange("(t p) c -> p t c", p=128),
                        "av%d" % s, reads=[sc["VM"].b], pwrites=[V.b])
                P.add("pool", lambda e: e.memset(V.t[:, :, 64:128], 1.0), [], pwrites=[V.b])
            else:
                dma("sp", K.t[:, :], sc["KD"].t[h * 128:(h + 1) * 128, :], "ak%d" % s, reads=[sc["KD"].b], writes=[K.b])
                dma("sp", Q.t[:, 0:QW], sc["QD"].t[h * 128:(h + 1) * 128, 0:QW], "aq%d" % s, reads=[sc["QD"].b], writes=[Q.b])
                for t0 in range(0, NKT, 16):
                    t1 = min(NKT, t0 + 16)
                    dma("sp", V.t[:, t0:t1, :], sc["VD"].t[t0 * 128:t1 * 128, h * 128:(h + 1) * 128].rearrange("(t p) c -> p t c", p=128),
                        "av%d" % s, reads=[sc["VD"].b], pwrites=[V.b])

        def softmax_block(K, Q, V, p0, p1, q0, N, kts, obk, sbk):
            groups = [kts[i:i + 2] for i in range(0, len(kts), 2)]
            pend = []

            def issue_qk(g):
                gi = grp["n"] % 3
                grp["n"] += 1
                for jj, kt in enumerate(g):
                    bk = BK[2 * gi + jj]
                    mm(bk.ap(c1=N), K.t[p0:p1, kt * 128:(kt + 1) * 128], Q.t[p0:p1, q0:q0 + N], True, True, [K.b, Q.b], bk.b)
                return gi

            def issue_rest(g, gi, first, last):
                p = pt[grp["p"] % 4]
                grp["p"] += 1
                ng = len(g)
                rb = [BK[2 * gi + jj].b for jj in range(ng)]
                if N == 512:
                    act(p.t[:, 0:ng * 512], pair_ap(gi, 0, 128, 0, ng * 512), AF.Exp, rb, [p.b])
                else:
                    for jj in range(ng):
                        act(p.t[:, jj * 512:jj * 512 + N], pair_ap(gi, 0, 128, jj * 512, jj * 512 + N), AF.Exp, [rb[jj]],
                            **({"writes": [p.b]} if jj == 0 else {"pwrites": [p.b]}))
                for jj, kt in enumerate(g):
                    st = first and jj == 0
                    sp_ = last and jj == ng - 1
                    mm(obk.ap(c1=N), V.t[:, kt, :], p.t[:, jj * 512:jj * 512 + N], st, sp_, [V.b, p.b], obk.b)
                    if sbk is not None:
                        mm(sbk.ap(c1=N), onesb.t[:], p.t[:, jj * 512:jj * 512 + N], st, sp_, [onesb.b, p.b], sbk.b)
                pull()

            q = []
            for gidx, g in enumerate(groups):
                gi = issue_qk(g)
                q.append((g, gi, gidx == 0, gidx == len(groups) - 1))
                if len(q) > 2:
                    issue_rest(*q.pop(0))
            while q:
                issue_rest(*q.pop(0))

        def qblocks():
            res = [(b * 512, 512, list(range(NKT))) for b in range(S // 512)]
            if ctx_out:
                res.append((S, 256, [NXT, NXT + 1]))
            return res

        load_unit(0)
        for ui, (kind, h) in enumerate(units):
            s = ui % 2
            K, Q, V = KT[s], QT[s], VT[s]
            if ui + 1 < len(units):
                load_unit(ui + 1)
            for (q0, N, kts) in qblocks():
                if kind == "m":
                    obk = BK[6]
                    softmax_block(K, Q, V, 0, 96, q0, N, kts, obk, None)
                    recip(rc.t[0:64, 0:N], obk.ap(64, 128, 0, N), [obk.b], [rc.b])
                    o = ob[0]
                    tt("dve", o.t[0:64, 0:N], obk.ap(0, 64, 0, N), rc.t[0:64, 0:N], ALU.mult, [obk.b, rc.b], [o.b])
                    dma("pool", sc["YM"].t[h * 64:(h + 1) * 64, q0:q0 + N], o.t[0:64, 0:N], "ob0", reads=[o.b], pwrites=[sc["YM"].b])
                else:
                    diff_block(K, Q, V, q0, N, kts)

                    def post(h=h, q0=q0, N=N):
                        stt("dve", on[0].t[:, 0:N], on[1].t[:, 0:N], nlam.t[:, 0:1], on[0].t[:, 0:N], ALU.mult, ALU.add, [on[1].b, nlam.b],
                            writes=[on[0].b])
                        tt("dve", dsq.t[:, 0:N], on[0].t[:, 0:N], on[0].t[:, 0:N], ALU.mult, [on[0].b], [dsq.b])
                        nb = BK[5]
                        mm(nb.ap(c1=N), onesf.t[:], dsq.t[:, 0:N], True, True, [onesf.b, dsq.b], nb.b)
                        rstd_lnexp(drs.t[:, 0:N], nb.ap(c1=N), 128, [nb.b], drs.b)
                        o = ob[1]
                        stt("dve", o.t[:, 0:N], on[0].t[:, 0:N], dng.t[:, 0:1], drs.t[:, 0:N], ALU.mult, ALU.mult, [on[0].b, dng.b, drs.b], [o.b])
                        dma("pool", sc["YD"].t[h * 128:(h + 1) * 128, q0:q0 + N], o.t[:, 0:N], "ob1", reads=[o.b], pwrites=[sc["YD"].b])
                    diff_pending.append(post)
        flush_diff_post()
        for _ in retgen:
            pass

    def ret_gen(l, ctx_out):
        sc = SC[l]
        RPE = "dve"
        rd, _ = sba.tile([128, 8], F32)
        dma("sp", rd.t[:], ret_decay[l].rearrange("a b -> (a b)").partition_broadcast(128), "mc", writes=[rd.b])
        lg, _ = sba.tile([128, 8], F32)
        act(lg.t[:], rd.t[:], AF.Exp, [rd.b], [lg.b])
        ts("dve", lg.t[:], lg.t[:], -1.0, None, ALU.mult, None, [], writes=[lg.b])
        cdec, _ = sba.tile([128, 8], F32)
        act(cdec.t[:], lg.t[:], AF.Exp, [lg.b], [cdec.b], scale=128.0)
        r4, _ = sba.tile([128, 4, 128], F32)
        dma("sp", r4.t[:], k_ret4.rearrange("a p q -> p a q"), "mc", writes=[r4.b])
        qdc, _ = sba.tile([128, 2, 128], F32)
        dma("sp", qdc.t[:], k_qd.rearrange("a p q -> p a q"), "mc", writes=[qdc.b])
        kdc, _ = sba.tile([128, 2], F32)
        dma("sp", kdc.t[:], k_kd, "mc", writes=[kdc.b])
        maskT, _ = sba.tile([128, 2, 4, 128], F32)
        qdT, _ = sba.tile([128, 2, 4, 128], F32)
        kdT, _ = sba.tile([128, 2, 4], F32)
        for d in range(2):
            for h in range(4):
                i = d * 4 + h
                act(maskT.t[:, d, h, :], r4.t[:, 2 * d, :], AF.Exp, [r4.b, lg.b], pwrites=[maskT.b], scale=lg.t[:, i:i + 1])
                tt("dve", maskT.t[:, d, h, :], maskT.t[:, d, h, :], r4.t[:, 2 * d + 1, :], ALU.mult, [r4.b], pwrites=[maskT.b])
                act(qdT.t[:, d, h, :], qdc.t[:, d, :], AF.Exp, [qdc.b, lg.b], pwrites=[qdT.b], scale=lg.t[:, i:i + 1])
            act(kdT.t[:, d, :], lg.t[:, d * 4:d * 4 + 4], AF.Exp, [lg.b, kdc.b], pwrites=[kdT.b], scale=kdc.t[:, d:d + 1])
        rng_col, _ = sba.tile([128, 1], F32)
        dma("sp", rng_col.t[:], ret_norm[l].rearrange("(p o) -> p o", o=1), "mc", writes=[rng_col.b], allow_slow_non_contiguous=True)

        qf = [sba.tile([64, 4, 128], BF16)[0] for _ in range(2)]
        kf = [sba.tile([64, 4, 128], BF16)[0] for _ in range(2)]
        ktm = [sba.tile([128, 256], BF16)[0] for _ in range(2)]
        vtm = [sba.tile([128, 512], BF16)[0] for _ in range(2)]
        am, _ = sba.tile([128, 4, 128], BF16)
        kdm, _ = sba.tile([128, 4, 64], BF16)
        qdm, _ = sba.tile([64, 4, 128], BF16)
        Sf, _ = sba.tile([64, 4, 128], F32)
        Sb16, _ = sba.tile([64, 4, 128], BF16)
        osb = [sba.tile([128, 4, 128], F32)[0] for _ in range(2)]
        ofl = [sba.tile([128, 4, 128], F32)[0] for _ in range(2)]
        gsb = [sba.tile([128, 4, 128], BF16)[0] for _ in range(2)]
        sqs, _ = sba.tile([128, 512], F32)
        rrs, _ = sba.tile([128, 512], F32)
        yo = [sba.tile([128, 4, 128], BF16)[0] for _ in range(2)]

        yield
        fwd = [NXT, NXT + 1] + list(range(NXT))
        bwd = [NXT + 1, NXT] + list(range(NXT - 1, -1, -1))
        step = 0
        posts = []

        def flush_posts():
            while posts:
                posts.pop(0)()

        for d, order in ((0, fwd), (1, bwd)):
            P.add("dve", lambda e: e.memset(Sf.t[:], 0.0), [], writes=[Sf.b])
            P.add("dve", lambda e: e.memset(Sb16.t[:], 0.0), [], writes=[Sb16.b])
            for t in order:
                s = step % 2
                step += 1
                isx = t < NXT
                need_out = isx or ctx_out
                c0 = t * 128
                dma("sp", qf[s].t[:], sc["RQ"].t[:, c0:c0 + 128].rearrange("(h d) q -> d h q", d=64), "rq%d" % s, reads=[sc["RQ"].b],
                    writes=[qf[s].b])
                dma("sp", kf[s].t[:], sc["RK"].t[:, c0:c0 + 128].rearrange("(h d) q -> d h q", d=64), "rk%d" % s, reads=[sc["RK"].b],
                    writes=[kf[s].b])
                dma("sp", ktm[s].t[:], sc["RKt"].t[c0:c0 + 128, :], "rkt%d" % s, reads=[sc["RKt"].b], writes=[ktm[s].b])
                dma("sp", vtm[s].t[:], sc["RV"].t[c0:c0 + 128, :], "rv%d" % s, reads=[sc["RV"].b], writes=[vtm[s].b])
                obk = BK[1] if s == 0 else BK[3]
                if need_out and d == 1:
                    dma("sp", ofl[s].t[:], sc["OF"].t[:, t, :].rearrange("p (h q) -> p h q", q=128), "ofl%d" % s, reads=[sc["OF"].b],
                        writes=[ofl[s].b])
                    dma("sp", gsb[s].t[:], sc["RG"].t[:, c0:c0 + 128].rearrange("(h p) q -> p h q", p=128), "gsb%d" % s, reads=[sc["RG"].b],
                        writes=[gsb[s].b])
                if need_out:
                    ab = BK[0]
                    for h in range(4):
                        mmp(ab.ap(0, 128, h * 128, (h + 1) * 128), kf[s].t[:, h, :], qf[s].t[:, h, :], True, True, [kf[s].b, qf[s].b], ab.b)
                    tt("dve", am.t[:], ab.ap().rearrange("p (h q) -> p h q", q=128), maskT.t[:, d, :, :], ALU.mult, [ab.b, maskT.b], [am.b])
                    tt(RPE, qdm.t[:], qf[s].t[:], qdT.t[0:64, d, :, :], ALU.mult, [qf[s].b, qdT.b], [qdm.b])
                    for h in range(4):
                        mmp(obk.ap(0, 128, h * 128, (h + 1) * 128), vtm[s].t[:, h * 128:(h + 1) * 128], am.t[:, h, :], True, False,
                            [vtm[s].b, am.b], obk.b)
                        mmp(obk.ap(0, 128, h * 128, (h + 1) * 128), Sb16.t[:, h, :], qdm.t[:, h, :], False, True, [Sb16.b, qdm.b], obk.b)
                tt(RPE, kdm.t[:], ktm[s].t[:].rearrange("p (h d) -> p h d", d=64), kdT.t[:, d, :].unsqueeze(2).to_broadcast([128, 4, 64]),
                   ALU.mult, [ktm[s].b, kdT.b], [kdm.b])
                ub = BK[2]
                for h in range(4):
                    mmp(ub.ap(0, 64, h * 128, (h + 1) * 128), kdm.t[:, h, :], vtm[s].t[:, h * 128:(h + 1) * 128], True, True, [kdm.b, vtm[s].b], ub.b)
                for h in range(4):
                    i = d * 4 + h
                    stt("dve", Sf.t[:, h, :], Sf.t[:, h, :], cdec.t[0:64, i:i + 1], ub.ap(0, 64, h * 128, (h + 1) * 128), ALU.mult, ALU.add,
                        [ub.b, cdec.b], pwrites=[Sf.b])
                cp("act", Sb16.t[:], Sf.t[:], [Sf.b], [Sb16.b])
                flush_posts()
                if not need_out:
                    continue

                def post(s=s, t=t, c0=c0, d=d, obk=obk):
                    o = osb[s]
                    if d == 0:
                        cp("act", o.t[:], obk.ap().rearrange("p (h q) -> p h q", q=128), [obk.b], [o.b])
                        dma("pool", sc["OF"].t[:, t, :].rearrange("p (h q) -> p h q", q=128), o.t[:], "osb%d" % s, reads=[o.b],
                            pwrites=[sc["OF"].b])
                        return
                    of = ofl[s]
                    gs = gsb[s]
                    tt("dve", o.t[:], obk.ap().rearrange("p (h q) -> p h q", q=128), of.t[:], ALU.add, [obk.b, of.b], [o.b])
                    of2 = o.t[:].rearrange("p h q -> p (h q)")
                    tt(RPE, sqs.t[:], of2, of2, ALU.mult, [o.b], [sqs.b])
                    nb = BK[4]
                    mm(nb.ap(), onesf.t[:], sqs.t[:], True, True, [onesf.b, sqs.b], nb.b)
                    act(rrs.t[:], nb.ap(), AF.Ln, [nb.b, epsc.b], [rrs.b], scale=1.0 / 128, bias=epsc.t[:])
                    act(rrs.t[:], rrs.t[:], AF.Exp, [], writes=[rrs.b], scale=-0.5)
                    stt("dve", of2, of2, rng_col.t[:, 0:1], rrs.t[:], ALU.mult, ALU.mult, [rng_col.b, rrs.b], writes=[o.b])
                    y = yo[s]
                    tt(RPE, y.t[:], o.t[:], gs.t[:], ALU.mult, [o.b, gs.b], [y.b])
                    dma("pool", sc["YR"].t[:, c0:c0 + 128].rearrange("(h p) q -> p h q", p=128), y.t[:], "yo%d" % s, reads=[y.b],
                        pwrites=[sc["YR"].b])
                posts.append(post)
            flush_posts()
        yield

    def pass_merge(l, stage_src, stage_dst, ctx_out):
        new_pass()
        sc = SC[l]
        wbb, _ = sba.tile([128, 12, D], BF16)
        wob = TL(sba.tile([128, 8, D], BF16)[0].t, wbb.b)
        for i in range(3):
            for k in range(4):
                load_w_cast(wbb, i * 4 + k, w_branch[l, i][k * 128:(k + 1) * 128, :], "wl", D)
        for k in range(8):
            load_w_cast(wob, k, w_out[l][k * 128:(k + 1) * 128, :], "wl", D)
        ysb = [sba.tile([128, 12, 512], BF16)[0] for _ in range(2)]
        gsb = [sba.tile([128, 24, 512], BF16)[0] for _ in range(2)]
        yT, _ = sba.tile([128, 8, 512], BF16)
        mt = [sba.tile([128, 512], F32)[0] for _ in range(3)]
        gate_bc, _ = sba.tile([128, D], F32)
        xres = [sba.tile([128, D], F32)[0] for _ in range(2)]
        ytmp = [sba.tile([128, 512], F32)[0] for _ in range(2)]
        blocks = ([c_block] if ctx_out else []) + x_blocks

        def load_blk(bi):
            blk = blocks[bi]
            N = len(blk[1]) * 128
            c0 = blk[1][0] * 128
            s = bi % 2
            for i, nm in enumerate(("YM", "YD", "YR")):
                dma("sp", ysb[s].t[:, i * 4:(i + 1) * 4, 0:N], sc[nm].t[:, c0:c0 + N].rearrange("(k p) n -> p k n", p=128), "my%d" % s,
                    reads=[sc[nm].b], pwrites=[ysb[s].b])
            for i in range(3):
                dma("sp", gsb[s].t[:, i * 8:(i + 1) * 8, 0:N], sc["GATE"].t[i * 1024:(i + 1) * 1024, c0:c0 + N].rearrange("(k p) n -> p k n", p=128),
                    "mg%d" % s, reads=[sc["GATE"].b], pwrites=[gsb[s].b])

        load_blk(0)
        cur_stream = None
        for bi, blk in enumerate(blocks):
            N = len(blk[1]) * 128
            s = bi % 2
            if bi + 1 < len(blocks):
                load_blk(bi + 1)
            if blk[0] != cur_stream:
                cur_stream = blk[0]
                load_gate_bc(gate_bc, l, 1, 0 if cur_stream == "x" else 1)
            for oc in range(8):
                zb = [BK[(oc % 2) * 3 + i] for i in range(3)]
                for i in range(3):
                    for k in range(4):
                        mm(zb[i].ap(c1=N), wbb.t[:, i * 4 + k, oc * 128:(oc + 1) * 128], ysb[s].t[:, i * 4 + k, 0:N], k == 0, k == 3,
                           [wbb.b, ysb[s].b], zb[i].b)
                for i in range(3):
                    tt("dve", mt[i].t[:, 0:N], zb[i].ap(c1=N), gsb[s].t[:, i * 8 + oc, 0:N], ALU.mult, [zb[i].b, gsb[s].b], [mt[i].b])
                tt("pool", mt[0].t[:, 0:N], mt[0].t[:, 0:N], mt[1].t[:, 0:N], ALU.add, [mt[1].b], writes=[mt[0].b])
                tt("pool", yT.t[:, oc, 0:N], mt[0].t[:, 0:N], mt[2].t[:, 0:N], ALU.add, [mt[0].b, mt[2].b], pwrites=[yT.b])
            for i in range(len(blk[1])):
                bks = [BK[6], BK[7]]
                for j in range(2):
                    for k in range(8):
                        mm(bks[j].ap(), yT.t[:, k, i * 128:(i + 1) * 128], wob.t[:, k, j * 512:(j + 1) * 512], k == 0, k == 7, [yT.b, wbb.b], bks[j].b)
                residual_store(stage_src, stage_dst, blk, i, bks, gate_bc, xres, ytmp)

    pass_mod()
    stage = None
    for l in range(DEPTH):
        last = l == DEPTH - 1
        ctx_out = not last
        pass_ffn(l, 1, stage, (l, 1), True, False)
        pass_proj(l, (l, 1), ctx_out)
        pass_attn(l, ctx_out)
        pass_merge(l, (l, 1), (l, 2), ctx_out)
        pass_ffn(l, 2, (l, 2), None if last else (l, 3), ctx_out, last)
        stage = (l, 3)
    P.barrier()
    stats = P.emit()
    return nc, stats


_CACHE = {}


def _get(S, DEBUG=()):
    key = (S, tuple(DEBUG))
    if key not in _CACHE:
        _CACHE[key] = (build(S, DEBUG), host_consts(S))
    return _CACHE[key]


def kernel(**inputs):
    x = np.asarray(inputs["x"], np.float32)
    B, S, _ = x.shape
    (nc, _), hc = _get(S)
    shared = {k: np.ascontiguousarray(np.asarray(v, np.float32)) for k, v in inputs.items() if k not in ("x", "c", "ctx")}
    in_maps = []
    for b in range(B):
        m = dict(shared)
        m.update(hc)
        m["x"] = np.ascontiguousarray(x[b])
        m["c"] = np.ascontiguousarray(np.asarray(inputs["c"], np.float32)[b])
        m["ctx"] = np.ascontiguousarray(np.asarray(inputs["ctx"], np.float32)[b])
        in_maps.append(m)
    res = run_bass_kernel_spmd(nc, in_maps, core_ids=list(range(B)))
    return np.stack([np.asarray(r["out"], np.float32) for r in res.results], axis=0)
```

```python
import math
import contextlib
import numpy as np
import ml_dtypes
import concourse.bass as bass
import concourse.mybir as mybir
from concourse.bass_utils import run_bass_kernel_spmd

F32 = mybir.dt.float32
BF16 = mybir.dt.bfloat16
AF = mybir.ActivationFunctionType
ALU = mybir.AluOpType

D = 1024
CTX = 256
DFF = 2816
NF = DFF // 128
PROJ = 6816
EPS = 1e-6
O_QLAT, O_KVLAT, O_KROPE, O_DQ, O_DK, O_DV = 0, 384, 640, 672, 1184, 1696
O_RQ, O_RK, O_RV, O_RG, O_GATE = 2208, 2464, 2720, 3232, 3744
DEPTH = 2


class Buf:
    __slots__ = ("writers", "readers")

    def __init__(self):
        self.writers = {}
        self.readers = {}


class Op:
    __slots__ = ("eng", "fn", "deps", "key", "is_dma", "signal", "val")


class Prog:
    def __init__(self, nc):
        self.nc = nc
        self.ops = []
        self.last = {}

    def add(self, eng, fn, reads=(), writes=(), pwrites=(), dma=None, extra=None, serial=True, track=True):
        op = Op()
        op.eng = eng
        op.fn = fn
        op.is_dma = dma is not None
        op.key = ("d", dma) if dma is not None else ("e", eng)
        op.signal = False
        op.val = 0
        idx = len(self.ops)
        deps = {}

        def need(d):
            for k, i in d.items():
                if deps.get(k, -1) < i:
                    deps[k] = i

        for b in reads:
            need(b.writers)
        for b in writes:
            need(b.writers)
            need(b.readers)
        for b in pwrites:
            need(b.writers)
            need(b.readers)
        if extra:
            need(extra)
        if op.is_dma and serial and op.key in self.last:
            need({op.key: self.last[op.key]})
        if eng == "pe" and not op.is_dma:
            deps.pop(("e", "pe"), None)
        op.deps = deps
        for b in reads:
            if b.readers.get(op.key, -1) < idx:
                b.readers[op.key] = idx
        for b in writes:
            b.writers = {op.key: idx}
            b.readers = {}
        for b in pwrites:
            if b.readers:
                b.writers = {op.key: idx}
                b.readers = {}
            else:
                b.writers[op.key] = idx
        self.ops.append(op)
        if track:
            self.last[op.key] = idx
        return idx

    def barrier(self):
        snap = dict(self.last)
        for eng in ("pe", "act", "dve", "pool", "sp"):
            self.add(eng, lambda e: None, extra=snap, track=False)

    def emit(self):
        nc = self.nc
        ops = self.ops
        for op in ops:
            for k, i in op.deps.items():
                ops[i].signal = True
        cnt = {}
        for op in ops:
            if op.signal:
                cnt[op.key] = cnt.get(op.key, 0) + 1
                op.val = cnt[op.key] * (16 if op.is_dma else 1)
        keys = sorted(cnt.keys(), key=str)
        with contextlib.ExitStack() as st:
            sems = {}
            for k in keys:
                sems[k] = st.enter_context(nc.semaphore("s_" + str(k[1])))
            block = st.enter_context(nc.Block())

            def run(engname, engobj):
                known = {}
                for op in ops:
                    if op.eng != engname:
                        continue
                    for k, i in op.deps.items():
                        v = ops[i].val
                        if known.get(k, 0) < v:
                            engobj.wait_ge(sems[k], v)
                            known[k] = v
                    ins = op.fn(engobj)
                    if op.signal:
                        assert ins is not None
                        ins.then_inc(sems[op.key], 16 if op.is_dma else 1)

            @block.tensor
            def _(e):
                run("pe", e)

            @block.scalar
            def _(e):
                run("act", e)

            @block.vector
            def _(e):
                run("dve", e)

            @block.gpsimd
            def _(e):
                run("pool", e)

            @block.sync
            def _(e):
                run("sp", e)
        return len(ops), len(keys)


class TL:
    __slots__ = ("t", "b")

    def __init__(self, t, b=None):
        self.t = t
        self.b = b if b is not None else Buf()


def _dtsize(dt):
    return 2 if dt == BF16 else 4


class SBAlloc:
    def __init__(self, nc):
        self.nc = nc
        self.base = (nc.sbuf_base + 63) // 64 * 64
        self.top = nc.sbuf_top
        self.cur = self.base
        self.n = 0

    def tile(self, shape, dt, at=None, buf=None):
        size = int(np.prod(shape[1:])) * _dtsize(dt)
        size = (size + 63) // 64 * 64
        off = self.cur if at is None else at
        self.n += 1
        t = self.nc.alloc_sbuf_tensor_at("sb%d" % self.n, list(shape), dt, offset=off)
        if at is None:
            self.cur += size
            assert self.cur <= self.top, ("SBUF overflow", self.cur - self.base, self.top - self.base)
        tl = TL(t, buf)
        return tl, off


def host_consts(S):
    t = np.arange(S)
    row = (t // 64).astype(np.float32)
    col = (t % 64).astype(np.float32)
    tt = t.astype(np.float32)

    def tab(bs, posf):
        C = np.zeros((128, S), np.float32)
        Sn = np.zeros((128, S), np.float32)
        h = bs // 2
        for r in range(128):
            i = r % bs
            f = i % h
            inv = np.float32(10000.0) ** (-(np.float32(2 * f) / np.float32(bs)))
            ang = (posf(r) * np.float32(inv)).astype(np.float32)
            C[r] = np.cos(ang.astype(np.float64)).astype(np.float32)
            Sn[r] = np.sin(ang.astype(np.float64)).astype(np.float32)
        return C, Sn

    cm, sm = tab(16, lambda r: row if (r % 32) < 16 else col)
    cd, sd = tab(32, lambda r: row if (r % 64) < 32 else col)
    cr, sr = tab(64, lambda r: tt)

    def perm(bs):
        h = bs // 2
        Pm = np.zeros((128, 128), np.float32)
        for i in range(128):
            if i % bs < h:
                Pm[i, i + h] = -1.0
            else:
                Pm[i, i - h] = 1.0
        return np.ascontiguousarray(Pm.T)

    k = np.arange(128)[:, None].astype(np.float32)
    q = np.arange(128)[None, :].astype(np.float32)
    relF = np.maximum(q - k, 0.0)
    mskF = (q >= k).astype(np.float32)
    relB = np.maximum(k - q, 0.0)
    mskB = (k >= q).astype(np.float32)
    ret4 = np.stack([relF, mskF, relB, mskB]).astype(np.float32)
    qd = np.stack([np.broadcast_to(q + 1.0, (128, 128)), np.broadcast_to(128.0 - q, (128, 128))]).astype(np.float32)
    kd = np.stack([127.0 - k[:, 0], k[:, 0]], axis=1).astype(np.float32)
    return {
        "k_cm": cm, "k_sm": sm, "k_cd": cd, "k_sd": sd, "k_cr": cr, "k_sr": sr,
        "k_p16": perm(16), "k_p32": perm(32), "k_p64": perm(64),
        "k_ident": np.eye(128, dtype=np.float32),
        "k_ret4": ret4, "k_qd": np.ascontiguousarray(qd), "k_kd": np.ascontiguousarray(kd),
    }


def build(S=8192, DEBUG=()):
    nc = bass.Bass("TRN2", target_bir_lowering=False)
    NXT = S // 128
    T = S + CTX
    NTT = NXT + 2
    NKT = NTT
    P = Prog(nc)
    sba = SBAlloc(nc)

    def dram_in(name, shape, dt=F32):
        return nc.dram_tensor(name, list(shape), dt, kind="ExternalInput").ap()

    def dram_scr(name, shape, dt):
        kind = "ExternalOutput" if name in DEBUG else "Internal"
        return TL(nc.dram_tensor(name, list(shape), dt, kind=kind).ap())

    x_in = dram_in("x", [S, D])
    c_in = dram_in("c", [D])
    ctx_in = dram_in("ctx", [CTX, D])
    cctx_in = dram_in("c_ctx", [D])
    ada_w = dram_in("ada_w", [DEPTH, D, 9 * D])
    ada_b = dram_in("ada_b", [DEPTH, 9 * D])
    norm_gain = dram_in("norm_gain", [DEPTH, 3, D])
    ffn_w = {}
    for nm in ("ffn1_w1", "ffn1_w3", "ffn2_w1", "ffn2_w3"):
        ffn_w[nm] = dram_in(nm, [DEPTH, D, DFF])
    for nm in ("ffn1_w2", "ffn2_w2"):
        ffn_w[nm] = dram_in(nm, [DEPTH, DFF, D])
    w_in = dram_in("w_in", [DEPTH, D, PROJ])
    mla_q_norm = dram_in("mla_q_norm", [DEPTH, 384])
    mla_w_qb = dram_in("mla_w_qb", [DEPTH, 384, 768])
    mla_kv_norm = dram_in("mla_kv_norm", [DEPTH, 256])
    mla_w_kvb = dram_in("mla_w_kvb", [DEPTH, 256, 1024])
    diff_lambda = dram_in("diff_lambda", [DEPTH, 4, 64])
    diff_norm = dram_in("diff_norm", [DEPTH, 128])
    ret_decay = dram_in("ret_decay", [DEPTH, 2, 4])
    ret_norm = dram_in("ret_norm", [DEPTH, 128])
    w_branch = dram_in("w_branch", [DEPTH, 3, 512, D])
    w_out = dram_in("w_out", [DEPTH, D, D])
    final_norm = dram_in("final_norm", [D])
    k_tab = {n: dram_in(n, [128, S]) for n in ("k_cm", "k_sm", "k_cd", "k_sd", "k_cr", "k_sr")}
    k_perm = {n: dram_in(n, [128, 128]) for n in ("k_p16", "k_p32", "k_p64")}
    k_ident = dram_in("k_ident", [128, 128])
    k_ret4 = dram_in("k_ret4", [4, 128, 128])
    k_qd = dram_in("k_qd", [2, 128, 128])
    k_kd = dram_in("k_kd", [128, 2])
    out = TL(nc.dram_tensor("out", [S, D], F32, kind="ExternalOutput").ap())

    modv = dram_scr("modv", [DEPTH, 2, 9 * D], F32)
    XS = {}
    for l in range(DEPTH):
        for st in (1, 2, 3):
            if l == DEPTH - 1 and st == 3:
                continue
            XS[(l, st)] = dram_scr("xs%d_%d" % (l, st), [T, D], F32)
    SC = {}
    for l in range(DEPTH):
        SC[l] = dict(
            QM=dram_scr("QM%d" % l, [8, 96, T], BF16),
            KMn=dram_scr("KMn%d" % l, [8, 64, T], BF16),
            KMr=dram_scr("KMr%d" % l, [32, T], BF16),
            VM=dram_scr("VM%d" % l, [T, 512], BF16),
            QD=dram_scr("QD%d" % l, [512, T], BF16),
            KD=dram_scr("KD%d" % l, [512, T], BF16),
            VD=dram_scr("VD%d" % l, [T, 512], BF16),
            RQ=dram_scr("RQ%d" % l, [256, T], BF16),
            RK=dram_scr("RK%d" % l, [256, T], BF16),
            RKt=dram_scr("RKt%d" % l, [T, 256], BF16),
            RV=dram_scr("RV%d" % l, [T, 512], BF16),
            RG=dram_scr("RG%d" % l, [512, T], BF16),
            GATE=dram_scr("GATE%d" % l, [3072, T], BF16),
            YM=dram_scr("YM%d" % l, [512, T], BF16),
            YD=dram_scr("YD%d" % l, [512, T], BF16),
            YR=dram_scr("YR%d" % l, [512, T], BF16),
            OF=dram_scr("OF%d" % l, [128, NTT, 512], F32),
        )

    PA = [TL(nc.alloc_psum_tensor("pa%d" % i, [128, 1024], F32)) for i in range(3)]
    PB = [TL(nc.alloc_psum_tensor("pb%d" % i, [128, 512], F32)) for i in range(2)]
    bankbufs = [Buf() for _ in range(8)]

    class Bank:
        def __init__(self, j):
            self.j = j
            self.b = bankbufs[j]
            if j < 6:
                self.t = PA[j // 2].t
                self.o = (j % 2) * 512
            else:
                self.t = PB[j - 6].t
                self.o = 0

        def ap(self, p0=0, p1=128, c0=0, c1=512):
            return self.t[p0:p1, self.o + c0:self.o + c1]

    BK = [Bank(j) for j in range(8)]

    def pair_ap(i, p0=0, p1=128, c0=0, c1=1024):
        return PA[i].t[p0:p1, c0:c1]

    GROUP_KEYS = ("wl", "tab", "m0")

    def dma(q, out_ap, in_ap, key, reads=(), writes=(), pwrites=(), **kw):
        grp_ = key in GROUP_KEYS or key[:2] in ("ak", "av", "my", "mg", "wl")
        P.add(q, lambda e: e.dma_start(out=out_ap, in_=in_ap, **kw), reads, writes, pwrites, dma=key, serial=not grp_)

    def mm(out_ap, lhsT, rhs, start, stop, reads, bank):
        if start:
            P.add("pe", lambda e: e.matmul(out_ap, lhsT=lhsT, rhs=rhs, start=start, stop=stop), reads, writes=[bank])
        else:
            P.add("pe", lambda e: e.matmul(out_ap, lhsT=lhsT, rhs=rhs, start=start, stop=stop), reads, pwrites=[bank])

    def mmp(out_ap, lhsT, rhs, start, stop, reads, bank):
        P.add("pe", lambda e: e.matmul(out_ap, lhsT=lhsT, rhs=rhs, start=start, stop=stop), reads, pwrites=[bank])

    def act(out_ap, in_ap, func, reads, writes=(), pwrites=(), **kw):
        P.add("act", lambda e: e.activation(out=out_ap, in_=in_ap, func=func, **kw), reads, writes, pwrites)

    def tt(eng, out_ap, a, b, op, reads, writes=(), pwrites=()):
        P.add(eng, lambda e: e.tensor_tensor(out=out_ap, in0=a, in1=b, op=op), reads, writes, pwrites)

    def ts(eng, out_ap, a, s1, s2, op0, op1, reads, writes=(), pwrites=()):
        if s2 is None:
            P.add(eng, lambda e: e.tensor_scalar(out=out_ap, in0=a, scalar1=s1, scalar2=None, op0=op0), reads, writes, pwrites)
        else:
            P.add(eng, lambda e: e.tensor_scalar(out=out_ap, in0=a, scalar1=s1, scalar2=s2, op0=op0, op1=op1), reads, writes, pwrites)

    def stt(eng, out_ap, a, s, b, op0, op1, reads, writes=(), pwrites=()):
        P.add(eng, lambda e: e.scalar_tensor_tensor(out=out_ap, in0=a, scalar=s, in1=b, op0=op0, op1=op1), reads, writes, pwrites)

    def cp(eng, out_ap, in_ap, reads, writes=(), pwrites=()):
        if eng == "act":
            P.add("act", lambda e: e.activation(out=out_ap, in_=in_ap, func=AF.Copy), reads, writes, pwrites)
        else:
            P.add(eng, lambda e: e.tensor_copy(out=out_ap, in_=in_ap), reads, writes, pwrites)

    def recip(out_ap, in_ap, reads, writes=(), pwrites=()):
        P.add("dve", lambda e: e.reciprocal(out=out_ap, in_=in_ap), reads, writes, pwrites)

    def colvec(v_ap):
        return v_ap.rearrange("(k p) -> p k", p=128)

    ident, _ = sba.tile([128, 128], F32)
    identb, _ = sba.tile([128, 128], BF16)
    onesb, _ = sba.tile([128, 128], BF16)
    onesf, _ = sba.tile([128, 128], F32)
    epsc, _ = sba.tile([128, 1], F32)
    dma("sp", ident.t[:], k_ident, "c0", writes=[ident.b])
    cp("dve", identb.t[:], ident.t[:], [ident.b], [identb.b])
    P.add("dve", lambda e: e.memset(onesb.t[:], 1.0), writes=[onesb.b])
    P.add("dve", lambda e: e.memset(onesf.t[:], 1.0), writes=[onesf.b])
    P.add("dve", lambda e: e.memset(epsc.t[:], EPS), writes=[epsc.b])
    pass_base = sba.cur

    def new_pass():
        P.barrier()
        sba.cur = pass_base

    def tok_src(stage, t):
        if stage is None:
            if t < NXT:
                return x_in[t * 128:(t + 1) * 128, :], None
            return ctx_in[(t - NXT) * 128:(t - NXT + 1) * 128, :], None
        tl = XS[stage]
        return tl.t[t * 128:(t + 1) * 128, :], tl

    x_blocks = [("x", list(range(4 * b, 4 * b + 4))) for b in range(S // 512)]
    c_block = ("c", [NXT, NXT + 1])

    def col0(t):
        return t * 128

    def pass_mod():
        cT, _ = sba.tile([128, 8, 2], F32)
        dma("sp", cT.t[:, :, 0], colvec(c_in), "m0", pwrites=[cT.b], allow_slow_non_contiguous=True)
        dma("sp", cT.t[:, :, 1], colvec(cctx_in), "m0", pwrites=[cT.b], allow_slow_non_contiguous=True)
        sT, _ = sba.tile([128, 8, 2], F32)
        act(sT.t[:], cT.t[:], AF.Silu, [cT.b], [sT.b])
        wch = [sba.tile([128, 8, 512], F32)[0] for _ in range(2)]
        bia = [sba.tile([2, 512], F32)[0] for _ in range(2)]
        res = [sba.tile([2, 512], F32)[0] for _ in range(2)]
        i = 0
        for l in range(DEPTH):
            for cb in range(18):
                s = i % 2
                dma("sp", wch[s].t[:], ada_w[l][:, cb * 512:(cb + 1) * 512].rearrange("(k p) n -> p k n", p=128),
                    "mw%d" % s, writes=[wch[s].b])
                dma("sp", bia[s].t[:], ada_b[l, cb * 512:(cb + 1) * 512].partition_broadcast(2), "mb%d" % s, writes=[bia[s].b])
                bk = BK[s]
                for k in range(8):
                    mm(bk.ap(0, 2), sT.t[:, k, :], wch[s].t[:, k, :], k == 0, k == 7, [sT.b, wch[s].b], bk.b)
                tt("dve", res[s].t[:], bk.ap(0, 2), bia[s].t[:], ALU.add, [bk.b, bia[s].b], [res[s].b])
                dma("pool", modv.t[l, :, cb * 512:(cb + 1) * 512], res[s].t[:], "ms%d" % s, reads=[res[s].b], pwrites=[modv.b])
                i += 1

    def load_modcols(l, n):
        gcol, _ = sba.tile([128, 8], F32)
        dma("sp", gcol.t[:], colvec(norm_gain[l, n]), "mc", writes=[gcol.b], allow_slow_non_contiguous=True)
        res = []
        for s in range(2):
            sc, _ = sba.tile([128, 8], F32)
            sh, _ = sba.tile([128, 8], F32)
            A, _ = sba.tile([128, 8], F32)
            dma("sp", sc.t[:], colvec(modv.t[l, s, (3 * n + 1) * D:(3 * n + 2) * D]), "mc", reads=[modv.b], writes=[sc.b],
                allow_slow_non_contiguous=True)
            dma("sp", sh.t[:], colvec(modv.t[l, s, (3 * n) * D:(3 * n + 1) * D]), "mc", reads=[modv.b], writes=[sh.b],
                allow_slow_non_contiguous=True)
            stt("dve", A.t[:], sc.t[:], 1.0, gcol.t[:], ALU.add, ALU.mult, [sc.b, gcol.b], [A.b])
            res.append((A, sh))
        return res

    class Prep:
        def __init__(self, cols, trpair):
            self.cols = cols
            self.trpair = trpair
            self.xin = []
            self._offs = []
            for _ in range(4):
                tl, off = sba.tile([128, D], F32)
                self.xin.append(tl)
                self._offs.append(off)
            self.xn = []
            self._xnoffs = []
            for _ in range(2):
                tl, off = sba.tile([128, D], F32)
                self.xn.append(tl)
                self._xnoffs.append(off)
            self.hT, _ = sba.tile([128, 8, 512], BF16)
            self.ss, _ = sba.tile([128, 4], F32)
            self.rs, _ = sba.tile([128, 4], F32)

        def load(self, stage, blk, i):
            ap, tl = tok_src(stage, blk[1][i])
            dma("sp", self.xin[i].t[:], ap, "xin%d" % i, reads=[tl.b] if tl else [], writes=[self.xin[i].b])

        def sumsq(self, blk, i):
            act(self.xn[i % 2].t[:], self.xin[i].t[:], AF.Square, [self.xin[i].b], writes=[self.xn[i % 2].b],
                pwrites=[self.ss.b], accum_out=self.ss.t[:, i:i + 1])

        def rstd(self, blk):
            nt = len(blk[1])
            act(self.rs.t[:, 0:nt], self.ss.t[:, 0:nt], AF.Sqrt, [self.ss.b, epsc.b], writes=[self.rs.b], scale=1.0 / D, bias=epsc.t[:])
            recip(self.rs.t[:, 0:nt], self.rs.t[:, 0:nt], [self.rs.b], writes=[self.rs.b])

        def transpose(self, blk, i):
            s = 0 if blk[0] == "x" else 1
            A, sh = self.cols[s]
            xn = self.xn[i % 2]
            ts("dve", xn.t[:], self.xin[i].t[:], self.rs.t[:, i:i + 1], None, ALU.mult, None, [self.xin[i].b, self.rs.b], [xn.b])
            pr = self.trpair
            bb = [BK[2 * pr].b, BK[2 * pr + 1].b]
            for k in range(8):
                P.add("pe", lambda e, k=k: e.transpose(out=pair_ap(pr, 0, 128, k * 128, (k + 1) * 128), in_=xn.t[:, k * 128:(k + 1) * 128],
                                                     identity=ident.t[:]),
                      [xn.b, ident.b], **({"writes": [bb[0]]} if k == 0 else ({"writes": [bb[1]]} if k == 4 else {"pwrites": [bb[k // 4]]})))
            for k in range(8):
                o = self.hT.t[:, k, i * 128:(i + 1) * 128]
                src = pair_ap(pr, 0, 128, k * 128, (k + 1) * 128)
                if k % 2 == 0:
                    act(o, src, AF.Identity, [bb[k // 4], A.b, sh.b], pwrites=[self.hT.b], scale=A.t[:, k:k + 1], bias=sh.t[:, k:k + 1])
                else:
                    ts("dve", o, src, A.t[:, k:k + 1], sh.t[:, k:k + 1], ALU.mult, ALU.add, [bb[k // 4], A.b, sh.b], pwrites=[self.hT.b])

        def full(self, stage, blk):
            for i in range(len(blk[1])):
                self.load(stage, blk, i)
            for i in range(len(blk[1])):
                self.sumsq(blk, i)
            self.rstd(blk)
            for i in range(len(blk[1])):
                self.transpose(blk, i)

    def load_w_cast(dst, k, src_rows_ap, key, ncols):
        c = 0
        while c < ncols:
            n = min(1024, ncols - c)
            dma("pool", dst.t[:, k, c:c + n], src_rows_ap[:, c:c + n], key, pwrites=[dst.b])
            c += n

    def residual_store(stage_src, stage_dst, blk, i, bks, gate_bc, xres, ytmp, final=None):
        t = blk[1][i]
        xr = xres[i % 2]
        ap, tl = tok_src(stage_src, t)
        dma("sp", xr.t[:], ap, "xres%d" % (i % 2), reads=[tl.b] if tl else [], writes=[xr.b])
        for j in range(2):
            yt = ytmp[j]
            tt("dve", yt.t[:], bks[j].ap(), gate_bc.t[:, j * 512:(j + 1) * 512], ALU.mult, [bks[j].b, gate_bc.b], [yt.b])
            tt("pool", xr.t[:, j * 512:(j + 1) * 512], xr.t[:, j * 512:(j + 1) * 512], yt.t[:], ALU.add, [yt.b], pwrites=[xr.b])
        if final is None:
            dst = XS[stage_dst]
            dma("pool", dst.t[t * 128:(t + 1) * 128, :], xr.t[:], "xst%d" % (i % 2), reads=[xr.b], pwrites=[dst.b])
        else:
            fn_bc, fss, frs, fjunk = final
            act(fjunk.t[:], xr.t[:], AF.Square, [xr.b], writes=[fjunk.b], pwrites=[fss.b], accum_out=fss.t[:, 0:1])
            act(frs.t[:], fss.t[:], AF.Sqrt, [fss.b, epsc.b], writes=[frs.b], scale=1.0 / D, bias=epsc.t[:])
            recip(frs.t[:], frs.t[:], [frs.b], writes=[frs.b])
            stt("dve", xr.t[:], xr.t[:], frs.t[:, 0:1], fn_bc.t[:], ALU.mult, ALU.mult, [frs.b, fn_bc.b], writes=[xr.b])
            dma("pool", out.t[t * 128:(t + 1) * 128, :], xr.t[:], "xst%d" % (i % 2), reads=[xr.b], pwrites=[out.b])

    def load_gate_bc(gate_bc, l, n, s, half=False):
        src = modv.t[l, s, (3 * n + 2) * D:(3 * n + 3) * D].partition_broadcast(128)
        dma("sp", gate_bc.t[:], src, "gbc", reads=[modv.b], writes=[gate_bc.b])
        if half:
            P.add("pool", lambda e: e.tensor_scalar(out=gate_bc.t[:], in0=gate_bc.t[:], scalar1=0.5, scalar2=None, op0=ALU.mult),
                  [], writes=[gate_bc.b])

    def pass_ffn(l, which, stage_src, stage_dst, with_ctx, final):
        new_pass()
        n = 0 if which == 1 else 2
        w1b, _ = sba.tile([128, 8, DFF], BF16)
        w3b = TL(sba.tile([128, 8, DFF], BF16)[0].t, w1b.b)
        w2b = TL(sba.tile([128, NF, D], BF16)[0].t, w1b.b)
        W1 = ffn_w["ffn%d_w1" % which][l]
        W3 = ffn_w["ffn%d_w3" % which][l]
        W2 = ffn_w["ffn%d_w2" % which][l]
        wbufs = [Buf() for _ in range(4)]
        for cb in range(3):
            cc0 = cb * 1024
            cn = min(1024, DFF - cc0)
            for k in range(8):
                dma("pool", w1b.t[:, k, cc0:cc0 + cn], W1[k * 128:(k + 1) * 128, cc0:cc0 + cn], "wl%d" % cb, pwrites=[wbufs[cb]])
                dma("pool", w3b.t[:, k, cc0:cc0 + cn], W3[k * 128:(k + 1) * 128, cc0:cc0 + cn], "wl%d" % cb, pwrites=[wbufs[cb]])
        for f in range(NF):
            dma("pool", w2b.t[:, f, :], W2[f * 128:(f + 1) * 128, :], "wl3", pwrites=[wbufs[3]])
        cols = load_modcols(l, n)
        prep = Prep(cols, 2)
        g, _ = sba.tile([128, NF, 512], BF16)
        gate_bc, _ = sba.tile([128, D], F32)
        xres = [sba.tile([128, D], F32)[0] for _ in range(2)]
        if final:
            y0 = sba.tile([128, 512], F32)[0]
            ytmp = [y0, y0]
            s0 = sba.tile([128, 512], BF16)[0]
            ssb = [s0, s0]
        else:
            ytmp = [sba.tile([128, 512], F32)[0] for _ in range(2)]
            ssb = [sba.tile([128, 512], BF16)[0] for _ in range(2)]
        fin = None
        if final:
            fn_bc, _ = sba.tile([128, D], F32)
            dma("sp", fn_bc.t[:], final_norm.partition_broadcast(128), "gbc", writes=[fn_bc.b])
            fss, _ = sba.tile([128, 1], F32)
            frs, _ = sba.tile([128, 1], F32)
            fjunk = TL(nc.alloc_sbuf_tensor_at("fjunk", [128, D], BF16, offset=prep._xnoffs[1]), prep.xn[1].b)
            fin = (fn_bc, fss, frs, fjunk)
        blocks = ([c_block] if with_ctx else []) + x_blocks
        prep.full(stage_src, blocks[0])
        cur_stream = None
        for bi, blk in enumerate(blocks):
            N = len(blk[1]) * 128
            nxt = blocks[bi + 1] if bi + 1 < len(blocks) else None
            if blk[0] != cur_stream:
                cur_stream = blk[0]
                load_gate_bc(gate_bc, l, n, 0 if cur_stream == "x" else 1, half=True)
            for f in range(NF):
                b1 = BK[f % 2]
                b3 = BK[2 + f % 2]
                wbf = wbufs[f // 8]
                for k in range(8):
                    mm(b1.ap(c1=N), w1b.t[:, k, f * 128:(f + 1) * 128], prep.hT.t[:, k, 0:N], k == 0, k == 7, [wbf, prep.hT.b], b1.b)
                for k in range(8):
                    mm(b3.ap(c1=N), w3b.t[:, k, f * 128:(f + 1) * 128], prep.hT.t[:, k, 0:N], k == 0, k == 7, [wbf, prep.hT.b], b3.b)
                s = ssb[f % 2]
                act(s.t[:, 0:N], b1.ap(c1=N), AF.Silu, [b1.b], [s.b])
                tt("dve", g.t[:, f, 0:N], s.t[:, 0:N], b3.ap(c1=N), ALU.mult, [s.b, b3.b], pwrites=[g.b])
                if nxt is not None:
                    nn = len(nxt[1])
                    if f < nn:
                        prep.load(stage_src, nxt, f)
                    if 4 <= f < 4 + nn:
                        prep.sumsq(nxt, f - 4)
                    if f == 9:
                        prep.rstd(nxt)
            if nxt is not None:
                for i in range(len(nxt[1])):
                    prep.transpose(nxt, i)
            for i in range(len(blk[1])):
                bks = [BK[6], BK[7]]
                for j in range(2):
                    for f in range(NF):
                        mm(bks[j].ap(), g.t[:, f, i * 128:(i + 1) * 128], w2b.t[:, f, j * 512:(j + 1) * 512], f == 0, f == NF - 1,
                           [g.b, wbufs[3]], bks[j].b)
                residual_store(stage_src, stage_dst, blk, i, bks, gate_bc, xres, ytmp, fin if blk[0] == "x" else None)

    def pass_proj(l, stage_src, ctx_out):
        new_pass()
        sc = SC[l]
        winb, _ = sba.tile([128, 8, PROJ], BF16)
        winB = Buf()
        for (ca, cbn, key, buf) in ((0, O_GATE, "wl0", winb.b), (O_GATE, PROJ, "wl1", winB)):
            for k in range(8):
                c = ca
                while c < cbn:
                    n = min(1024, cbn - c)
                    dma("pool", winb.t[:, k, c:c + n], w_in[l][k * 128:(k + 1) * 128, c:c + n], key, pwrites=[buf])
                    c += n
        wqb = TL(sba.tile([128, 3, 768], BF16)[0].t, winb.b)
        wkvb = TL(sba.tile([128, 2, 1024], BF16)[0].t, winb.b)
        perms = {}
        for nm in ("k_p16", "k_p32", "k_p64"):
            pt = TL(sba.tile([128, 128], BF16)[0].t, winb.b)
            dma("pool", pt.t[:], k_perm[nm], "wl", pwrites=[winb.b])
            perms[nm] = pt
        cols = load_modcols(l, 1)
        qg, _ = sba.tile([128, 3], F32)
        kg, _ = sba.tile([128, 2], F32)
        dma("sp", qg.t[:], mla_q_norm[l].rearrange("(k p) -> p k", p=128), "mc", writes=[qg.b], allow_slow_non_contiguous=True)
        dma("sp", kg.t[:], mla_kv_norm[l].rearrange("(k p) -> p k", p=128), "mc", writes=[kg.b], allow_slow_non_contiguous=True)
        prep = Prep(cols, 2)
        stq = TL(nc.alloc_sbuf_tensor_at("stq%d" % l, [128, 3, 768], F32, offset=prep._offs[0]))
        stkv = TL(nc.alloc_sbuf_tensor_at("stkv%d" % l, [128, 2, 1024], F32, offset=prep._xnoffs[0]))
        qbufs = [prep.xin[0].b, prep.xin[1].b, prep.xin[2].b]
        kvbufs = [prep.xn[0].b, prep.xn[1].b]
        for k in range(3):
            dma("sp", stq.t[:, k, :], mla_w_qb[l][k * 128:(k + 1) * 128, :], "xin0", pwrites=qbufs)
        for k in range(2):
            dma("sp", stkv.t[:, k, :], mla_w_kvb[l][k * 128:(k + 1) * 128, :], "xin1", pwrites=kvbufs)
        qsc = (64 + 32) ** -0.5
        for k in range(3):
            ts("dve", wqb.t[:, k, :], stq.t[:, k, :], qg.t[:, k:k + 1], qsc, ALU.mult, ALU.mult, qbufs + [qg.b], pwrites=[winb.b])
        for k in range(2):
            ts("dve", wkvb.t[:, k, :], stkv.t[:, k, :], kg.t[:, k:k + 1], None, ALU.mult, None, kvbufs + [kg.b], pwrites=[winb.b])
        for (o, w) in ((O_DQ, 512), (O_RK, 256)):
            P.add("pool", lambda e, o=o, w=w: e.tensor_scalar(out=winb.t[:, :, o:o + w], in0=winb.t[:, :, o:o + w], scalar1=0.125, scalar2=None,
                                                          op0=ALU.mult), [], pwrites=[winb.b])
        W = winb.b
        latq, _ = sba.tile([128, 3, 512], BF16)
        latkv, _ = sba.tile([128, 2, 512], BF16)
        sqt = [sba.tile([128, 512], BF16)[0] for _ in range(2)]
        rq_rep, _ = sba.tile([128, 512], F32)
        rkv_rep, _ = sba.tile([128, 512], F32)
        rkv_tm, _ = sba.tile([128, 4], F32)
        tabbuf = Buf()
        tabs = {nm: TL(sba.tile([128, 512], F32)[0].t, tabbuf) for nm in k_tab}
        asb = [sba.tile([128, 512], BF16)[0] for _ in range(2)]
        m1, _ = sba.tile([128, 512], F32)
        m2, _ = sba.tile([128, 512], F32)
        ost = [sba.tile([128, 512], BF16)[0] for _ in range(3)]
        vst = [sba.tile([128, 4, 512], BF16)[0] for _ in range(2)]
        kts, _ = sba.tile([128, 4, 256], BF16)
        rkf = [sba.tile([128, 512], BF16)[0] for _ in range(2)]
        cnt = {"o": 0, "a": 0, "bk": 0, "v": 0}

        pending = []

        def flush():
            while pending:
                pending.pop(0)()

        def nbk():
            cnt["bk"] += 1
            return BK[cnt["bk"] % int("6")]

        def nost():
            cnt["o"] += 1
            return ost[cnt["o"] % 3], "ost%d" % (cnt["o"] % 3)

        def fm_group(lhs_fn, nk, M, rhs_fn, N, rb):
            bk = nbk()
            for k in range(nk):
                mm(bk.ap(0, M, 0, N), lhs_fn(k), rhs_fn(k), k == 0, k == nk - 1, rb, bk.b)
            if "1" == "1":
                flush()
            return bk

        blocks = [c_block] + x_blocks
        prep.full(stage_src, blocks[0])
        for bi, blk in enumerate(blocks):
            isx = blk[0] == "x"
            nt = len(blk[1])
            N = nt * 128
            c0 = blk[1][0] * 128
            x0 = c0
            hT = prep.hT
            if isx:
                for nm in k_tab:
                    dma("sp", tabs[nm].t[:, 0:N], k_tab[nm][:, x0:x0 + N], "tab", pwrites=[tabbuf])
            nxt = blocks[bi + 1] if bi + 1 < len(blocks) else None
            EARLY = "1" == "1"
            if nxt is not None and EARLY:
                for i in range(len(nxt[1])):
                    prep.load(stage_src, nxt, i)

            def win_group(o, M=128):
                return fm_group(lambda k: winb.t[:, k, o:o + M], 8, M, lambda k: hT.t[:, k, 0:N], N, [W if o < O_GATE else winB, hT.b])

            def rope_out(bk, M, permname, cname, sname, dst_list, p0=0, otile=None):
                if otile is None:
                    o, okey = nost()
                else:
                    o, okey = otile
                if isx:
                    a = asb[cnt["a"] % 2]
                    cnt["a"] += 1
                    cp("act", a.t[p0:p0 + M, 0:N], bk.ap(p0, p0 + M, 0, N), [bk.b], [a.b])

                    def tail():
                        pm = perms[permname]
                        b2 = nbk()
                        mm(b2.ap(p0, p0 + M, 0, N), pm.t[p0:p0 + M, p0:p0 + M], a.t[p0:p0 + M, 0:N], True, True, [W, a.b], b2.b)
                        tt("dve", m1.t[p0:p0 + M, 0:N], a.t[p0:p0 + M, 0:N], tabs[cname].t[p0:p0 + M, 0:N], ALU.mult, [a.b, tabs[cname].b], [m1.b])
                        tt("dve", m2.t[p0:p0 + M, 0:N], b2.ap(p0, p0 + M, 0, N), tabs[sname].t[p0:p0 + M, 0:N], ALU.mult, [b2.b, tabs[sname].b], [m2.b])
                        tt("dve", o.t[p0:p0 + M, 0:N], m1.t[p0:p0 + M, 0:N], m2.t[p0:p0 + M, 0:N], ALU.add, [m1.b, m2.b], [o.b])
                        for (dst_tl, dst_ap, r0, r1) in dst_list:
                            dma("pool", dst_ap, o.t[r0:r1, 0:N], okey, reads=[o.b], pwrites=[dst_tl.b])
                    pending.append(tail)
                    if "1" != "1":
                        flush()
                else:
                    cp("act", o.t[p0:p0 + M, 0:N], bk.ap(p0, p0 + M, 0, N), [bk.b], [o.b])
                    for (dst_tl, dst_ap, r0, r1) in dst_list:
                        dma("pool", dst_ap, o.t[r0:r1, 0:N], okey, reads=[o.b], pwrites=[dst_tl.b])

            for (lat, o_lat, nch, rrep, dim) in ((latq, O_QLAT, 3, rq_rep, 384), (latkv, O_KVLAT, 2, rkv_rep, 256)):
                ssb = BK[6]
                for c in range(nch):
                    bk = win_group(o_lat + c * 128)
                    cp("act", lat.t[:, c, 0:N], bk.ap(c1=N), [bk.b], pwrites=[lat.b])
                    sq = sqt[c % 2]
                    act(sq.t[:, 0:N], bk.ap(c1=N), AF.Square, [bk.b], [sq.b])
                    mm(ssb.ap(c1=N), onesb.t[:], sq.t[:, 0:N], c == 0, c == nch - 1, [onesb.b, sq.b], ssb.b)
                    if lat is latkv:
                        for i in range(nt):
                            mmp(BK[7].ap(0, 128, i, i + 1), sq.t[:, i * 128:(i + 1) * 128], onesb.t[:, 0:1], c == 0 and i == 0,
                                c == nch - 1 and i == nt - 1, [onesb.b, sq.b], BK[7].b)
                act(rrep.t[:, 0:N], ssb.ap(c1=N), AF.Sqrt, [ssb.b, epsc.b], [rrep.b], scale=1.0 / dim, bias=epsc.t[:])
                recip(rrep.t[:, 0:N], rrep.t[:, 0:N], [rrep.b], writes=[rrep.b])
            act(rkv_tm.t[:, 0:nt], BK[7].ap(0, 128, 0, nt), AF.Sqrt, [BK[7].b, epsc.b], [rkv_tm.b], scale=1.0 / 256, bias=epsc.t[:])
            recip(rkv_tm.t[:, 0:nt], rkv_tm.t[:, 0:nt], [rkv_tm.b], writes=[rkv_tm.b])

            bk = win_group(O_KROPE, 32)
            rope_out(bk, 32, "k_p16", "k_cm", "k_sm", [(sc["KMr"], sc["KMr"].t[:, c0:c0 + N], 0, 32)])
            for c in range(4):
                bk = win_group(O_DQ + c * 128)
                rope_out(bk, 128, "k_p32", "k_cd", "k_sd", [(sc["QD"], sc["QD"].t[c * 128:(c + 1) * 128, c0:c0 + N], 0, 128)])
            for c in range(4):
                bk = win_group(O_DK + c * 128)
                rope_out(bk, 128, "k_p32", "k_cd", "k_sd", [(sc["KD"], sc["KD"].t[c * 128:(c + 1) * 128, c0:c0 + N], 0, 128)])
            for c in range(2):
                bk = win_group(O_RQ + c * 128)
                rope_out(bk, 128, "k_p64", "k_cr", "k_sr", [(sc["RQ"], sc["RQ"].t[c * 128:(c + 1) * 128, c0:c0 + N], 0, 128)])
            for c in range(2):
                bk = win_group(O_RK + c * 128)
                rope_out(bk, 128, "k_p64", "k_cr", "k_sr", [(sc["RK"], sc["RK"].t[c * 128:(c + 1) * 128, c0:c0 + N], 0, 128)],
                         otile=(rkf[c], "rkf%d" % c))
            if isx or ctx_out:
                for c in range(4):
                    bk = win_group(O_RG + c * 128)
                    o, okey = nost()
                    act(o.t[:, 0:N], bk.ap(c1=N), AF.Silu, [bk.b], [o.b])
                    dma("pool", sc["RG"].t[c * 128:(c + 1) * 128, c0:c0 + N], o.t[:, 0:N], okey, reads=[o.b], pwrites=[sc["RG"].b])
                for c in range(24):
                    bk = win_group(O_GATE + c * 128)
                    o, okey = nost()
                    act(o.t[:, 0:N], bk.ap(c1=N), AF.Sigmoid, [bk.b], [o.b])
                    dma("pool", sc["GATE"].t[c * 128:(c + 1) * 128, c0:c0 + N], o.t[:, 0:N], okey, reads=[o.b], pwrites=[sc["GATE"].b])
            flush()
            tb = BK[4]
            tbv = PA[2].t[:, 0:512].bitcast(BF16)
            first = True
            for i in range(nt):
                for c in range(2):
                    P.add("pe", lambda e, i=i, c=c: e.transpose(out=tbv[:, (i * 2 + c) * 128:(i * 2 + c + 1) * 128],
                                                           in_=rkf[c].t[:, i * 128:(i + 1) * 128], identity=identb.t[:]),
                          [rkf[c].b, identb.b], **({"writes": [tb.b]} if first else {"pwrites": [tb.b]}))
                    first = False
            cp("dve", kts.t[:, 0:nt, :], tbv[:, 0:nt * 256].rearrange("p (i c) -> p i c", c=256), [tb.b], [kts.b])
            dma("pool", sc["RKt"].t[c0:c0 + N, :].rearrange("(i p) c -> p i c", p=128), kts.t[:, 0:nt, :], "kts", reads=[kts.b],
                pwrites=[sc["RKt"].b])
            if nxt is not None and EARLY:
                for i in range(len(nxt[1])):
                    prep.sumsq(nxt, i)
                prep.rstd(nxt)
            for (o_v, dst) in ((O_DV, sc["VD"]), (O_RV, sc["RV"])):
                v = vst[cnt["v"] % 2]
                vkey = "vst%d" % (cnt["v"] % 2)
                cnt["v"] += 1
                for i in range(nt):
                    bk = nbk()
                    for k in range(8):
                        mm(bk.ap(), hT.t[:, k, i * 128:(i + 1) * 128], winb.t[:, k, o_v:o_v + 512], k == 0, k == 7, [W, hT.b], bk.b)
                    cp("act" if i % 2 == 0 else "dve", v.t[:, i, :], bk.ap(), [bk.b], pwrites=[v.b])
                dma("pool", dst.t[c0:c0 + N, :].rearrange("(i p) c -> p i c", p=128), v.t[:, 0:nt, :], vkey, reads=[v.b], pwrites=[dst.b])
            if nxt is not None and EARLY:
                for i in range(len(nxt[1])):
                    prep.transpose(nxt, i)
            for h in range(8):
                bk = fm_group(lambda k: wqb.t[:, k, h * 96:h * 96 + 96], 3, 96, lambda k: latq.t[:, k, 0:N], N, [W, latq.b])
                o, okey = nost()
                if isx:
                    a = asb[cnt["a"] % 2]
                    cnt["a"] += 1
                    cp("act", a.t[0:96, 0:N], bk.ap(0, 96, 0, N), [bk.b], [a.b])
                    MLT = "0" == "1"
                    if MLT:
                        tt("dve", o.t[0:64, 0:N], bk.ap(0, 64, 0, N), rq_rep.t[0:64, 0:N], ALU.mult, [bk.b, rq_rep.b], [o.b])

                    def tail(a=a, o=o, okey=okey, h=h, bk=bk, MLT=MLT):
                        b2 = nbk()
                        mm(b2.ap(0, 96, 0, N), perms["k_p16"].t[0:96, 0:96], a.t[0:96, 0:N], True, True, [W, a.b], b2.b)
                        tt("dve", m1.t[64:96, 0:N], a.t[64:96, 0:N], tabs["k_cm"].t[64:96, 0:N], ALU.mult, [a.b, tabs["k_cm"].b], [m1.b])
                        tt("dve", m2.t[64:96, 0:N], b2.ap(64, 96, 0, N), tabs["k_sm"].t[64:96, 0:N], ALU.mult, [b2.b, tabs["k_sm"].b], [m2.b])
                        tt("dve", m1.t[64:96, 0:N], m1.t[64:96, 0:N], m2.t[64:96, 0:N], ALU.add, [m2.b], writes=[m1.b])
                        if not MLT:
                            tt("dve", o.t[0:64, 0:N], bk.ap(0, 64, 0, N), rq_rep.t[0:64, 0:N], ALU.mult, [bk.b, rq_rep.b], [o.b])
                        tt("dve", o.t[64:96, 0:N], m1.t[64:96, 0:N], rq_rep.t[64:96, 0:N], ALU.mult, [m1.b, rq_rep.b], pwrites=[o.b])
                        dma("pool", sc["QM"].t[h, :, c0:c0 + N], o.t[0:96, 0:N], okey, reads=[o.b], pwrites=[sc["QM"].b])
                    pending.append(tail)
                    if "1" != "1":
                        flush()
                else:
                    tt("dve", o.t[0:96, 0:N], bk.ap(0, 96, 0, N), rq_rep.t[0:96, 0:N], ALU.mult, [bk.b, rq_rep.b], [o.b])
                    dma("pool", sc["QM"].t[h, :, c0:c0 + N], o.t[0:96, 0:N], okey, reads=[o.b], pwrites=[sc["QM"].b])
            for h in range(8):
                bk = fm_group(lambda k: wkvb.t[:, k, h * 128:h * 128 + 64], 2, 64, lambda k: latkv.t[:, k, 0:N], N, [W, latkv.b])
                o, okey = nost()
                tt("dve", o.t[0:64, 0:N], bk.ap(0, 64, 0, N), rkv_rep.t[0:64, 0:N], ALU.mult, [bk.b, rkv_rep.b], [o.b])
                dma("pool", sc["KMn"].t[h, :, c0:c0 + N], o.t[0:64, 0:N], okey, reads=[o.b], pwrites=[sc["KMn"].b])
            v = vst[cnt["v"] % 2]
            vkey = "vst%d" % (cnt["v"] % 2)
            cnt["v"] += 1
            wv = wkvb.t[:].rearrange("p k (h c) -> p k h c", c=128)
            for i in range(nt):
                bk = nbk()
                for k in range(2):
                    mm(bk.ap().rearrange("p (h c) -> p h c", c=64), latkv.t[:, k, i * 128:(i + 1) * 128], wv[:, k, :, 64:128], k == 0, k == 1,
                       [W, latkv.b], bk.b)
                ts("dve", v.t[:, i, :], bk.ap(), rkv_tm.t[:, i:i + 1], None, ALU.mult, None, [bk.b, rkv_tm.b], pwrites=[v.b])
            dma("pool", sc["VM"].t[c0:c0 + N, :].rearrange("(i p) c -> p i c", p=128), v.t[:, 0:nt, :], vkey, reads=[v.b], pwrites=[sc["VM"].b])
            flush()
            if nxt is not None and not EARLY:
                prep.full(stage_src, nxt)

    def pass_attn(l, ctx_out):
        new_pass()
        sc = SC[l]
        lam_init = 0.8 - 0.6 * math.exp(-0.3 * l)
        QW = T if ctx_out else S
        KT = [sba.tile([128, T], BF16)[0] for _ in range(2)]
        QT = [sba.tile([128, T], BF16)[0] for _ in range(2)]
        VT = [sba.tile([128, NKT, 128], BF16)[0] for _ in range(2)]
        pt = [sba.tile([128, 1024], BF16)[0] for _ in range(4)]
        accD, _ = sba.tile([128, 1024], F32)
        accP, _ = sba.tile([128, 1024], F32)
        rc, _ = sba.tile([128, 512], F32)
        on = [sba.tile([128, 512], F32)[0] for _ in range(2)]
        dsq, _ = sba.tile([128, 512], F32)
        drs, _ = sba.tile([128, 512], F32)
        ob = [sba.tile([128, 512], BF16)[0] for _ in range(2)]
        dl, _ = sba.tile([128, 4, 64], F32)
        dma("sp", dl.t[:], diff_lambda[l].rearrange("a b -> (a b)").partition_broadcast(128).rearrange("p (a b) -> p a b", b=64),
            "mc", writes=[dl.b])
        dpr, _ = sba.tile([128, 2, 64], F32)
        tt("dve", dpr.t[:, 0, :], dl.t[:, 0, :], dl.t[:, 1, :], ALU.mult, [dl.b], pwrites=[dpr.b])
        tt("dve", dpr.t[:, 1, :], dl.t[:, 2, :], dl.t[:, 3, :], ALU.mult, [dl.b], pwrites=[dpr.b])
        dsum, _ = sba.tile([128, 2], F32)
        P.add("dve", lambda e: e.reduce_sum(out=dsum.t[:], in_=dpr.t[:], axis=mybir.AxisListType.X), [dpr.b], [dsum.b])
        dex, _ = sba.tile([128, 2], F32)
        act(dex.t[:], dsum.t[:], AF.Exp, [dsum.b], [dex.b])
        nlam, _ = sba.tile([128, 1], F32)
        stt("dve", nlam.t[:], dex.t[:, 1:2], -lam_init, dex.t[:, 0:1], ALU.add, ALU.subtract, [dex.b], [nlam.b])
        dng, _ = sba.tile([128, 1], F32)
        dma("sp", dng.t[:], diff_norm[l].rearrange("(p o) -> p o", o=1), "mc", writes=[dng.b], allow_slow_non_contiguous=True)
        ts("dve", dng.t[:], dng.t[:], 1.0 - lam_init, None, ALU.mult, None, [], writes=[dng.b])

        units = [("m", h) for h in range(8)] + [("d", h) for h in range(4)]
        grp = {"n": 0, "p": 0}
        retgen = ret_gen(l, ctx_out)

        def pull():
            pass

        def rstd_lnexp(out_ap, in_ap, dim, reads, wbuf):
            act(out_ap, in_ap, AF.Ln, list(reads) + [epsc.b], [wbuf], scale=1.0 / dim, bias=epsc.t[:])
            act(out_ap, out_ap, AF.Exp, [], writes=[wbuf], scale=-0.5)

        POOLACC = True

        diff_pending = []

        def flush_diff_post():
            while diff_pending:
                diff_pending.pop(0)()

        def diff_block(K, Q, V, q0, N, kts):
            O1, O2, S1 = BK[6], BK[7], BK[4]
            P.add("dve", lambda e: e.memset(accD.t[:, 0:512], 0.0), [], writes=[accD.b])

            def issue_qk(kt):
                gi = grp["n"] % 2
                grp["n"] += 1
                for j in range(2):
                    bk = BK[2 * gi + j]
                    mm(bk.ap(c1=N), K.t[j * 64:(j + 1) * 64, kt * 128:(kt + 1) * 128], Q.t[j * 64:(j + 1) * 64, q0:q0 + N], True, True,
                       [K.b, Q.b], bk.b)
                return gi

            def issue_rest(kt, gi, idx, first, last):
                p = pt[grp["p"] % 4]
                grp["p"] += 1
                rb = [BK[2 * gi].b, BK[2 * gi + 1].b]
                if N == 512:
                    act(p.t[:, :], pair_ap(gi), AF.Exp, rb, [p.b])
                else:
                    act(p.t[:, 0:N], pair_ap(gi, 0, 128, 0, N), AF.Exp, [rb[0]], [p.b])
                    act(p.t[:, 512:512 + N], pair_ap(gi, 0, 128, 512, 512 + N), AF.Exp, [rb[1]], pwrites=[p.b])
                mm(O1.ap(c1=N), V.t[:, kt, :], p.t[:, 0:N], first, last, [V.b, p.b], O1.b)
                mm(S1.ap(c1=N), onesb.t[:], p.t[:, 0:N], first, last, [onesb.b, p.b], S1.b)
                mm(O2.ap(c1=N), V.t[:, kt, :], p.t[:, 512:512 + N], first, last, [V.b, p.b], O2.b)
                tt("dve", accD.t[:, 0:N], accD.t[:, 0:N], p.t[:, 512:512 + N], ALU.add, [p.b], writes=[accD.b])
                if idx == 2:
                    flush_diff_post()

            q = []
            for idx, kt in enumerate(kts):
                gi = issue_qk(kt)
                q.append((kt, gi, idx, idx == 0, idx == len(kts) - 1))
                if len(q) > 1:
                    issue_rest(*q.pop(0))
            while q:
                issue_rest(*q.pop(0))
            flush_diff_post()
            S2 = BK[5]
            mm(S2.ap(c1=N), onesf.t[:], accD.t[:, 0:N], True, True, [onesf.b, accD.b], S2.b)
            for j, (O, Sb) in enumerate(((O1, S1), (O2, S2))):
                recip(rc.t[:, 0:N], Sb.ap(c1=N), [Sb.b], [rc.b])
                tt("dve", on[j].t[:, 0:N], O.ap(c1=N), rc.t[:, 0:N], ALU.mult, [O.b, rc.b], [on[j].b])

        def load_unit(ui):
            kind, h = units[ui]
            s = ui % 2
            K, Q, V = KT[s], QT[s], VT[s]
            if kind == "m":
                dma("sp", K.t[0:64, :], sc["KMn"].t[h], "ak%d" % s, reads=[sc["KMn"].b], pwrites=[K.b])
                dma("sp", K.t[64:96, :], sc["KMr"].t[:, :], "ak%d" % s, reads=[sc["KMr"].b], pwrites=[K.b])
                dma("sp", Q.t[0:96, 0:QW], sc["QM"].t[h, :, 0:QW], "aq%d" % s, reads=[sc["QM"].b], writes=[Q.b])
                for t0 in range(0, NKT, 16):
                    t1 = min(NKT, t0 + 16)
                    dma("sp", V.t[:, t0:t1, 0:64], sc["VM"].t[t0 * 128:t1 * 128, h * 64:(h + 1) * 64].rearrange("(t p) c -> p t c", p=128),
                        "av%d" % s, reads=[sc["VM"].b], pwrites=[V.b])
                P.add("pool", lambda e: e.memset(V.t[:, :, 64:128], 1.0), [], pwrites=[V.b])
            else:
                dma("sp", K.t[:, :], sc["KD"].t[h * 128:(h + 1) * 128, :], "ak%d" % s, reads=[sc["KD"].b], writes=[K.b])
                dma("sp", Q.t[:, 0:QW], sc["QD"].t[h * 128:(h + 1) * 128, 0:QW], "aq%d" % s, reads=[sc["QD"].b], writes=[Q.b])
                for t0 in range(0, NKT, 16):
                    t1 = min(NKT, t0 + 16)
                    dma("sp", V.t[:, t0:t1, :], sc["VD"].t[t0 * 128:t1 * 128, h * 128:(h + 1) * 128].rearrange("(t p) c -> p t c", p=128),
                        "av%d" % s, reads=[sc["VD"].b], pwrites=[V.b])

        def softmax_block(K, Q, V, p0, p1, q0, N, kts, obk, sbk):
            groups = [kts[i:i + 2] for i in range(0, len(kts), 2)]
            pend = []

            def issue_qk(g):
                gi = grp["n"] % 3
                grp["n"] += 1
                for jj, kt in enumerate(g):
                    bk = BK[2 * gi + jj]
                    mm(bk.ap(c1=N), K.t[p0:p1, kt * 128:(kt + 1) * 128], Q.t[p0:p1, q0:q0 + N], True, True, [K.b, Q.b], bk.b)
                return gi

            def issue_rest(g, gi, first, last):
                p = pt[grp["p"] % 4]
                grp["p"] += 1
                ng = len(g)
                rb = [BK[2 * gi + jj].b for jj in range(ng)]
                if N == 512:
                    act(p.t[:, 0:ng * 512], pair_ap(gi, 0, 128, 0, ng * 512), AF.Exp, rb, [p.b])
                else:
                    for jj in range(ng):
                        act(p.t[:, jj * 512:jj * 512 + N], pair_ap(gi, 0, 128, jj * 512, jj * 512 + N), AF.Exp, [rb[jj]],
                            **({"writes": [p.b]} if jj == 0 else {"pwrites": [p.b]}))
                for jj, kt in enumerate(g):
                    st = first and jj == 0
                    sp_ = last and jj == ng - 1
                    mm(obk.ap(c1=N), V.t[:, kt, :], p.t[:, jj * 512:jj * 512 + N], st, sp_, [V.b, p.b], obk.b)
                    if sbk is not None:
                        mm(sbk.ap(c1=N), onesb.t[:], p.t[:, jj * 512:jj * 512 + N], st, sp_, [onesb.b, p.b], sbk.b)
                pull()

            q = []
            for gidx, g in enumerate(groups):
                gi = issue_qk(g)
                q.append((g, gi, gidx == 0, gidx == len(groups) - 1))
                if len(q) > 2:
                    issue_rest(*q.pop(0))
            while q:
                issue_rest(*q.pop(0))

        def qblocks():
            res = [(b * 512, 512, list(range(NKT))) for b in range(S // 512)]
            if ctx_out:
                res.append((S, 256, [NXT, NXT + 1]))
            return res

        load_unit(0)
        for ui, (kind, h) in enumerate(units):
            s = ui % 2
            K, Q, V = KT[s], QT[s], VT[s]
            if ui + 1 < len(units):
                load_unit(ui + 1)
            for qi, (q0, N, kts) in enumerate(qblocks()):
                if kind == "m":
                    obk = BK[6 + qi % 2]
                    softmax_block(K, Q, V, 0, 96, q0, N, kts, obk, None)
                    recip(rc.t[0:64, 0:N], obk.ap(64, 128, 0, N), [obk.b], [rc.b])
                    o = ob[0]
                    tt("dve", o.t[0:64, 0:N], obk.ap(0, 64, 0, N), rc.t[0:64, 0:N], ALU.mult, [obk.b, rc.b], [o.b])
                    dma("pool", sc["YM"].t[h * 64:(h + 1) * 64, q0:q0 + N], o.t[0:64, 0:N], "ob0", reads=[o.b], pwrites=[sc["YM"].b])
                else:
                    diff_block(K, Q, V, q0, N, kts)

                    def post(h=h, q0=q0, N=N):
                        stt("dve", on[0].t[:, 0:N], on[1].t[:, 0:N], nlam.t[:, 0:1], on[0].t[:, 0:N], ALU.mult, ALU.add, [on[1].b, nlam.b],
                            writes=[on[0].b])
                        tt("dve", dsq.t[:, 0:N], on[0].t[:, 0:N], on[0].t[:, 0:N], ALU.mult, [on[0].b], [dsq.b])
                        nb = BK[5]
                        mm(nb.ap(c1=N), onesf.t[:], dsq.t[:, 0:N], True, True, [onesf.b, dsq.b], nb.b)
                        rstd_lnexp(drs.t[:, 0:N], nb.ap(c1=N), 128, [nb.b], drs.b)
                        o = ob[1]
                        stt("dve", o.t[:, 0:N], on[0].t[:, 0:N], dng.t[:, 0:1], drs.t[:, 0:N], ALU.mult, ALU.mult, [on[0].b, dng.b, drs.b], [o.b])
                        dma("pool", sc["YD"].t[h * 128:(h + 1) * 128, q0:q0 + N], o.t[:, 0:N], "ob1", reads=[o.b], pwrites=[sc["YD"].b])
                    diff_pending.append(post)
        flush_diff_post()
        for _ in retgen:
            pass

    def ret_gen(l, ctx_out):
        sc = SC[l]
        RPE = "dve"
        rd, _ = sba.tile([128, 8], F32)
        dma("sp", rd.t[:], ret_decay[l].rearrange("a b -> (a b)").partition_broadcast(128), "mc", writes=[rd.b])
        lg, _ = sba.tile([128, 8], F32)
        act(lg.t[:], rd.t[:], AF.Exp, [rd.b], [lg.b])
        ts("dve", lg.t[:], lg.t[:], -1.0, None, ALU.mult, None, [], writes=[lg.b])
        cdec, _ = sba.tile([128, 8], F32)
        act(cdec.t[:], lg.t[:], AF.Exp, [lg.b], [cdec.b], scale=128.0)
        r4, _ = sba.tile([128, 4, 128], F32)
        dma("sp", r4.t[:], k_ret4.rearrange("a p q -> p a q"), "mc", writes=[r4.b])
        qdc, _ = sba.tile([128, 2, 128], F32)
        dma("sp", qdc.t[:], k_qd.rearrange("a p q -> p a q"), "mc", writes=[qdc.b])
        kdc, _ = sba.tile([128, 2], F32)
        dma("sp", kdc.t[:], k_kd, "mc", writes=[kdc.b])
        maskT, _ = sba.tile([128, 2, 4, 128], F32)
        qdT, _ = sba.tile([128, 2, 4, 128], F32)
        kdT, _ = sba.tile([128, 2, 4], F32)
        for d in range(2):
            for h in range(4):
                i = d * 4 + h
                act(maskT.t[:, d, h, :], r4.t[:, 2 * d, :], AF.Exp, [r4.b, lg.b], pwrites=[maskT.b], scale=lg.t[:, i:i + 1])
                tt("dve", maskT.t[:, d, h, :], maskT.t[:, d, h, :], r4.t[:, 2 * d + 1, :], ALU.mult, [r4.b], pwrites=[maskT.b])
                act(qdT.t[:, d, h, :], qdc.t[:, d, :], AF.Exp, [qdc.b, lg.b], pwrites=[qdT.b], scale=lg.t[:, i:i + 1])
            act(kdT.t[:, d, :], lg.t[:, d * 4:d * 4 + 4], AF.Exp, [lg.b, kdc.b], pwrites=[kdT.b], scale=kdc.t[:, d:d + 1])
        rng_col, _ = sba.tile([128, 1], F32)
        dma("sp", rng_col.t[:], ret_norm[l].rearrange("(p o) -> p o", o=1), "mc", writes=[rng_col.b], allow_slow_non_contiguous=True)

        qf = [sba.tile([64, 4, 128], BF16)[0] for _ in range(2)]
        kf = [sba.tile([64, 4, 128], BF16)[0] for _ in range(2)]
        ktm = [sba.tile([128, 256], BF16)[0] for _ in range(2)]
        vtm = [sba.tile([128, 512], BF16)[0] for _ in range(2)]
        am, _ = sba.tile([128, 4, 128], BF16)
        kdm, _ = sba.tile([128, 4, 64], BF16)
        qdm, _ = sba.tile([64, 4, 128], BF16)
        Sf, _ = sba.tile([64, 4, 128], F32)
        Sb16, _ = sba.tile([64, 4, 128], BF16)
        osb = [sba.tile([128, 4, 128], F32)[0] for _ in range(2)]
        ofl = [sba.tile([128, 4, 128], F32)[0] for _ in range(2)]
        gsb = [sba.tile([128, 4, 128], BF16)[0] for _ in range(2)]
        sqs, _ = sba.tile([128, 512], F32)
        rrs, _ = sba.tile([128, 512], F32)
        yo = [sba.tile([128, 4, 128], BF16)[0] for _ in range(2)]

        yield
        fwd = [NXT, NXT + 1] + list(range(NXT))
        bwd = [NXT + 1, NXT] + list(range(NXT - 1, -1, -1))
        step = 0
        posts = []

        def flush_posts():
            while posts:
                posts.pop(0)()

        for d, order in ((0, fwd), (1, bwd)):
            P.add("dve", lambda e: e.memset(Sf.t[:], 0.0), [], writes=[Sf.b])
            P.add("dve", lambda e: e.memset(Sb16.t[:], 0.0), [], writes=[Sb16.b])
            for t in order:
                s = step % 2
                step += 1
                isx = t < NXT
                need_out = isx or ctx_out
                c0 = t * 128
                dma("sp", qf[s].t[:], sc["RQ"].t[:, c0:c0 + 128].rearrange("(h d) q -> d h q", d=64), "rq%d" % s, reads=[sc["RQ"].b],
                    writes=[qf[s].b])
                dma("sp", kf[s].t[:], sc["RK"].t[:, c0:c0 + 128].rearrange("(h d) q -> d h q", d=64), "rk%d" % s, reads=[sc["RK"].b],
                    writes=[kf[s].b])
                dma("sp", ktm[s].t[:], sc["RKt"].t[c0:c0 + 128, :], "rkt%d" % s, reads=[sc["RKt"].b], writes=[ktm[s].b])
                dma("sp", vtm[s].t[:], sc["RV"].t[c0:c0 + 128, :], "rv%d" % s, reads=[sc["RV"].b], writes=[vtm[s].b])
                obk = BK[1] if s == 0 else BK[3]
                if need_out and d == 1:
                    dma("sp", ofl[s].t[:], sc["OF"].t[:, t, :].rearrange("p (h q) -> p h q", q=128), "ofl%d" % s, reads=[sc["OF"].b],
                        writes=[ofl[s].b])
                    dma("sp", gsb[s].t[:], sc["RG"].t[:, c0:c0 + 128].rearrange("(h p) q -> p h q", p=128), "gsb%d" % s, reads=[sc["RG"].b],
                        writes=[gsb[s].b])
                if need_out:
                    ab = BK[0]
                    for h in range(4):
                        mmp(ab.ap(0, 128, h * 128, (h + 1) * 128), kf[s].t[:, h, :], qf[s].t[:, h, :], True, True, [kf[s].b, qf[s].b], ab.b)
                    tt("dve", am.t[:], ab.ap().rearrange("p (h q) -> p h q", q=128), maskT.t[:, d, :, :], ALU.mult, [ab.b, maskT.b], [am.b])
                    tt(RPE, qdm.t[:], qf[s].t[:], qdT.t[0:64, d, :, :], ALU.mult, [qf[s].b, qdT.b], [qdm.b])
                    for h in range(4):
                        mmp(obk.ap(0, 128, h * 128, (h + 1) * 128), vtm[s].t[:, h * 128:(h + 1) * 128], am.t[:, h, :], True, False,
                            [vtm[s].b, am.b], obk.b)
                        mmp(obk.ap(0, 128, h * 128, (h + 1) * 128), Sb16.t[:, h, :], qdm.t[:, h, :], False, True, [Sb16.b, qdm.b], obk.b)
                tt(RPE, kdm.t[:], ktm[s].t[:].rearrange("p (h d) -> p h d", d=64), kdT.t[:, d, :].unsqueeze(2).to_broadcast([128, 4, 64]),
                   ALU.mult, [ktm[s].b, kdT.b], [kdm.b])
                ub = BK[2]
                for h in range(4):
                    mmp(ub.ap(0, 64, h * 128, (h + 1) * 128), kdm.t[:, h, :], vtm[s].t[:, h * 128:(h + 1) * 128], True, True, [kdm.b, vtm[s].b], ub.b)
                for h in range(4):
                    i = d * 4 + h
                    stt("dve", Sf.t[:, h, :], Sf.t[:, h, :], cdec.t[0:64, i:i + 1], ub.ap(0, 64, h * 128, (h + 1) * 128), ALU.mult, ALU.add,
                        [ub.b, cdec.b], pwrites=[Sf.b])
                cp("act", Sb16.t[:], Sf.t[:], [Sf.b], [Sb16.b])
                flush_posts()
                if not need_out:
                    continue

                def post(s=s, t=t, c0=c0, d=d, obk=obk):
                    o = osb[s]
                    if d == 0:
                        cp("act", o.t[:], obk.ap().rearrange("p (h q) -> p h q", q=128), [obk.b], [o.b])
                        dma("pool", sc["OF"].t[:, t, :].rearrange("p (h q) -> p h q", q=128), o.t[:], "osb%d" % s, reads=[o.b],
                            pwrites=[sc["OF"].b])
                        return
                    of = ofl[s]
                    gs = gsb[s]
                    tt("dve", o.t[:], obk.ap().rearrange("p (h q) -> p h q", q=128), of.t[:], ALU.add, [obk.b, of.b], [o.b])
                    of2 = o.t[:].rearrange("p h q -> p (h q)")
                    tt(RPE, sqs.t[:], of2, of2, ALU.mult, [o.b], [sqs.b])
                    nb = BK[4]
                    mm(nb.ap(), onesf.t[:], sqs.t[:], True, True, [onesf.b, sqs.b], nb.b)
                    act(rrs.t[:], nb.ap(), AF.Ln, [nb.b, epsc.b], [rrs.b], scale=1.0 / 128, bias=epsc.t[:])
                    act(rrs.t[:], rrs.t[:], AF.Exp, [], writes=[rrs.b], scale=-0.5)
                    stt("dve", of2, of2, rng_col.t[:, 0:1], rrs.t[:], ALU.mult, ALU.mult, [rng_col.b, rrs.b], writes=[o.b])
                    y = yo[s]
                    tt(RPE, y.t[:], o.t[:], gs.t[:], ALU.mult, [o.b, gs.b], [y.b])
                    dma("pool", sc["YR"].t[:, c0:c0 + 128].rearrange("(h p) q -> p h q", p=128), y.t[:], "yo%d" % s, reads=[y.b],
                        pwrites=[sc["YR"].b])
                posts.append(post)
            flush_posts()
        yield

    def pass_merge(l, stage_src, stage_dst, ctx_out):
        new_pass()
        sc = SC[l]
        wbb, _ = sba.tile([128, 12, D], BF16)
        wob = TL(sba.tile([128, 8, D], BF16)[0].t, wbb.b)
        for i in range(3):
            for k in range(4):
                load_w_cast(wbb, i * 4 + k, w_branch[l, i][k * 128:(k + 1) * 128, :], "wl", D)
        for k in range(8):
            load_w_cast(wob, k, w_out[l][k * 128:(k + 1) * 128, :], "wl", D)
        ysb = [sba.tile([128, 12, 512], BF16)[0] for _ in range(2)]
        gsb = [sba.tile([128, 24, 512], BF16)[0] for _ in range(2)]
        yT, _ = sba.tile([128, 8, 512], BF16)
        mt = [sba.tile([128, 512], F32)[0] for _ in range(3)]
        gate_bc, _ = sba.tile([128, D], F32)
        xres = [sba.tile([128, D], F32)[0] for _ in range(2)]
        ytmp = [sba.tile([128, 512], F32)[0] for _ in range(2)]
        blocks = ([c_block] if ctx_out else []) + x_blocks

        def load_blk(bi):
            blk = blocks[bi]
            N = len(blk[1]) * 128
            c0 = blk[1][0] * 128
            s = bi % 2
            for i, nm in enumerate(("YM", "YD", "YR")):
                dma("sp", ysb[s].t[:, i * 4:(i + 1) * 4, 0:N], sc[nm].t[:, c0:c0 + N].rearrange("(k p) n -> p k n", p=128), "my%d" % s,
                    reads=[sc[nm].b], pwrites=[ysb[s].b])
            for i in range(3):
                dma("sp", gsb[s].t[:, i * 8:(i + 1) * 8, 0:N], sc["GATE"].t[i * 1024:(i + 1) * 1024, c0:c0 + N].rearrange("(k p) n -> p k n", p=128),
                    "mg%d" % s, reads=[sc["GATE"].b], pwrites=[gsb[s].b])

        load_blk(0)
        cur_stream = None
        for bi, blk in enumerate(blocks):
            N = len(blk[1]) * 128
            s = bi % 2
            if bi + 1 < len(blocks):
                load_blk(bi + 1)
            if blk[0] != cur_stream:
                cur_stream = blk[0]
                load_gate_bc(gate_bc, l, 1, 0 if cur_stream == "x" else 1)
            for oc in range(8):
                zb = [BK[(oc % 2) * 3 + i] for i in range(3)]
                for i in range(3):
                    for k in range(4):
                        mm(zb[i].ap(c1=N), wbb.t[:, i * 4 + k, oc * 128:(oc + 1) * 128], ysb[s].t[:, i * 4 + k, 0:N], k == 0, k == 3,
                           [wbb.b, ysb[s].b], zb[i].b)
                for i in range(3):
                    tt("dve", mt[i].t[:, 0:N], zb[i].ap(c1=N), gsb[s].t[:, i * 8 + oc, 0:N], ALU.mult, [zb[i].b, gsb[s].b], [mt[i].b])
                tt("pool", mt[0].t[:, 0:N], mt[0].t[:, 0:N], mt[1].t[:, 0:N], ALU.add, [mt[1].b], writes=[mt[0].b])
                tt("pool", yT.t[:, oc, 0:N], mt[0].t[:, 0:N], mt[2].t[:, 0:N], ALU.add, [mt[0].b, mt[2].b], pwrites=[yT.b])
            for i in range(len(blk[1])):
                bks = [BK[6], BK[7]]
                for j in range(2):
                    for k in range(8):
                        mm(bks[j].ap(), yT.t[:, k, i * 128:(i + 1) * 128], wob.t[:, k, j * 512:(j + 1) * 512], k == 0, k == 7, [yT.b, wbb.b], bks[j].b)
                residual_store(stage_src, stage_dst, blk, i, bks, gate_bc, xres, ytmp)

    pass_mod()
    stage = None
    for l in range(DEPTH):
        last = l == DEPTH - 1
        ctx_out = not last
        pass_ffn(l, 1, stage, (l, 1), True, False)
        pass_proj(l, (l, 1), ctx_out)
        pass_attn(l, ctx_out)
        pass_merge(l, (l, 1), (l, 2), ctx_out)
        pass_ffn(l, 2, (l, 2), None if last else (l, 3), ctx_out, last)
        stage = (l, 3)
    P.barrier()
    stats = P.emit()
    return nc, stats


_CACHE = {}


def _get(S, DEBUG=()):
    key = (S, tuple(DEBUG))
    if key not in _CACHE:
        _CACHE[key] = (build(S, DEBUG), host_consts(S))
    return _CACHE[key]


def kernel(**inputs):
    x = np.asarray(inputs["x"], np.float32)
    B, S, _ = x.shape
    (nc, _), hc = _get(S)
    shared = {k: np.ascontiguousarray(np.asarray(v, np.float32)) for k, v in inputs.items() if k not in ("x", "c", "ctx")}
    in_maps = []
    for b in range(B):
        m = dict(shared)
        m.update(hc)
        m["x"] = np.ascontiguousarray(x[b])
        m["c"] = np.ascontiguousarray(np.asarray(inputs["c"], np.float32)[b])
        m["ctx"] = np.ascontiguousarray(np.asarray(inputs["ctx"], np.float32)[b])
        in_maps.append(m)
    res = run_bass_kernel_spmd(nc, in_maps, core_ids=list(range(B)))
    return np.stack([np.asarray(r["out"], np.float32) for r in res.results], axis=0)
```
